# Optimizing a Trainium2 kernel written in Bass

```python
import math
import jax, jax.numpy as jnp
from jax import lax
import numpy as np

D_MODEL = 2048
BATCH = 16
SEQ = 2048
DEPTH = 4

N_MIXERS = 3
N_CONV_LAYERS = (DEPTH + 2) // 3
N_MLA_LAYERS = (DEPTH + 1) // 3
N_SSM_LAYERS = DEPTH // 3
DEEPNORM_ALPHA = (2 * DEPTH) ** 0.25
DEEPNORM_BETA = (8 * DEPTH) ** -0.25
LN_EPS = 1e-5
RMS_EPS = 1e-6

SC_WIDTH = 3

MLA_HEADS = 16
MLA_Q_LORA = 512
MLA_KV_LORA = 512
MLA_NOPE = 128
MLA_ROPE = 64
MLA_V = 128
MLA_QK = MLA_NOPE + MLA_ROPE
MLA_Q_BLOCK = 128
ROPE_THETA = 10000.0

SSM_INNER = 2 * D_MODEL
SSM_HEAD_DIM = 64
SSM_HEADS = SSM_INNER // SSM_HEAD_DIM
SSM_GROUPS = 8
SSM_HEADS_PER_GROUP = SSM_HEADS // SSM_GROUPS
SSM_STATE = 128
SSM_CONV = 4
SSM_CHUNK = 128
SSM_CONV_DIM = SSM_INNER + 2 * SSM_GROUPS * SSM_STATE
SSM_IN_DIM = SSM_INNER + SSM_CONV_DIM + SSM_HEADS

PEER_HEADS = 8
PEER_KEY_DIM = 256
PEER_HALF = PEER_KEY_DIM // 2
PEER_N_KEYS = 128
PEER_EXPERTS = PEER_N_KEYS * PEER_N_KEYS
PEER_TOPK = 16
PEER_TOKEN_BLOCK = 128

kernel_name = 'hybrid_conv_mla_ssd_peer_deepnorm'


def layer_norm(x, g, b):
    xf = x.astype(jnp.float32)
    mu = jnp.mean(xf, axis=-1, keepdims=True)
    var = jnp.mean(jnp.square(xf - mu), axis=-1, keepdims=True)
    y = (xf - mu) * lax.rsqrt(var + LN_EPS) * g.astype(jnp.float32) + b.astype(jnp.float32)
    return y.astype(x.dtype)


def rms_norm(x, w):
    xf = x.astype(jnp.float32)
    y = xf * lax.rsqrt(jnp.mean(jnp.square(xf), axis=-1, keepdims=True) + RMS_EPS)
    return (y * w.astype(jnp.float32)).astype(x.dtype)


def causal_dwconv(u, w):
    k, c = w.shape
    return lax.conv_general_dilated(
        u, w[:, None, :].astype(u.dtype), window_strides=(1,), padding=[(k - 1, 0)],
        dimension_numbers=('NWC', 'WIO', 'NWC'), feature_group_count=c)


def rotate_half(x):
    x1, x2 = jnp.split(x, 2, axis=-1)
    return jnp.concatenate([-x2, x1], axis=-1)


def short_conv_mixer(x, w_in, conv_w, w_out):
    proj = x @ w_in
    gate_b, gate_c, h = jnp.split(proj, 3, axis=-1)
    y = gate_b * causal_dwconv(gate_c * h, conv_w)
    return y @ w_out


def mla_mixer(x, positions, w_in, q_norm, kv_norm, w_uq, w_ukv, w_o):
    bsz, s, _ = x.shape
    proj = x @ w_in
    c_q = rms_norm(proj[..., :MLA_Q_LORA], q_norm)
    c_kv = rms_norm(proj[..., MLA_Q_LORA:MLA_Q_LORA + MLA_KV_LORA], kv_norm)
    k_rope = proj[..., MLA_Q_LORA + MLA_KV_LORA:]
    q = (c_q @ w_uq).reshape(bsz, s, MLA_HEADS, MLA_QK)
    q_nope, q_rope = q[..., :MLA_NOPE], q[..., MLA_NOPE:]
    kv = (c_kv @ w_ukv).reshape(bsz, s, MLA_HEADS, MLA_NOPE + MLA_V)
    k_nope, v = kv[..., :MLA_NOPE], kv[..., MLA_NOPE:]
    inv_freq = 1.0 / (ROPE_THETA ** (jnp.arange(0, MLA_ROPE, 2, dtype=jnp.float32) / MLA_ROPE))
    ang = positions.astype(jnp.float32)[..., None] * inv_freq
    ang = jnp.concatenate([ang, ang], axis=-1)
    cos, sin = jnp.cos(ang), jnp.sin(ang)
    q_rope = (q_rope.astype(jnp.float32) * cos[:, :, None] + rotate_half(q_rope.astype(jnp.float32)) * sin[:, :, None]).astype(x.dtype)
    k_rope = (k_rope.astype(jnp.float32) * cos + rotate_half(k_rope.astype(jnp.float32)) * sin).astype(x.dtype)
    nb = s // MLA_Q_BLOCK
    qn_blocks = jnp.moveaxis(q_nope.reshape(bsz, nb, MLA_Q_BLOCK, MLA_HEADS, MLA_NOPE), 1, 0)
    qr_blocks = jnp.moveaxis(q_rope.reshape(bsz, nb, MLA_Q_BLOCK, MLA_HEADS, MLA_ROPE), 1, 0)
    q_idx = jnp.arange(s).reshape(nb, MLA_Q_BLOCK)
    k_idx = jnp.arange(s)
    scale = 1.0 / math.sqrt(MLA_QK)

    def attend(args):
        qn, qr, qi = args
        sc = (jnp.einsum('bqhd,bkhd->bhqk', qn, k_nope)
              + jnp.einsum('bqhd,bkd->bhqk', qr, k_rope)).astype(jnp.float32) * scale
        sc = jnp.where((qi[:, None] >= k_idx[None, :])[None, None], sc, -jnp.inf)
        p = jax.nn.softmax(sc, axis=-1).astype(v.dtype)
        return jnp.einsum('bhqk,bkhd->bqhd', p, v)

    o = lax.map(attend, (qn_blocks, qr_blocks, q_idx))
    o = jnp.moveaxis(o, 0, 1).reshape(bsz, s, MLA_HEADS * MLA_V)
    return o @ w_o


def ssd_chunked_scan(xs, dt, a, bm, cm):
    bsz, s, g, e, p = xs.shape
    n = bm.shape[-1]
    nc = s // SSM_CHUNK
    da = dt * a
    xdt = xs * dt[..., None]
    to_chunks = lambda t: jnp.moveaxis(t.reshape((bsz, nc, SSM_CHUNK) + t.shape[2:]), 1, 0)
    causal = jnp.tril(jnp.ones((SSM_CHUNK, SSM_CHUNK), dtype=bool))

    def step(h, inp):
        xdt_c, da_c, b_c, c_c = inp
        acum = jnp.cumsum(da_c, axis=1)
        a_t = jnp.transpose(acum, (0, 2, 3, 1))
        seg = a_t[..., :, None] - a_t[..., None, :]
        decay = jnp.exp(jnp.where(causal, seg, -jnp.inf))
        cb = jnp.einsum('btgn,bsgn->bgts', c_c, b_c)
        y_diag = jnp.einsum('bgets,bsgep->btgep', cb[:, :, None] * decay, xdt_c)
        y_off = jnp.einsum('btgn,bgepn->btgep', c_c, h) * jnp.exp(acum)[..., None]
        decay_last = jnp.exp(a_t[..., -1:] - a_t)
        xdt_dec = xdt_c * jnp.transpose(decay_last, (0, 3, 1, 2))[..., None]
        h_new = h * jnp.exp(a_t[..., -1])[..., None, None] + jnp.einsum('bsgn,bsgep->bgepn', b_c, xdt_dec)
        return h_new, y_diag + y_off

    h0 = jnp.zeros((bsz, g, e, p, n), dtype=jnp.float32)
    _, ys = lax.scan(step, h0, (to_chunks(xdt), to_chunks(da), to_chunks(bm), to_chunks(cm)))
    return jnp.moveaxis(ys, 0, 1).reshape(bsz, s, g, e, p)


def mamba2_mixer(x, w_in, conv_w, conv_b, dt_bias, a_log, d_skip, norm_w, w_out):
    bsz, s, _ = x.shape
    proj = x @ w_in
    z = proj[..., :SSM_INNER]
    xbc = proj[..., SSM_INNER:SSM_INNER + SSM_CONV_DIM]
    dt_raw = proj[..., SSM_INNER + SSM_CONV_DIM:]
    xbc = jax.nn.silu(causal_dwconv(xbc, conv_w) + conv_b.astype(xbc.dtype))
    gbc = SSM_GROUPS * SSM_STATE
    xs = xbc[..., :SSM_INNER].astype(jnp.float32).reshape(bsz, s, SSM_GROUPS, SSM_HEADS_PER_GROUP, SSM_HEAD_DIM)
    bm = xbc[..., SSM_INNER:SSM_INNER + gbc].astype(jnp.float32).reshape(bsz, s, SSM_GROUPS, SSM_STATE)
    cm = xbc[..., SSM_INNER + gbc:].astype(jnp.float32).reshape(bsz, s, SSM_GROUPS, SSM_STATE)
    dt = jax.nn.softplus(dt_raw.astype(jnp.float32) + dt_bias.astype(jnp.float32))
    dt = dt.reshape(bsz, s, SSM_GROUPS, SSM_HEADS_PER_GROUP)
    a = -jnp.exp(a_log.astype(jnp.float32)).reshape(SSM_GROUPS, SSM_HEADS_PER_GROUP)
    y = ssd_chunked_scan(xs, dt, a, bm, cm)
    y = y + d_skip.astype(jnp.float32).reshape(SSM_GROUPS, SSM_HEADS_PER_GROUP)[:, :, None] * xs
    y = y.reshape(bsz, s, SSM_INNER) * jax.nn.silu(z.astype(jnp.float32))
    yg = y.reshape(bsz, s, SSM_GROUPS, SSM_INNER // SSM_GROUPS)
    yg = yg * lax.rsqrt(jnp.mean(jnp.square(yg), axis=-1, keepdims=True) + RMS_EPS)
    y = (yg.reshape(bsz, s, SSM_INNER) * norm_w.astype(jnp.float32)).astype(x.dtype)
    return y @ w_out


def peer_ffn(x, w_q, sub_keys, u_tab, v_tab):
    bsz, s, d = x.shape
    t = bsz * s
    xt = x.reshape(t, d)
    q = (xt @ w_q).reshape(t, PEER_HEADS, PEER_KEY_DIM)
    s1 = jnp.einsum('thd,hkd->thk', q[..., :PEER_HALF], sub_keys[:, 0]).astype(jnp.float32)
    s2 = jnp.einsum('thd,hkd->thk', q[..., PEER_HALF:], sub_keys[:, 1]).astype(jnp.float32)
    v1, i1 = lax.top_k(s1, PEER_TOPK)
    v2, i2 = lax.top_k(s2, PEER_TOPK)
    cand = (v1[..., :, None] + v2[..., None, :]).reshape(t, PEER_HEADS, PEER_TOPK * PEER_TOPK)
    cidx = (i1[..., :, None] * PEER_N_KEYS + i2[..., None, :]).reshape(t, PEER_HEADS, PEER_TOPK * PEER_TOPK)
    top_s, pos = lax.top_k(cand, PEER_TOPK)
    idx = jnp.take_along_axis(cidx, pos, axis=-1)
    gate = jax.nn.softmax(top_s, axis=-1)
    nblk = t // PEER_TOKEN_BLOCK
    kk = PEER_HEADS * PEER_TOPK
    xb = xt.reshape(nblk, PEER_TOKEN_BLOCK, d)
    ib = idx.reshape(nblk, PEER_TOKEN_BLOCK, kk)
    gb = gate.reshape(nblk, PEER_TOKEN_BLOCK, kk)

    def experts(args):
        xc, ic, gc = args
        u = jnp.take(u_tab, ic, axis=0)
        hid = jax.nn.gelu(jnp.einsum('td,tkd->tk', xc, u).astype(jnp.float32), approximate=False) * gc
        vv = jnp.take(v_tab, ic, axis=0)
        return jnp.einsum('tk,tkd->td', hid.astype(vv.dtype), vv)

    out = lax.map(experts, (xb, ib, gb))
    return out.reshape(bsz, s, d)


def setup_inputs(seed: int = 0) -> dict:
    key = jax.random.key(seed)
    ks = jax.random.split(key, 32)
    nrm = lambda k, shape, sc: jax.random.normal(k, shape, dtype=jnp.float32) * sc
    d = D_MODEL
    x = nrm(ks[0], (BATCH, SEQ, d), 1.0)
    positions = jnp.broadcast_to(jnp.arange(SEQ, dtype=jnp.int32)[None, :], (BATCH, SEQ)).astype(jnp.int32)
    sc_w_in = nrm(ks[1], (N_CONV_LAYERS, d, 3 * d), d ** -0.5)
    sc_conv_w = nrm(ks[2], (N_CONV_LAYERS, SC_WIDTH, d), SC_WIDTH ** -0.5)
    sc_w_out = nrm(ks[3], (N_CONV_LAYERS, d, d), d ** -0.5 * DEEPNORM_BETA)
    mla_w_in = nrm(ks[4], (N_MLA_LAYERS, d, MLA_Q_LORA + MLA_KV_LORA + MLA_ROPE), d ** -0.5)
    mla_q_norm = 1.0 + nrm(ks[5], (N_MLA_LAYERS, MLA_Q_LORA), 0.01)
    mla_kv_norm = 1.0 + nrm(ks[6], (N_MLA_LAYERS, MLA_KV_LORA), 0.01)
    mla_w_uq = nrm(ks[7], (N_MLA_LAYERS, MLA_Q_LORA, MLA_HEADS * MLA_QK), MLA_Q_LORA ** -0.5)
    mla_w_ukv = nrm(ks[8], (N_MLA_LAYERS, MLA_KV_LORA, MLA_HEADS * (MLA_NOPE + MLA_V)), MLA_KV_LORA ** -0.5)
    mla_w_o = nrm(ks[9], (N_MLA_LAYERS, MLA_HEADS * MLA_V, d), (MLA_HEADS * MLA_V) ** -0.5 * DEEPNORM_BETA)
    ssm_w_in = nrm(ks[10], (N_SSM_LAYERS, d, SSM_IN_DIM), d ** -0.5)
    ssm_conv_w = nrm(ks[11], (N_SSM_LAYERS, SSM_CONV, SSM_CONV_DIM), SSM_CONV ** -0.5)
    ssm_conv_b = nrm(ks[12], (N_SSM_LAYERS, SSM_CONV_DIM), 0.01)
    u = jax.random.uniform(ks[13], (N_SSM_LAYERS, SSM_HEADS), dtype=jnp.float32)
    dt0 = jnp.exp(u * (math.log(0.1) - math.log(0.001)) + math.log(0.001))
    ssm_dt_bias = dt0 + jnp.log(-jnp.expm1(-dt0))
    ssm_a_log = jnp.log(jax.random.uniform(ks[14], (N_SSM_LAYERS, SSM_HEADS), dtype=jnp.float32, minval=1.0, maxval=16.0))
    ssm_d = 1.0 + nrm(ks[15], (N_SSM_LAYERS, SSM_HEADS), 0.01)
    ssm_norm_w = 1.0 + nrm(ks[16], (N_SSM_LAYERS, SSM_INNER), 0.01)
    ssm_w_out = nrm(ks[17], (N_SSM_LAYERS, SSM_INNER, d), SSM_INNER ** -0.5 * DEEPNORM_BETA)
    peer_w_q = nrm(ks[18], (DEPTH, d, PEER_HEADS * PEER_KEY_DIM), d ** -0.5)
    peer_sub_keys = nrm(ks[19], (DEPTH, PEER_HEADS, 2, PEER_N_KEYS, PEER_HALF), PEER_HALF ** -0.5)
    peer_u = nrm(ks[20], (DEPTH, PEER_EXPERTS, d), d ** -0.5)
    peer_v = nrm(ks[21], (DEPTH, PEER_EXPERTS, d), (PEER_HEADS * PEER_TOPK) ** -0.5 * DEEPNORM_BETA)
    ln_g = 1.0 + nrm(ks[22], (DEPTH, 2, d), 0.01)
    ln_b = nrm(ks[23], (DEPTH, 2, d), 0.01)
    return {'x': x, 'positions': positions,
            'sc_w_in': sc_w_in, 'sc_conv_w': sc_conv_w, 'sc_w_out': sc_w_out,
            'mla_w_in': mla_w_in, 'mla_q_norm': mla_q_norm, 'mla_kv_norm': mla_kv_norm,
            'mla_w_uq': mla_w_uq, 'mla_w_ukv': mla_w_ukv, 'mla_w_o': mla_w_o,
            'ssm_w_in': ssm_w_in, 'ssm_conv_w': ssm_conv_w, 'ssm_conv_b': ssm_conv_b,
            'ssm_dt_bias': ssm_dt_bias, 'ssm_a_log': ssm_a_log, 'ssm_d': ssm_d,
            'ssm_norm_w': ssm_norm_w, 'ssm_w_out': ssm_w_out,
            'peer_w_q': peer_w_q, 'peer_sub_keys': peer_sub_keys, 'peer_u': peer_u, 'peer_v': peer_v,
            'ln_g': ln_g, 'ln_b': ln_b}


def reference(x, positions, sc_w_in, sc_conv_w, sc_w_out, mla_w_in, mla_q_norm, mla_kv_norm,
              mla_w_uq, mla_w_ukv, mla_w_o, ssm_w_in, ssm_conv_w, ssm_conv_b, ssm_dt_bias,
              ssm_a_log, ssm_d, ssm_norm_w, ssm_w_out, peer_w_q, peer_sub_keys, peer_u, peer_v,
              ln_g, ln_b):
    for i in range(DEPTH):
        j = i // N_MIXERS
        kind = i % N_MIXERS
        if kind == 0:
            h = short_conv_mixer(x, sc_w_in[j], sc_conv_w[j], sc_w_out[j])
        elif kind == 1:
            h = mla_mixer(x, positions, mla_w_in[j], mla_q_norm[j], mla_kv_norm[j],
                          mla_w_uq[j], mla_w_ukv[j], mla_w_o[j])
        else:
            h = mamba2_mixer(x, ssm_w_in[j], ssm_conv_w[j], ssm_conv_b[j], ssm_dt_bias[j],
                             ssm_a_log[j], ssm_d[j], ssm_norm_w[j], ssm_w_out[j])
        x = layer_norm(DEEPNORM_ALPHA * x + h, ln_g[i, 0], ln_b[i, 0])
        f = peer_ffn(x, peer_w_q[i], peer_sub_keys[i], peer_u[i], peer_v[i])
        x = layer_norm(DEEPNORM_ALPHA * x + f, ln_g[i, 1], ln_b[i, 1])
    return x
```

```python
import numpy as np
from contextlib import ExitStack
import concourse.bass as bass
import concourse.mybir as mybir
from concourse.bass_utils import run_bass_kernel_spmd

F32 = mybir.dt.float32
BF16 = mybir.dt.bfloat16
I32 = mybir.dt.int32
AF = mybir.ActivationFunctionType
ALU = mybir.AluOpType
AX = mybir.AxisListType

D = 2048
DC = 16
S = 2048
DEPTH = 4
ALPHA = (2 * DEPTH) ** 0.25
LN_EPS = 1e-5
RMS_EPS = 1e-6
TG = 512


class Tile:
    __slots__ = ("h", "w", "r", "name")

    def __init__(self, h, name=""):
        self.h = h
        self.w = {}
        self.r = {}
        self.name = name

    def __getitem__(self, idx):
        return self.h[idx]


class Eng:
    def __init__(self, K, name, e, is_pe=False):
        self.name = name
        self.e = e
        self.is_pe = is_pe
        self.sem = K.newsem("c_" + name)
        self.count = 0
        self.seen = {}


class Stream:
    def __init__(self, K, name):
        self.sem = K.newsem("d_" + name)
        self.count = 0
        self.nobar = name.startswith("cast")


class KB:
    def __init__(self):
        self.nc = bass.Bass("TRN2", target_bir_lowering=False)
        self.es = ExitStack()
        self.ses = self.es
        self.nsem = 0
        self.streams = {}
        nc = self.nc
        self.PE = Eng(self, "pe", nc.tensor, is_pe=True)
        self.ACT = Eng(self, "act", nc.scalar)
        self.DVE = Eng(self, "dve", nc.vector)
        self.POOL = Eng(self, "pool", nc.gpsimd)
        self.SP = Eng(self, "sp", nc.sync)
        self.n_ins = 0

    def newsem(self, name):
        self.nsem += 1
        return self.es.enter_context(self.nc.semaphore(name))

    def sb(self, name, shape, dt):
        self.nsb = getattr(self, "nsb", 0) + 1
        name = "%s_u%d" % (name, self.nsb)
        return Tile(self.ses.enter_context(self.nc.sbuf_tensor(name, list(shape), dt)), name)

    def stream(self, name):
        if name not in self.streams:
            self.streams[name] = Stream(self, name)
        return self.streams[name]

    def engines(self):
        return [self.PE, self.ACT, self.DVE, self.POOL, self.SP]

    def barrier(self):
        for E in self.engines():
            for Fe in self.engines():
                if Fe is E or Fe.count == 0:
                    continue
                if E.seen.get(id(Fe.sem), 0) < Fe.count:
                    E.e.wait_ge(Fe.sem, Fe.count)
                    E.seen[id(Fe.sem)] = Fe.count
            for st in self.streams.values():
                if st.nobar:
                    continue
                if st.count and E.seen.get(id(st.sem), 0) < st.count:
                    E.e.wait_ge(st.sem, st.count)
                    E.seen[id(st.sem)] = st.count

    def stage(self):
        K = self

        class _S:
            def __enter__(s):
                s.es = ExitStack()
                s.es.__enter__()
                K.ses = s.es
                return s

            def __exit__(s, *a):
                K.barrier()
                K.ses = K.es
                return s.es.__exit__(*a)
        return _S()

    def ps(self, name, shape=(128, 512), dt=F32):
        return Tile(self.es.enter_context(self.nc.psum_tensor(name, list(shape), dt)), name)

    def dram(self, name, shape, dt, kind=None):
        if kind is None:
            return self.nc.dram_tensor(name, list(shape), dt)
        return self.nc.dram_tensor(name, list(shape), dt, kind=kind)

    def vt(self, name=""):
        return Tile(None, name)

    def _waits(self, E, reads, writes, skip_sem=None):
        need = {}
        for t in reads:
            for k, v in t.w.items():
                if k not in need or need[k][1] < v[1]:
                    need[k] = v
        for t in writes:
            for dct in (t.w, t.r):
                for k, v in dct.items():
                    if dct is t.w and k == skip_sem:
                        continue
                    if k not in need or need[k][1] < v[1]:
                        need[k] = v
        for k, (sem, val, src) in need.items():
            if src is E and E.is_pe:
                continue
            if E.seen.get(k, 0) >= val:
                continue
            E.e.wait_ge(sem, val)
            E.seen[k] = val
            self.n_ins += 1

    def _record(self, tok, reads, writes):
        k = id(tok[0])
        for t in reads:
            t.r[k] = tok
        for t in writes:
            if t.h is None:
                t.w[k] = tok
            else:
                t.w = {k: tok}
                t.r = {}

    def op(self, E, fn, reads=(), writes=(), mark=True):
        self._waits(E, reads, writes)
        ins = fn()
        self.n_ins += 1
        if mark:
            E.count += 1
            ins.then_inc(E.sem, 1)
            tok = (E.sem, E.count, E)
        else:
            tok = (E.sem, E.count + 1, E)
        self._record(tok, reads, writes)
        return ins

    def dma(self, Q, st, out, in_, reads=(), writes=()):
        self._waits(Q, reads, writes, skip_sem=id(st.sem))
        ins = Q.e.dma_start(out=out, in_=in_)
        self.n_ins += 1
        st.count += 16
        ins.then_inc(st.sem, 16)
        tok = (st.sem, st.count, None)
        self._record(tok, reads, writes)
        return ins

    def retoken(self, st, tiles):
        k = id(st.sem)
        for t in tiles:
            hit = False
            for dct in (t.w, t.r):
                if k in dct:
                    dct[k] = (st.sem, st.count, None)
                    hit = True
            if not hit:
                t.w[k] = (st.sem, st.count, None)

    def wait_all(self, E, tiles):
        self._waits(E, tiles, tiles)

    def mm(self, out_t, out_ap, l_t, l_ap, r_t, r_ap, start, stop, mark=None):
        nc = self.nc
        return self.op(self.PE, lambda: nc.tensor.matmul(out_ap, l_ap, r_ap, start=start, stop=stop),
                       reads=[l_t, r_t], writes=[out_t], mark=stop if mark is None else mark)


def pv_layout():
    off = {}
    n = 0

    def add(name, cols):
        nonlocal n
        off[name] = n
        n += cols

    for i in range(DEPTH):
        for w in range(2):
            add(("ln_g", i, w), DC)
            add(("ln_b", i, w), DC)
    for j in range(2):
        for k in range(3):
            add(("sc_cw", j, k), DC)
    add(("q_norm",), 4)
    add(("kv_norm",), 4)
    for k in range(4):
        add(("ssm_cw", k), 48)
    add(("ssm_cb",), 48)
    add(("ssm_nw",), 32)
    add(("ssm_dsk",), 32)
    return off, n


PV_OFF, NPV = pv_layout()


def chunked(v):
    return np.ascontiguousarray(np.asarray(v, dtype=np.float32).reshape(-1, 128).T)


def build_pv(inp):
    pv = np.zeros((128, NPV), np.float32)

    def put(key, v):
        c = chunked(v)
        pv[:, PV_OFF[key]:PV_OFF[key] + c.shape[1]] = c

    for i in range(DEPTH):
        for w in range(2):
            put(("ln_g", i, w), inp["ln_g"][i, w])
            put(("ln_b", i, w), inp["ln_b"][i, w])
    for j in range(2):
        for k in range(3):
            put(("sc_cw", j, k), inp["sc_conv_w"][j, k])
    put(("q_norm",), inp["mla_q_norm"][0])
    put(("kv_norm",), inp["mla_kv_norm"][0])
    for k in range(4):
        put(("ssm_cw", k), inp["ssm_conv_w"][0, k])
    put(("ssm_cb",), inp["ssm_conv_b"][0])
    put(("ssm_nw",), inp["ssm_norm_w"][0])
    put(("ssm_dsk",), np.repeat(np.asarray(inp["ssm_d"][0], np.float32), 64))
    return pv


def lay_kmajor(w):
    K, N = w.shape
    return np.ascontiguousarray(w.reshape(K // 128, 128, N).transpose(1, 0, 2).reshape(128, -1))


def lay_conv_in(w):
    a = w.reshape(16, 128, 3, 16, 128)
    return np.ascontiguousarray(a.transpose(3, 1, 0, 2, 4).reshape(16, 128, 16 * 3 * 128))


class Ctx:
    pass


def setup_common(K, T):
    nc = K.nc
    C = Ctx()
    C.T = T
    C.pv_d = K.dram("pv", [128, NPV], F32, kind="ExternalInput")
    C.pv = K.sb("pv_sb", [128, NPV], F32)
    C.st_misc = K.stream("misc")
    K.dma(K.SP, C.st_misc, C.pv[:, :], C.pv_d[:, :], writes=[C.pv])
    C.const_tiles = [C.pv]
    C.ones_f = K.sb("ones_f", [128, 128], F32)
    K.op(K.DVE, lambda: nc.vector.memset(C.ones_f[:, :], 1.0), writes=[C.ones_f])
    C.ones_b = K.sb("ones_b", [128, 128], BF16)
    K.op(K.DVE, lambda: nc.vector.memset(C.ones_b[:, :], 1.0), writes=[C.ones_b])
    C.psum = [K.ps("ps%d" % i) for i in range(8)]
    C.ln_sq = [K.sb("ln_sq%d" % i, [128, TG], F32) for i in range(1)] * 2
    C.ln_mean = K.sb("ln_mean", [128, TG], F32)
    C.ln_m2 = K.sb("ln_m2", [128, TG], F32)
    C.ln_rstd = K.sb("ln_rstd", [128, TG], F32)
    C.ln_t = [K.sb("ln_t%d" % i, [128, TG], F32) for i in range(2)]
    return C


def pvcol(C, key, c):
    o = PV_OFF[key] + c
    return C.pv[:, o:o + 1]


def cast_dram(K, st, dst, src, rows, cols, tiles_w, tiles_r=(), maxc=16384):
    maxc = min(maxc, 4096)
    for r0 in range(0, rows, 128):
        r1 = min(rows, r0 + 128)
        for c0 in range(0, cols, maxc):
            c1 = min(cols, c0 + maxc)
            K.dma(K.POOL, st, dst[r0:r1, c0:c1], src[r0:r1, c0:c1], reads=list(tiles_r), writes=list(tiles_w))


def layer_norm_inplace(K, C, xf, li, which, ps_a, ps_b):
    nc = K.nc
    for c in range(DC):
        K.mm(ps_a, ps_a[:, :], C.ones_f, C.ones_f[:, :], xf, xf[:, c, :], c == 0, c == DC - 1)
    for c in range(DC):
        sq = C.ln_sq[c % 2]
        K.op(K.ACT, lambda sq=sq, c=c: nc.scalar.activation(sq[:, :], xf[:, c, :], AF.Square),
             reads=[xf], writes=[sq])
        K.mm(ps_b, ps_b[:, :], C.ones_f, C.ones_f[:, :], sq, sq[:, :], c == 0, c == DC - 1, mark=True)
    K.op(K.DVE, lambda: nc.vector.tensor_scalar(C.ln_mean[:, :], ps_a[:, :], 1.0 / D, None, ALU.mult),
         reads=[ps_a], writes=[C.ln_mean])
    K.op(K.DVE, lambda: nc.vector.tensor_tensor(C.ln_m2[:, :], C.ln_mean[:, :], C.ln_mean[:, :], ALU.mult),
         reads=[C.ln_mean], writes=[C.ln_m2])
    K.op(K.DVE, lambda: nc.vector.scalar_tensor_tensor(C.ln_m2[:, :], ps_b[:, :], 1.0 / D, C.ln_m2[:, :],
                                                       ALU.mult, ALU.subtract),
         reads=[ps_b, C.ln_m2], writes=[C.ln_m2])
    K.op(K.DVE, lambda: nc.vector.tensor_scalar(C.ln_m2[:, :], C.ln_m2[:, :], LN_EPS, None, ALU.add),
         reads=[C.ln_m2], writes=[C.ln_m2])
    K.op(K.ACT, lambda: nc.scalar.activation(C.ln_m2[:, :], C.ln_m2[:, :], AF.Sqrt),
         reads=[C.ln_m2], writes=[C.ln_m2])
    K.op(K.DVE, lambda: nc.vector.reciprocal(C.ln_rstd[:, :], C.ln_m2[:, :]),
         reads=[C.ln_m2], writes=[C.ln_rstd])
    for c in range(DC):
        t = C.ln_t[c % 2]
        K.op(K.DVE, lambda t=t, c=c: nc.vector.tensor_tensor(t[:, :], xf[:, c, :], C.ln_mean[:, :], ALU.subtract),
             reads=[xf, C.ln_mean], writes=[t])
        K.op(K.DVE, lambda t=t: nc.vector.tensor_tensor(t[:, :], t[:, :], C.ln_rstd[:, :], ALU.mult),
             reads=[t, C.ln_rstd], writes=[t])
        K.op(K.ACT, lambda t=t, c=c: nc.scalar.activation(
            xf[:, c, :], t[:, :], AF.Identity,
            bias=pvcol(C, ("ln_b", li, which), c), scale=pvcol(C, ("ln_g", li, which), c)),
            reads=[t, C.pv], writes=[xf])


def conv_stage(K, C, li, j, w_in_bf, w_out_bf, xin, xin_t, xout, xout_t, prep_t):
    nc = K.nc
    T = C.T
    NG = T // TG
    with K.stage():
        wo = K.sb("cv_wo", [128, DC, D], BF16)
        xb = K.sb("cv_xb", [128, DC, TG], BF16)
        xf = K.sb("cv_xf", [128, DC, TG], F32)
        yb = K.sb("cv_yb", [128, DC, TG], BF16)
        w3 = [K.sb("cv_w3_%d" % i, [128, DC * 3 * 128], BF16) for i in range(3)]
        halo = K.sb("cv_halo", [128, DC, 2], F32)
        ub = [K.sb("cv_ub%d" % i, [128, TG + 2], F32) for i in range(2)]
        cS = [K.sb("cv_cS%d" % i, [128, TG], F32) for i in range(2)]
        vv = [K.sb("cv_v%d" % i, [128, TG], F32) for i in range(2)]
        st_wo = K.stream("wres")
        st_w3 = [K.stream("wring%d" % i) for i in range(3)]
        st_xb = K.stream("xb")
        st_xf = K.stream("xf")
        wov = w_out_bf[:, :].rearrange("p (k n) -> p k n", k=DC)
        for q in range(4):
            K.dma(K.SP, st_wo, wo[:, 4 * q:4 * q + 4, :], wov[:, 4 * q:4 * q + 4, :], reads=[prep_t], writes=[wo])
        xin_v = xin[:, :].rearrange("(c p) t -> p c t", p=128)
        xout_v = xout[:, :].rearrange("(c p) t -> p c t", p=128)
        P = C.psum
        wl = 0
        for g in range(NG):
            tsl = slice(g * TG, (g + 1) * TG)
            for q in range(4):
                K.dma(K.POOL, st_xb, xb[:, 4 * q:4 * q + 4, :], xin_v[:, 4 * q:4 * q + 4, tsl], reads=[xin_t[g]], writes=[xb])
                K.dma(K.SP, st_xf, xf[:, 4 * q:4 * q + 4, :], xin_v[:, 4 * q:4 * q + 4, tsl], reads=[xin_t[g]], writes=[xf])
            if (g * TG) % S == 0:
                K.op(K.DVE, lambda: nc.vector.memset(halo[:, :, :], 0.0), writes=[halo])
            for jc in range(DC):
                w = w3[wl % 3]
                K.dma(K.SP, st_w3[wl % 3], w[:, :], w_in_bf[jc, :, :], reads=[prep_t], writes=[w])
                wl += 1
                pB, pC, ph = P[(jc % 2) * 3 + 0], P[(jc % 2) * 3 + 1], P[(jc % 2) * 3 + 2]
                for s, pp in enumerate((pB, pC, ph)):
                    for kc in range(DC):
                        o = (kc * 3 + s) * 128
                        K.mm(pp, pp[:, :], w, w[:, o:o + 128], xb, xb[:, kc, :], kc == 0, kc == DC - 1)
                u = ub[jc % 2]
                c_s = cS[jc % 2]
                v = vv[jc % 2]
                K.op(K.ACT, lambda c_s=c_s, pC=pC: nc.scalar.copy(c_s[:, :], pC[:, :]), reads=[pC], writes=[c_s])
                K.op(K.DVE, lambda u=u, jc=jc: nc.vector.tensor_copy(u[:, 0:2], halo[:, jc, :]), reads=[halo], writes=[u])
                K.op(K.DVE, lambda u=u, c_s=c_s, ph=ph: nc.vector.tensor_tensor(u[:, 2:TG + 2], c_s[:, :], ph[:, :], ALU.mult),
                     reads=[c_s, ph], writes=[u])
                K.op(K.DVE, lambda u=u, jc=jc: nc.vector.tensor_copy(halo[:, jc, :], u[:, TG:TG + 2]), reads=[u], writes=[halo])
                K.op(K.DVE, lambda u=u, v=v, jc=jc: nc.vector.tensor_scalar(
                    v[:, :], u[:, 0:TG], pvcol(C, ("sc_cw", j, 0), jc), None, ALU.mult), reads=[u, C.pv], writes=[v])
                for k in (1, 2):
                    K.op(K.DVE, lambda u=u, v=v, jc=jc, k=k: nc.vector.scalar_tensor_tensor(
                        v[:, :], u[:, k:TG + k], pvcol(C, ("sc_cw", j, k), jc), v[:, :], ALU.mult, ALU.add),
                        reads=[u, v, C.pv], writes=[v])
                K.op(K.DVE, lambda v=v, pB=pB, jc=jc: nc.vector.tensor_tensor(yb[:, jc, :], pB[:, :], v[:, :], ALU.mult),
                     reads=[pB, v], writes=[yb])
            for n in range(DC):
                po = P[6 + n % 2]
                for kc in range(DC):
                    K.mm(po, po[:, :], wo, wo[:, kc, n * 128:(n + 1) * 128], yb, yb[:, kc, :], kc == 0, kc == DC - 1)
                K.op(K.DVE, lambda po=po, n=n: nc.vector.scalar_tensor_tensor(
                    xf[:, n, :], xf[:, n, :], ALPHA, po[:, :], ALU.mult, ALU.add), reads=[xf, po], writes=[xf])
            layer_norm_inplace(K, C, xf, li, 0, P[6], P[7])
            for q in range(4):
                K.dma(K.ACT, st_xf, xout_v[:, 4 * q:4 * q + 4, tsl], xf[:, 4 * q:4 * q + 4, :], reads=[xf], writes=[xout_t[g]])


NH = 8
SBK = 4
NEG = -1.0e30
THETA_MARGIN = 2.0e-4
ACC_LAG = 1


def host_consts():
    ident = np.eye(128, dtype=np.float32)
    selc = np.zeros((16, NH, 128), np.float32)
    selz = np.zeros((8, NH, 128), np.float32)
    for h in range(NH):
        selc[h, h, :] = -1.0
        selc[8 + h, h, :] = -1.0
        selz[h, h, :] = 1.0
    return {"c_ident": ident}


def lay_ut(u):
    a = u.reshape(128, 128, 16, 128)
    return np.ascontiguousarray(a.transpose(0, 3, 2, 1).reshape(128 * 128, 2048))


def lay_wq(w):
    a = w.reshape(16, 128, 16, 128)
    return np.ascontiguousarray(a.transpose(2, 1, 0, 3).reshape(16 * 128, 2048))


def lay_sk(sk):
    return np.ascontiguousarray(sk.transpose(3, 0, 1, 2).reshape(128, NH * 2 * 128))


def setup_peer_consts(K, C):
    nc = K.nc
    C.c_ident_d = K.dram("c_ident", [128, 128], F32, kind="ExternalInput")
    C.ident_f = K.sb("ident_f", [128, 128], F32)
    C.ident_b = K.sb("ident_b", [128, 128], BF16)
    stp = K.stream("misc_p")
    K.dma(K.SP, stp, C.ident_f[:, :], C.c_ident_d[:, :], writes=[C.ident_f])
    K.op(K.DVE, lambda: nc.vector.tensor_copy(C.ident_b[:, :], C.ident_f[:, :]), reads=[C.ident_f], writes=[C.ident_b])


def peer_stage(K, C, li, wq_b, sk_b, ut_b, v_b, xin, xin_t, xout, xout_t, prep_t):
    nc = K.nc
    T = C.T
    NG = T // TG
    P = C.psum
    NTT = TG // 128
    NBLK = 128 // SBK
    with K.stage():
        xb = K.sb("pr_xb", [128, DC, TG], BF16)
        xacc = K.sb("pr_xacc", [128, DC, TG], F32)
        skT = K.sb("pr_skT", [128, NH * 2 * 128], BF16)
        wq = [K.sb("pr_wq%d" % i, [128, 2048], BF16) for i in range(2)]
        qr = [K.sb("pr_qr%d" % i, [128, TG], BF16) for i in range(4)]
        sc = K.sb("pr_sc", [128, NTT, NH, 256], F32)
        m16 = K.sb("pr_m16", [128, NH * 2 * 16], F32)
        c16 = K.sb("pr_c16", [128, NH * 16], F32)
        m16s = [K.vt() for _ in range(NH)]
        c16s = [K.vt() for _ in range(NH)]
        d16 = K.sb("pr_d16", [128, NH * 16], F32)
        zs = K.sb("pr_zs", [128, NH], F32)
        thp = K.sb("pr_thp", [128, NTT, NH], F32)
        nlz = K.sb("pr_nlz", [128, NTT, NH], F32)
        Ut = [K.sb("pr_U%d" % i, [128, 2048], BF16) for i in range(SBK)]
        Vt = [K.sb("pr_V%d" % i, [128, 2048], BF16) for i in range(2 * SBK)]
        WT = [K.sb("pr_W%d" % i, [128, TG], BF16) for i in range(2 * SBK)]
        u_t = [K.sb("pr_u%d" % i, [128, TG], F32) for i in range(3)]
        e_t = [K.sb("pr_e%d" % i, [128, TG], BF16) for i in range(4)]
        g_t = [K.sb("pr_g%d" % i, [128, TG], BF16) for i in range(6)]
        at_t = [K.sb("pr_at%d" % i, [128, TG], BF16) for i in range(2)]
        Gsb = [K.sb("pr_G%d" % i, [128, SBK, TG], BF16) for i in range(2)]
        gel = [K.sb("pr_gel%d" % i, [128, TG], BF16) for i in range(SBK)]
        gel2 = [K.sb("pr_gl2%d" % i, [128, TG], BF16) for i in range(2)]
        st_x = K.stream("xf")
        st_sk = K.stream("wres")
        st_wq = [K.stream("wring%d" % i) for i in range(2)]
        st_u = [K.stream("uring%d" % i) for i in range(SBK)]
        st_v = [K.stream("vring%d" % i) for i in range(2 * SBK)]
        K.dma(K.SP, st_sk, skT[:, :], sk_b[:, :], reads=[prep_t], writes=[skT])
        xin_v = xin[:, :].rearrange("(c p) t -> p c t", p=128)
        xout_v = xout[:, :].rearrange("(c p) t -> p c t", p=128)
        skv = lambda h, s: skT[:, (h * 2 + s) * 128:(h * 2 + s + 1) * 128]
        accT = [P[0], P[1]]
        pGs = [P[2], P[3]]
        pHs = [P[4], P[5]]
        pVs = [P[6], P[7]]
        wl = 0
        ql = 0
        ui = 0
        vl = 0

        for g in range(NG):
            tsl = slice(g * TG, (g + 1) * TG)
            for q in range(4):
                K.dma(K.SP, st_x, xacc[:, 4 * q:4 * q + 4, :], xin_v[:, 4 * q:4 * q + 4, tsl], reads=[xin_t[g]], writes=[xacc])
            K.op(K.ACT, lambda: nc.scalar.copy(xb[:, :, :], xacc[:, :, :]), reads=[xacc], writes=[xb])
            K.op(K.DVE, lambda: nc.vector.tensor_scalar(xacc[:, :, :], xacc[:, :, :], ALPHA, None, ALU.mult),
                 reads=[xacc], writes=[xacc])
            for h in range(NH):
                qs = []
                for s_ in range(2):
                    n = 2 * h + s_
                    w = wq[wl % 2]
                    K.dma(K.SP, st_wq[wl % 2], w[:, :], wq_b[n * 128:(n + 1) * 128, :], reads=[prep_t], writes=[w])
                    wl += 1
                    pq = P[4 + n % 2]
                    for kc in range(DC):
                        K.mm(pq, pq[:, :], w, w[:, kc * 128:(kc + 1) * 128], xb, xb[:, kc, :], kc == 0, kc == DC - 1)
                    qq = qr[ql % 4]
                    ql += 1
                    K.op(K.ACT, lambda pq=pq, qq=qq: nc.scalar.copy(qq[:, :], pq[:, :]), reads=[pq], writes=[qq])
                    qs.append(qq)
                for half in range(2):
                    ps = P[6 + half]
                    for t2 in range(2):
                        tt = half * 2 + t2
                        for s_ in range(2):
                            o = (t2 * 2 + s_) * 128
                            K.mm(ps, ps[:, o:o + 128], qs[s_], qs[s_][:, tt * 128:(tt + 1) * 128], skT, skv(h, s_), True, True)
                    K.op(K.ACT, lambda ps=ps, half=half, h=h: nc.scalar.copy(
                        sc[:, 2 * half:2 * half + 2, h, :], ps[:, :].rearrange("p (t k) -> p t k", t=2)), reads=[ps], writes=[sc])
            tmps = [WT[4 + i][:, :].bitcast(F32)[:, 0:128] for i in range(4)]
            tmp_t = [WT[4 + i] for i in range(4)]
            cands = [u_t[i][:, 0:256] for i in range(3)] + [at_t[0][:, :].bitcast(F32)]
            cand_t = [u_t[0], u_t[1], u_t[2], at_t[0]]
            ctmps = [WT[i][:, :].bitcast(F32) for i in range(4)]
            ctmp_t = [WT[i] for i in range(4)]
            for tt in range(NTT):
                for hq4 in range(NH // 4):
                    hs = [hq4 * 4 + i for i in range(4)]
                    for s_ in range(2):
                        for i, h in enumerate(hs):
                            src = sc[:, tt, h, s_ * 128:(s_ + 1) * 128]
                            mo = (h * 2 + s_) * 16
                            K.op(K.DVE, lambda src=src, mo=mo: nc.vector.max(out=m16[:, mo:mo + 8], in_=src), reads=[sc], writes=[m16s[h]])
                        for i, h in enumerate(hs):
                            src = sc[:, tt, h, s_ * 128:(s_ + 1) * 128]
                            mo = (h * 2 + s_) * 16
                            K.op(K.DVE, lambda src=src, mo=mo, i=i: nc.vector.match_replace(
                                out=tmps[i], in_to_replace=m16[:, mo:mo + 8], in_values=src, imm_value=NEG),
                                reads=[sc, m16s[h]], writes=[tmp_t[i]])
                        for i, h in enumerate(hs):
                            mo = (h * 2 + s_) * 16
                            K.op(K.DVE, lambda mo=mo, i=i: nc.vector.max(out=m16[:, mo + 8:mo + 16], in_=tmps[i]), reads=[tmp_t[i]], writes=[m16s[h]])
                    for i, h in enumerate(hs):
                        v1 = m16[:, (h * 2) * 16:(h * 2) * 16 + 16]
                        v2 = m16[:, (h * 2 + 1) * 16:(h * 2 + 1) * 16 + 16]
                        K.op(K.DVE, lambda v1=v1, v2=v2, i=i: nc.vector.tensor_tensor(
                            cands[i].rearrange("p (i j) -> p i j", i=16),
                            v1.unsqueeze(2).to_broadcast([128, 16, 16]),
                            v2.unsqueeze(1).to_broadcast([128, 16, 16]), ALU.add), reads=[m16s[h]], writes=[cand_t[i]])
                    for i, h in enumerate(hs):
                        co = h * 16
                        K.op(K.DVE, lambda co=co, i=i: nc.vector.max(out=c16[:, co:co + 8], in_=cands[i]), reads=[cand_t[i]], writes=[c16s[h]])
                    for i, h in enumerate(hs):
                        co = h * 16
                        K.op(K.DVE, lambda co=co, i=i: nc.vector.match_replace(
                            out=ctmps[i], in_to_replace=c16[:, co:co + 8], in_values=cands[i], imm_value=NEG),
                            reads=[cand_t[i], c16s[h]], writes=[ctmp_t[i]])
                    for i, h in enumerate(hs):
                        co = h * 16
                        K.op(K.DVE, lambda co=co, i=i: nc.vector.max(out=c16[:, co + 8:co + 16], in_=ctmps[i]), reads=[ctmp_t[i]], writes=[c16s[h]])
                c16_all = c16s
                c16v = c16[:, :].rearrange("p (h k) -> p h k", h=NH)
                d16v = d16[:, :].rearrange("p (h k) -> p h k", h=NH)
                K.op(K.DVE, lambda tt=tt: nc.vector.tensor_scalar(thp[:, tt, :], c16v[:, :, 15], -THETA_MARGIN, None, ALU.add),
                     reads=c16s, writes=[thp])
                K.op(K.DVE, lambda tt=tt: nc.vector.tensor_tensor(d16v, c16v, thp[:, tt, :].unsqueeze(2).to_broadcast([128, NH, 16]), ALU.subtract),
                     reads=c16s + [thp], writes=[d16])
                K.op(K.ACT, lambda: nc.scalar.activation(d16[:, :], d16[:, :], AF.Exp), reads=[d16], writes=[d16])
                K.op(K.DVE, lambda: nc.vector.reduce_sum(zs[:, :], d16v, axis=AX.X), reads=[d16], writes=[zs])
                K.op(K.ACT, lambda: nc.scalar.activation(zs[:, :], zs[:, :], AF.Ln), reads=[zs], writes=[zs])
                K.op(K.DVE, lambda tt=tt: nc.vector.tensor_tensor(zs[:, :], zs[:, :], thp[:, tt, :], ALU.add), reads=[zs, thp], writes=[zs])
                K.op(K.DVE, lambda tt=tt: nc.vector.tensor_scalar(nlz[:, tt, :], zs[:, :], -1.0, None, ALU.mult), reads=[zs], writes=[nlz])

            dq = []
            clock = [0]

            import os as _os
            _DM = _os.environ.get("DEFER_MODE", "")

            def defer(n, fn, kind="hv"):
                if _DM == "none" or (_DM == "tr" and kind != "tr") or (_DM == "hv" and kind == "tr") or (_DM == "h" and kind != "h") or (_DM == "v" and kind != "v"):
                    fn()
                    return
                dq.append([clock[0] + n, fn])

            def run_due():
                k = 0
                while k < len(dq):
                    if dq[k][0] <= clock[0]:
                        dq.pop(k)[1]()
                    else:
                        k += 1

            def tick():
                clock[0] += 1

            def drain():
                while dq:
                    dq.sort(key=lambda x: x[0])
                    clock[0] = max(clock[0], dq[0][0])
                    dq.pop(0)[1]()

            def h_items(blk, Ublk):
                items = []
                for j in range(SBK):
                    U = Ublk[j]
                    pH = pHs[j % 2]
                    ge = gel[j]
                    for kc in range(DC):
                        def f(U=U, pH=pH, kc=kc, ge=ge):
                            K.mm(pH, pH[:, :], U, U[:, kc * 128:(kc + 1) * 128], xb, xb[:, kc, :], kc == 0, kc == DC - 1)
                            if kc == DC - 1:
                                defer(1, lambda: K.op(K.ACT, lambda: nc.scalar.copy(ge[:, :], pH[:, :]), reads=[pH], writes=[ge]), "h")
                        items.append(f)
                return items

            def v_items(blk, vbase):
                items = []
                for dc in range(DC):
                    pV = pVs[dc % 2]
                    for ei in range(SBK):
                        Ve = Vt[(vbase + ei) % (2 * SBK)]
                        We = WT[(blk * SBK + ei) % (2 * SBK)]

                        def f(pV=pV, Ve=Ve, We=We, dc=dc, ei=ei):
                            K.mm(pV, pV[:, :], Ve, Ve[:, dc * 128:(dc + 1) * 128], We, We[:, :], ei == 0, ei == SBK - 1)
                            if ei == SBK - 1:
                                defer(1, lambda: K.op(K.DVE, lambda: nc.vector.tensor_tensor(xacc[:, dc, :], xacc[:, dc, :], pV[:, :], ALU.add),
                                                      reads=[xacc, pV], writes=[xacc]), "v")
                        items.append(f)
                return items

            prev_v = []
            for blk in range(NBLK):
                a0 = blk * SBK
                Gs = Gsb[blk % 2]
                vbase = vl
                Ublk = []
                for j in range(SBK):
                    U = Ut[j]
                    K.dma(K.SP, st_u[j], U[:, :], ut_b[(a0 + j) * 128:(a0 + j + 1) * 128, :], reads=[prep_t], writes=[U])
                    Ublk.append(U)
                    V = Vt[vl % (2 * SBK)]
                    K.dma(K.SP, st_v[vl % (2 * SBK)], V[:, :], v_b[(a0 + j) * 128:(a0 + j + 1) * 128, :], reads=[prep_t], writes=[V])
                    vl += 1
                hq = h_items(blk, Ublk)
                mainq = []
                while prev_v or hq:
                    for _ in range(2):
                        if prev_v:
                            mainq.append(prev_v.pop(0))
                    for _ in range(2):
                        if hq:
                            mainq.append(hq.pop(0))
                per_unit = -(-len(mainq) // (NTT * NH))
                pend_acc = []

                def tr_steps(tt, Gs=Gs):
                    pG = pGs[tt % 2]
                    at = at_t[tt % 2]
                    aT = accT[tt % 2]

                    def s1():
                        K.op(K.ACT, lambda: nc.scalar.copy(at[:, :], aT[:, :]), reads=[aT], writes=[at])
                        defer(2, s2, 'tr')

                    def s2():
                        for j in range(SBK):
                            K.mm(pG, pG[:, j * 128:(j + 1) * 128], at, at[:, j * 128:(j + 1) * 128], C.ident_b, C.ident_b[:, :],
                                 j == 0, j == SBK - 1, mark=(j == SBK - 1))
                        defer(2, s3, 'tr')

                    def s3():
                        K.op(K.ACT, lambda: nc.scalar.copy(
                            Gs[:, :, tt * 128:(tt + 1) * 128], pG[:, :].rearrange("p (j t) -> p j t", j=SBK)), reads=[pG], writes=[Gs])
                    return s1

                for tt in range(NTT):
                    for h in range(NH):
                        uu = u_t[ui % 3]
                        e = e_t[ui % 4]
                        gg = g_t[ui % 6]
                        ui += 1
                        aT = accT[tt % 2]
                        K.op(K.POOL, lambda uu=uu, tt=tt, h=h: nc.gpsimd.tensor_tensor(
                            uu[:, :].rearrange("p (j b) -> p j b", j=SBK),
                            sc[:, tt, h, 128:256].unsqueeze(1).to_broadcast([128, SBK, 128]),
                            sc[:, tt, h, a0:a0 + SBK].unsqueeze(2).to_broadcast([128, SBK, 128]), ALU.add), reads=[sc], writes=[uu])
                        K.op(K.ACT, lambda e=e, uu=uu, tt=tt, h=h: nc.scalar.activation(e[:, :], uu[:, :], AF.Exp, bias=nlz[:, tt, h:h + 1]),
                             reads=[uu, nlz], writes=[e])
                        K.op(K.DVE, lambda gg=gg, e=e, uu=uu, tt=tt, h=h: nc.vector.scalar_tensor_tensor(
                            gg[:, :], uu[:, :], thp[:, tt, h:h + 1], e[:, :], ALU.is_ge, ALU.mult), reads=[uu, e, thp], writes=[gg])
                        run_due()

                        def acc_fn(aT=aT, gg=gg, h=h, tt=tt):
                            K.mm(aT, aT[:, :], C.ident_b, C.ident_b[:, :], gg, gg[:, :], h == 0, h == NH - 1, mark=True)
                            if h == NH - 1:
                                defer(2, tr_steps(tt), 'tr')
                        if ACC_LAG:
                            if pend_acc:
                                pend_acc.pop(0)()
                            pend_acc.append(acc_fn)
                        else:
                            acc_fn()
                        for _ in range(per_unit):
                            if mainq:
                                mainq.pop(0)()
                        tick()
                while pend_acc:
                    pend_acc.pop(0)()
                k_ = 0
                while mainq:
                    mainq.pop(0)()
                    k_ += 1
                    if k_ % SBK == 0:
                        tick()
                        run_due()
                drain()
                for j in range(SBK):
                    W = WT[(a0 + j) % (2 * SBK)]
                    g2 = gel2[j % 2]
                    K.op(K.ACT, lambda g2=g2, j=j: nc.scalar.activation(g2[:, :], gel[j][:, :], AF.Gelu), reads=[gel[j]], writes=[g2])
                    K.op(K.DVE, lambda W=W, g2=g2, j=j: nc.vector.tensor_tensor(W[:, :], g2[:, :], Gs[:, j, :], ALU.mult),
                         reads=[g2, Gs], writes=[W])
                prev_v = v_items(blk, vbase)
            for idx, f in enumerate(prev_v):
                f()
                if idx % SBK == SBK - 1:
                    tick()
                    run_due()
            drain()
            layer_norm_inplace(K, C, xacc, li, 1, P[0], P[1])
            for q in range(4):
                K.dma(K.ACT, st_x, xout_v[:, 4 * q:4 * q + 4, tsl], xacc[:, 4 * q:4 * q + 4, :], reads=[xacc], writes=[xout_t[g]])


MH = 16
SCALE = 1.0 / (192.0 ** 0.5)
TWO_PI = 2.0 * np.pi


def mla_host_consts():
    pm = np.zeros((64, 64), np.float32)
    for j in range(32):
        pm[j + 32, j] = -1.0
        pm[j, j + 32] = 1.0
    inv = (1.0 / (10000.0 ** (np.arange(0, 64, 2, dtype=np.float32) / 64.0))).astype(np.float32)
    invf = np.concatenate([inv, inv]).reshape(64, 1).astype(np.float32)
    cm = np.zeros((128, 4, 512), np.float32)
    k = np.arange(128)[:, None]
    q = np.arange(512)[None, :]
    for j in range(4):
        cm[:, j, :] = (q >= 128 * j + k).astype(np.float32)
    return {"c_pm": pm, "c_invf": invf, "c_cmask": cm.reshape(128, 2048)}


def setup_mla_consts(K, C):
    nc = K.nc
    C.c_pm_d = K.dram("c_pm", [64, 64], F32, kind="ExternalInput")
    C.c_invf_d = K.dram("c_invf", [64, 1], F32, kind="ExternalInput")
    C.c_cmask_d = K.dram("c_cmask", [128, 2048], F32, kind="ExternalInput")
    C.pm = K.sb("pm_f", [64, 64], F32)
    C.invf = K.sb("invf", [64, 1], F32)
    stp = K.stream("misc_m")
    K.dma(K.SP, stp, C.pm[:, :], C.c_pm_d[:, :], writes=[C.pm])
    K.dma(K.SP, stp, C.invf[:, :], C.c_invf_d[:, :], writes=[C.invf])
    K.retoken(stp, [C.pm, C.invf])


def rope_tables(K, nc, C, pos_i, cs, scr):
    posf, t, ti, tf = scr
    K.op(K.DVE, lambda: nc.vector.tensor_copy(posf[:, :], pos_i[:, :]), reads=[pos_i], writes=[posf])
    for which, shift in ((1, 0.0), (0, 0.25)):
        out = cs[which]
        K.op(K.DVE, lambda shift=shift: nc.vector.tensor_scalar(t[:, :], posf[:, :], C.invf[:, 0:1], 1.0 / TWO_PI, ALU.mult, ALU.mult),
             reads=[posf, C.invf], writes=[t])
        if shift:
            K.op(K.DVE, lambda shift=shift: nc.vector.tensor_scalar(t[:, :], t[:, :], shift, None, ALU.add), reads=[t], writes=[t])
        K.op(K.DVE, lambda: nc.vector.tensor_copy(ti[:, :], t[:, :]), reads=[t], writes=[ti])
        K.op(K.DVE, lambda: nc.vector.tensor_copy(tf[:, :], ti[:, :]), reads=[ti], writes=[tf])
        K.op(K.DVE, lambda: nc.vector.tensor_tensor(t[:, :], t[:, :], tf[:, :], ALU.subtract), reads=[t, tf], writes=[t])
        K.op(K.DVE, lambda: nc.vector.scalar_tensor_tensor(tf[:, :], t[:, :], 0.5, t[:, :], ALU.is_gt, ALU.subtract),
             reads=[t], writes=[tf])
        K.op(K.DVE, lambda: nc.vector.scalar_tensor_tensor(t[:, :], t[:, :], -0.5, tf[:, :], ALU.is_lt, ALU.subtract),
             reads=[t, tf], writes=[t])
        K.op(K.ACT, lambda out=out: nc.scalar.activation(out[:, :], t[:, :], AF.Sin, scale=TWO_PI), reads=[t], writes=[out])


def load_x_group(K, nc, st, xacc, xb, xin_v, tsl, xin_tile):
    for q in range(4):
        K.dma(K.SP, st, xacc[:, 4 * q:4 * q + 4, :], xin_v[:, 4 * q:4 * q + 4, tsl], reads=[xin_tile], writes=[xacc])
    if xb is not None:
        K.op(K.ACT, lambda: nc.scalar.copy(xb[:, :, :], xacc[:, :, :]), reads=[xacc], writes=[xb])


def mla_stage_a(K, C, win_b, pos_d, xin, xin_t, cq_d, ckv_d, kr_d, cs_d, mid_t, prep_t):
    nc = K.nc
    T = C.T
    NG = T // TG
    P = C.psum
    with K.stage():
        win = K.sb("ma_win", [128, DC, 1088], BF16)
        xb = K.sb("ma_xb", [128, DC, TG], BF16)
        xacc = K.sb("ma_xacc", [128, DC, TG], F32)
        pj = K.sb("ma_pj", [128, 9, TG], F32)
        sq = [K.sb("ma_sq%d" % i, [128, TG], F32) for i in range(2)]
        rstd = K.sb("ma_rstd", [128, TG], F32)
        tmpn = [K.sb("ma_tn%d" % i, [128, TG], F32) for i in range(2)]
        stg = K.sb("ma_stg", [128, 4, TG], BF16)
        pos_i = K.sb("ma_pos", [64, TG], I32)
        scr = (K.sb("ma_posf", [64, TG], F32), K.sb("ma_t", [64, TG], F32), K.sb("ma_ti", [64, TG], I32), K.sb("ma_tf", [64, TG], F32))
        cs = [K.sb("ma_cos", [64, TG], F32), K.sb("ma_sin", [64, TG], F32)]
        ra = K.sb("ma_ra", [64, TG], F32)
        rb = K.sb("ma_rb", [64, TG], F32)
        krb = K.sb("ma_krb", [64, TG], BF16)
        st_w = K.stream("wres")
        st_x = K.stream("xf")
        st_p = K.stream("pos")
        st_o = K.stream("stg")
        st_o2 = K.stream("stg2")
        winv = win_b[:, :].rearrange("p (k n) -> p k n", k=DC)
        for q in range(4):
            K.dma(K.SP, st_w, win[:, 4 * q:4 * q + 4, :], winv[:, 4 * q:4 * q + 4, :], reads=[prep_t], writes=[win])
        xin_v = xin[:, :].rearrange("(c p) t -> p c t", p=128)
        for g in range(NG):
            tsl = slice(g * TG, (g + 1) * TG)
            load_x_group(K, nc, st_x, xacc, xb, xin_v, tsl, xin_t[g])
            K.dma(K.SP, st_p, pos_i[:, :], pos_d[0:1, tsl].to_broadcast([64, TG]), writes=[pos_i])
            for c in range(9):
                M = 128 if c < 8 else 64
                pp = P[c % 2]
                for kc in range(DC):
                    K.mm(pp, pp[0:M, :], win, win[:, kc, c * 128:c * 128 + M], xb, xb[:, kc, :], kc == 0, kc == DC - 1)
                K.op(K.ACT, lambda pp=pp, c=c, M=M: nc.scalar.copy(pj[0:M, c, :], pp[0:M, :]), reads=[pp], writes=[pj])
            for which, (key, dst) in enumerate(((("q_norm",), cq_d), (("kv_norm",), ckv_d))):
                ps = P[2 + which]
                for c in range(4):
                    s_ = sq[c % 2]
                    K.op(K.ACT, lambda s_=s_, c=c, which=which: nc.scalar.activation(s_[:, :], pj[:, which * 4 + c, :], AF.Square),
                         reads=[pj], writes=[s_])
                    K.mm(ps, ps[:, :], C.ones_f, C.ones_f[:, :], s_, s_[:, :], c == 0, c == 3, mark=True)
                K.op(K.DVE, lambda ps=ps: nc.vector.tensor_scalar(rstd[:, :], ps[:, :], 1.0 / 512.0, RMS_EPS, ALU.mult, ALU.add),
                     reads=[ps], writes=[rstd])
                K.op(K.ACT, lambda: nc.scalar.activation(rstd[:, :], rstd[:, :], AF.Sqrt), reads=[rstd], writes=[rstd])
                K.op(K.DVE, lambda: nc.vector.reciprocal(rstd[:, :], rstd[:, :]), reads=[rstd], writes=[rstd])
                for c in range(4):
                    tn = tmpn[c % 2]
                    K.op(K.DVE, lambda tn=tn, c=c, which=which: nc.vector.tensor_tensor(tn[:, :], pj[:, which * 4 + c, :], rstd[:, :], ALU.mult),
                         reads=[pj, rstd], writes=[tn])
                    K.op(K.ACT, lambda tn=tn, c=c, key=key: nc.scalar.activation(stg[:, c, :], tn[:, :], AF.Copy, scale=pvcol(C, key, c)),
                         reads=[tn, C.pv], writes=[stg])
                K.dma(K.ACT, st_o, dst[:, :].rearrange("(c p) t -> p c t", p=128)[:, :, tsl], stg[:, :, :], reads=[stg], writes=[mid_t[g]])
            rope_tables(K, nc, C, pos_i, cs, scr)
            K.dma(K.ACT, st_o2, cs_d[0:64, tsl], cs[0][:, :], reads=[cs[0]], writes=[mid_t[g]])
            K.dma(K.ACT, st_o2, cs_d[64:128, tsl], cs[1][:, :], reads=[cs[1]], writes=[mid_t[g]])
            pr = P[4]
            K.mm(pr, pr[0:64, :], C.pm, C.pm[:, :], pj, pj[0:64, 8, :], True, True)
            K.op(K.DVE, lambda: nc.vector.tensor_tensor(ra[:, :], pj[0:64, 8, :], cs[0][:, :], ALU.mult), reads=[pj, cs[0]], writes=[ra])
            K.op(K.DVE, lambda: nc.vector.tensor_tensor(rb[:, :], pr[0:64, :], cs[1][:, :], ALU.mult), reads=[pr, cs[1]], writes=[rb])
            K.op(K.DVE, lambda: nc.vector.tensor_tensor(krb[:, :], ra[:, :], rb[:, :], ALU.add), reads=[ra, rb], writes=[krb])
            K.dma(K.ACT, st_o2, kr_d[:, tsl], krb[:, :], reads=[krb], writes=[mid_t[g]])
            K.retoken(st_o2, [cs[0], cs[1], krb, mid_t[g]])


def mla_stage_b(K, C, wuq_b, wukv_b, cq_d, ckv_d, kr_d, cs_d, mid_t, o_d, o_t, prep_t):
    nc = K.nc
    T = C.T
    NSEQ = T // S
    P = C.psum
    with K.stage():
        wuq = K.sb("mb_wuq", [128, 4, 3072], BF16)
        wukv = K.sb("mb_wukv", [128, 4, 4096], BF16)
        cmf = K.sb("mb_cmf", [128, 2048], F32)
        cm = K.sb("mb_cm", [128, 4, TG], BF16)
        cqT = K.sb("mb_cq", [128, 4, S], BF16)
        ckvT = K.sb("mb_ckv", [128, 4, S], BF16)
        krT = K.sb("mb_kr", [64, S], BF16)
        cos = K.sb("mb_cos", [64, S], F32)
        sin = K.sb("mb_sin", [64, S], F32)
        qn = [K.sb("mb_qn%d" % i, [128, S], BF16) for i in range(2)]
        qr = [K.sb("mb_qr%d" % i, [64, S], BF16) for i in range(2)]
        kn = [K.sb("mb_kn%d" % i, [128, S], BF16) for i in range(2)]
        Vh = [K.sb("mb_v%d" % i, [128, 16, 128], BF16) for i in range(2)]
        qraw = [K.sb("mb_qraw%d" % i, [64, TG], F32) for i in range(2)]
        ra = [K.sb("mb_ra%d" % i, [64, TG], F32) for i in range(2)]
        rb = [K.sb("mb_rb%d" % i, [64, TG], F32) for i in range(2)]
        E = [K.sb("mb_E%d" % i, [128, TG], BF16) for i in range(3)]
        rec = K.sb("mb_rec", [128, TG], F32)
        ost = [K.sb("mb_o%d" % i, [128, TG], BF16) for i in range(2)]
        st_w = K.stream("wres")
        st_a = K.stream("xf")
        st_o = [K.stream("stg"), K.stream("stg2")]
        for q in range(4):
            K.dma(K.SP, st_w, wuq[:, q, :], wuq_b[:, q * 3072:(q + 1) * 3072], reads=[prep_t], writes=[wuq])
            K.dma(K.SP, st_w, wukv[:, q, :], wukv_b[:, q * 4096:(q + 1) * 4096], reads=[prep_t], writes=[wukv])
        K.dma(K.SP, st_w, cmf[:, :], C.c_cmask_d[:, :], writes=[cmf])
        K.retoken(st_w, [wuq, wukv, cmf])
        K.op(K.DVE, lambda: nc.vector.tensor_copy(cm[:, :, :], cmf[:, :].rearrange("p (j t) -> p j t", j=4)), reads=[cmf], writes=[cm])
        oi = 0
        ei = 0
        for sq_ in range(NSEQ):
            ssl = slice(sq_ * S, (sq_ + 1) * S)
            gts = mid_t[sq_ * 4:(sq_ + 1) * 4]
            K.dma(K.SP, st_a, cqT[:, :, :], cq_d[:, :].rearrange("(c p) t -> p c t", p=128)[:, :, ssl], reads=gts, writes=[cqT])
            K.dma(K.SP, st_a, ckvT[:, :, :], ckv_d[:, :].rearrange("(c p) t -> p c t", p=128)[:, :, ssl], reads=gts, writes=[ckvT])
            K.dma(K.SP, st_a, krT[:, :], kr_d[:, ssl], reads=gts, writes=[krT])
            K.dma(K.SP, st_a, cos[:, :], cs_d[0:64, ssl], reads=gts, writes=[cos])
            K.dma(K.SP, st_a, sin[:, :], cs_d[64:128, ssl], reads=gts, writes=[sin])
            K.retoken(st_a, [cqT, ckvT, krT, cos, sin])
            for h in range(MH):
                b = h % 2
                for g in range(4):
                    gsl = slice(g * TG, (g + 1) * TG)
                    pq, pr, pk, prot = P[0], P[1], P[2], P[3]
                    for kc in range(4):
                        K.mm(pq, pq[:, :], wuq, wuq[:, kc, h * 192:h * 192 + 128], cqT, cqT[:, kc, gsl], kc == 0, kc == 3)
                    K.op(K.ACT, lambda pq=pq, b=b, gsl=gsl: nc.scalar.activation(qn[b][:, gsl], pq[:, :], AF.Copy, scale=SCALE),
                         reads=[pq], writes=[qn[b]])
                    for kc in range(4):
                        K.mm(pr, pr[0:64, :], wuq, wuq[:, kc, h * 192 + 128:h * 192 + 192], cqT, cqT[:, kc, gsl], kc == 0, kc == 3)
                    qw = qraw[g % 2]
                    K.op(K.ACT, lambda pr=pr, qw=qw: nc.scalar.activation(qw[:, :], pr[0:64, :], AF.Copy, scale=SCALE),
                         reads=[pr], writes=[qw])
                    for kc in range(4):
                        K.mm(pk, pk[:, :], wukv, wukv[:, kc, h * 256:h * 256 + 128], ckvT, ckvT[:, kc, gsl], kc == 0, kc == 3)
                    K.op(K.ACT, lambda pk=pk, b=b, gsl=gsl: nc.scalar.copy(kn[b][:, gsl], pk[:, :]), reads=[pk], writes=[kn[b]])
                    K.mm(prot, prot[0:64, :], C.pm, C.pm[:, :], qw, qw[:, :], True, True)
                    a_, b_ = ra[g % 2], rb[g % 2]
                    K.op(K.DVE, lambda a_=a_, qw=qw, gsl=gsl: nc.vector.tensor_tensor(a_[:, :], qw[:, :], cos[:, gsl], ALU.mult),
                         reads=[qw, cos], writes=[a_])
                    K.op(K.DVE, lambda b_=b_, prot=prot, gsl=gsl: nc.vector.tensor_tensor(b_[:, :], prot[0:64, :], sin[:, gsl], ALU.mult),
                         reads=[prot, sin], writes=[b_])
                    K.op(K.DVE, lambda a_=a_, b_=b_, b=b, gsl=gsl: nc.vector.tensor_tensor(qr[b][:, gsl], a_[:, :], b_[:, :], ALU.add),
                         reads=[a_, b_], writes=[qr[b]])
                for q4 in range(4):
                    pv = P[4 + q4 % 2]
                    for j in range(4):
                        tt = q4 * 4 + j
                        for kc in range(4):
                            K.mm(pv, pv[:, j * 128:(j + 1) * 128], ckvT, ckvT[:, kc, tt * 128:(tt + 1) * 128],
                                 wukv, wukv[:, kc, h * 256 + 128:h * 256 + 256], kc == 0, kc == 3)
                    K.op(K.DVE, lambda pv=pv, b=b, q4=q4: nc.vector.tensor_copy(
                        Vh[b][:, 4 * q4:4 * q4 + 4, :], pv[:, :].rearrange("p (j d) -> p j d", j=4)), reads=[pv], writes=[Vh[b]])
                for qg in range(4):
                    qsl = slice(qg * TG, (qg + 1) * TG)
                    pO, pD = P[6], P[7]
                    nk = 4 * (qg + 1)
                    pend_pv = []
                    for kt in range(nk):
                        ksl = slice(kt * 128, (kt + 1) * 128)
                        pS = P[4 + ei % 2]
                        e = E[ei % 3]
                        ei += 1
                        K.mm(pS, pS[:, :], kn[b], kn[b][:, ksl], qn[b], qn[b][:, qsl], True, False)
                        K.mm(pS, pS[:, :], krT, krT[:, ksl], qr[b], qr[b][:, qsl], False, True)
                        K.op(K.ACT, lambda e=e, pS=pS: nc.scalar.activation(e[:, :], pS[:, :], AF.Exp), reads=[pS], writes=[e])
                        j = kt - 4 * qg
                        if j >= 0:
                            K.op(K.DVE, lambda e=e, j=j: nc.vector.tensor_tensor(e[:, :], e[:, :], cm[:, j, :], ALU.mult),
                                 reads=[e, cm], writes=[e])

                        def pv_fn(e=e, kt=kt):
                            K.mm(pO, pO[:, :], Vh[b], Vh[b][:, kt, :], e, e[:, :], kt == 0, kt == nk - 1, mark=False)
                            K.mm(pD, pD[:, :], C.ones_b, C.ones_b[:, :], e, e[:, :], kt == 0, kt == nk - 1, mark=True)
                        if pend_pv:
                            pend_pv.pop(0)()
                        pend_pv.append(pv_fn)
                    while pend_pv:
                        pend_pv.pop(0)()
                    o_ = ost[oi % 2]
                    K.op(K.DVE, lambda pD=pD: nc.vector.reciprocal(rec[:, :], pD[:, :]), reads=[pD], writes=[rec])
                    K.op(K.DVE, lambda o_=o_, pO=pO: nc.vector.tensor_tensor(o_[:, :], pO[:, :], rec[:, :], ALU.mult),
                         reads=[pO, rec], writes=[o_])
                    K.dma(K.ACT, st_o[oi % 2], o_d[h * 128:(h + 1) * 128, sq_ * S + qg * TG:sq_ * S + (qg + 1) * TG], o_[:, :],
                          reads=[o_], writes=[o_t[sq_ * 4 + qg]])
                    oi += 1
        for tl in o_t:
            tl.w = {id(s_.sem): (s_.sem, s_.count, None) for s_ in st_o}


def mla_stage_c(K, C, li, wo_b, o_d, o_t, xin, xin_t, xout, xout_t, prep_t):
    nc = K.nc
    T = C.T
    NG = T // TG
    P = C.psum
    with K.stage():
        wo = K.sb("mc_wo", [128, DC, D], BF16)
        ob = K.sb("mc_ob", [128, DC, TG], BF16)
        xacc = K.sb("mc_xacc", [128, DC, TG], F32)
        st_w = K.stream("wres")
        st_x = K.stream("xf")
        st_ob = K.stream("xb")
        wov = wo_b[:, :].rearrange("p (k n) -> p k n", k=DC)
        for q in range(4):
            K.dma(K.SP, st_w, wo[:, 4 * q:4 * q + 4, :], wov[:, 4 * q:4 * q + 4, :], reads=[prep_t], writes=[wo])
        xin_v = xin[:, :].rearrange("(c p) t -> p c t", p=128)
        xout_v = xout[:, :].rearrange("(c p) t -> p c t", p=128)
        o_v = o_d[:, :].rearrange("(c p) t -> p c t", p=128)
        for g in range(NG):
            tsl = slice(g * TG, (g + 1) * TG)
            load_x_group(K, nc, st_x, xacc, None, xin_v, tsl, xin_t[g])
            for q in range(4):
                K.dma(K.SP, st_ob, ob[:, 4 * q:4 * q + 4, :], o_v[:, 4 * q:4 * q + 4, tsl], reads=[o_t[g]], writes=[ob])
            for n in range(DC):
                po = P[n % 2]
                for kc in range(DC):
                    K.mm(po, po[:, :], wo, wo[:, kc, n * 128:(n + 1) * 128], ob, ob[:, kc, :], kc == 0, kc == DC - 1)
                K.op(K.DVE, lambda po=po, n=n: nc.vector.scalar_tensor_tensor(
                    xacc[:, n, :], xacc[:, n, :], ALPHA, po[:, :], ALU.mult, ALU.add), reads=[xacc, po], writes=[xacc])
            layer_norm_inplace(K, C, xacc, li, 0, P[6], P[7])
            for q in range(4):
                K.dma(K.ACT, st_x, xout_v[:, 4 * q:4 * q + 4, tsl], xacc[:, 4 * q:4 * q + 4, :], reads=[xacc], writes=[xout_t[g]])


SH = 64
SP_ = 64
SN = 128
NCH_IN = 81


def lay_ssm_in(w):
    wp = np.zeros((2048, NCH_IN * 128), np.float32)
    wp[:, :10304] = w
    a = wp.reshape(16, 128, NCH_IN, 128)
    return np.ascontiguousarray(a.transpose(2, 1, 0, 3).reshape(NCH_IN * 128, 2048))


def lay_ssm_out(w):
    a = w.reshape(32, 128, 16, 128)
    return np.ascontiguousarray(a.transpose(2, 1, 0, 3).reshape(16 * 128, 4096))


def ssm_host_vec(inp):
    return np.ascontiguousarray(np.stack([inp["ssm_dt_bias"][0], inp["ssm_a_log"][0]]).astype(np.float32).reshape(1, 128))


def ssm_host_consts():
    tri = (np.arange(128)[:, None] <= np.arange(128)[None, :]).astype(np.float32)
    return {"c_tri": tri}


def ssd_stage_a(K, C, win_b, vec_d, xin, xin_t, z_d, xbc_d, dt_d, mid_t, prep_t):
    nc = K.nc
    T = C.T
    NG = T // TG
    P = C.psum
    with K.stage():
        xb = K.sb("sa_xb", [128, DC, TG], BF16)
        xacc = K.sb("sa_xacc", [128, DC, TG], F32)
        wr = [K.sb("sa_w%d" % i, [128, 2048], BF16) for i in range(3)]
        ub = [K.sb("sa_ub%d" % i, [128, TG + 3], F32) for i in range(2)]
        vv = [K.sb("sa_v%d" % i, [128, TG], F32) for i in range(2)]
        halo = K.sb("sa_halo", [128, 48, 3], F32)
        stg = [K.sb("sa_stg%d" % i, [128, TG], BF16) for i in range(4)]
        vec = K.sb("sa_vec", [128, 128], F32)
        dtt = [K.sb("sa_dt%d" % i, [128, 64], F32) for i in range(2)]
        st_x = K.stream("xf")
        st_w = [K.stream("wring%d" % i) for i in range(3)]
        st_s = [K.stream("s_stg%d" % i) for i in range(4)]
        st_v = K.stream("wres")
        st_dt = [K.stream("stg"), K.stream("stg2")]
        K.dma(K.SP, st_v, vec[:, :], vec_d[0:1, :].to_broadcast([128, 128]), writes=[vec])
        xin_v = xin[:, :].rearrange("(c p) t -> p c t", p=128)
        wl = 0
        si = 0
        for g in range(NG):
            tsl = slice(g * TG, (g + 1) * TG)
            load_x_group(K, nc, st_x, xacc, xb, xin_v, tsl, xin_t[g])
            if (g * TG) % S == 0:
                K.op(K.DVE, lambda: nc.vector.memset(halo[:, :, :], 0.0), writes=[halo])
            for n in range(NCH_IN):
                w = wr[wl % 3]
                K.dma(K.SP, st_w[wl % 3], w[:, :], win_b[n * 128:(n + 1) * 128, :], reads=[prep_t], writes=[w])
                wl += 1
                if n < 80:
                    pp = P[n % 4]
                    for kc in range(DC):
                        K.mm(pp, pp[:, :], w, w[:, kc * 128:(kc + 1) * 128], xb, xb[:, kc, :], kc == 0, kc == DC - 1)
                    so = stg[si % 4]
                    sst = st_s[si % 4]
                    si += 1
                    if n < 32:
                        K.op(K.ACT, lambda so=so, pp=pp: nc.scalar.activation(so[:, :], pp[:, :], AF.Silu), reads=[pp], writes=[so])
                        K.dma(K.ACT, sst, z_d[n * 128:(n + 1) * 128, tsl], so[:, :], reads=[so], writes=[mid_t[g]])
                    else:
                        cc = n - 32
                        u = ub[cc % 2]
                        v = vv[cc % 2]
                        K.op(K.DVE, lambda u=u, cc=cc: nc.vector.tensor_copy(u[:, 0:3], halo[:, cc, :]), reads=[halo], writes=[u])
                        K.op(K.ACT, lambda u=u, pp=pp: nc.scalar.copy(u[:, 3:TG + 3], pp[:, :]), reads=[pp], writes=[u])
                        K.op(K.DVE, lambda u=u, cc=cc: nc.vector.tensor_copy(halo[:, cc, :], u[:, TG:TG + 3]), reads=[u], writes=[halo])
                        K.op(K.DVE, lambda u=u, v=v, cc=cc: nc.vector.tensor_scalar(
                            v[:, :], u[:, 0:TG], pvcol(C, ("ssm_cw", 0), cc), None, ALU.mult), reads=[u, C.pv], writes=[v])
                        for k in (1, 2, 3):
                            K.op(K.DVE, lambda u=u, v=v, cc=cc, k=k: nc.vector.scalar_tensor_tensor(
                                v[:, :], u[:, k:TG + k], pvcol(C, ("ssm_cw", k), cc), v[:, :], ALU.mult, ALU.add),
                                reads=[u, v, C.pv], writes=[v])
                        K.op(K.ACT, lambda so=so, v=v, cc=cc: nc.scalar.activation(
                            so[:, :], v[:, :], AF.Silu, bias=pvcol(C, ("ssm_cb",), cc)), reads=[v, C.pv], writes=[so])
                        K.dma(K.ACT, sst, xbc_d[cc * 128:(cc + 1) * 128, tsl], so[:, :], reads=[so], writes=[mid_t[g]])
                else:
                    for tt in range(4):
                        pd = P[4 + tt % 2]
                        for kc in range(DC):
                            K.mm(pd, pd[:, 0:64], xb, xb[:, kc, tt * 128:(tt + 1) * 128], w, w[:, kc * 128:kc * 128 + 64],
                                 kc == 0, kc == DC - 1)
                        d_ = dtt[tt % 2]
                        K.op(K.DVE, lambda d_=d_, pd=pd: nc.vector.tensor_tensor(d_[:, :], pd[:, 0:64], vec[:, 0:64], ALU.add),
                             reads=[pd, vec], writes=[d_])
                        K.op(K.ACT, lambda d_=d_: nc.scalar.activation(d_[:, :], d_[:, :], AF.Exp), reads=[d_], writes=[d_])
                        K.op(K.ACT, lambda d_=d_: nc.scalar.activation(d_[:, :], d_[:, :], AF.Ln, bias=1.0), reads=[d_], writes=[d_])
                        K.dma(K.ACT, st_dt[tt % 2], dt_d[g * TG + tt * 128:g * TG + (tt + 1) * 128, :], d_[:, :],
                              reads=[d_], writes=[mid_t[g]])
        for tl in mid_t:
            for s_ in st_s + st_dt:
                tl.w[id(s_.sem)] = (s_.sem, s_.count, None)


def setup_ssm_consts(K, C):
    nc = K.nc
    C.c_tri_d = K.dram("c_tri", [128, 128], F32, kind="ExternalInput")
    C.tri = K.sb("tri_f", [128, 128], F32)
    stp = K.stream("misc_s")
    K.dma(K.SP, stp, C.tri[:, :], C.c_tri_d[:, :], writes=[C.tri])


def ssd_stage_b(K, C, vec_d, z_d, xbc_d, dt_d, mid_t, yn_d, yn_t):
    nc = K.nc
    T = C.T
    NSEQ = T // S
    P = C.psum
    with K.stage():
        vec = K.sb("sb_vec", [128, 128], F32)
        aneg = K.sb("sb_aneg", [128, 64], F32)
        state_f = K.sb("sb_stf", [128, SH, SP_], F32)
        state_b = K.sb("sb_stb", [128, SH, SP_], BF16)
        xsT2 = [K.sb("sb_xsT%d" % i, [128, 32, 128], BF16) for i in range(2)]
        zsT2 = [K.sb("sb_zsT%d" % i, [128, 32, 128], BF16) for i in range(2)]
        BT2 = [K.sb("sb_BT%d" % i, [128, 8, 128], BF16) for i in range(2)]
        CT2 = [K.sb("sb_CT%d" % i, [128, 8, 128], BF16) for i in range(2)]
        dtT2 = [K.sb("sb_dtT%d" % i, [128, 64], F32) for i in range(2)]
        xtok = K.sb("sb_xtok", [128, SH, SP_], BF16)
        Btok = K.sb("sb_Btok", [128, 8, 128], BF16)
        xdt = K.sb("sb_xdt", [128, SH, SP_], BF16)
        xdd = K.sb("sb_xdd", [128, SH, SP_], BF16)
        daT = K.sb("sb_daT", [128, 64], F32)
        acT = K.sb("sb_acT", [128, 64], F32)
        acF = K.sb("sb_acF", [64, 128], F32)
        lastb = K.sb("sb_lastb", [128, 64], F32)
        decT = K.sb("sb_decT", [128, 64], F32)
        ddT = K.sb("sb_ddT", [128, 64], F32)
        fac = K.sb("sb_fac", [128, 64], F32)
        cbm = K.sb("sb_cbm", [128, 8, 128], BF16)
        bcl = [K.sb("sb_bcl%d" % i, [128, 4, 128], F32) for i in range(2)]
        MT4 = [K.sb("sb_MT%d" % i, [128, 4, 128], BF16) for i in range(2)]
        ebc = [K.sb("sb_ebc%d" % i, [128, 4, 128], F32) for i in range(2)]
        Cs4 = [K.sb("sb_Cs%d" % i, [128, 4, 128], BF16) for i in range(2)]
        stmp = K.sb("sb_stmp", [128, 8, SP_], F32)
        t1 = [K.sb("sb_t1_%d" % i, [128, 128], F32) for i in range(2)]
        t2 = K.sb("sb_t2", [128, 4, 128], F32)
        sqs = [K.sb("sb_sq%d" % i, [128, 128], F32) for i in range(2)]
        rstd = K.sb("sb_rstd", [128, 128], F32)
        t3 = [K.sb("sb_t3_%d" % i, [128, 128], F32) for i in range(2)]
        ynb = K.sb("sb_ynb", [128, 32, 128], BF16)
        st_v = K.stream("wres")
        st_l2 = [K.stream("xf"), K.stream("xb")]
        st_o = K.stream("stg")
        K.dma(K.SP, st_v, vec[:, :], vec_d[0:1, :].to_broadcast([128, 128]), writes=[vec])
        K.op(K.ACT, lambda: nc.scalar.activation(aneg[:, :], vec[:, 64:128], AF.Exp), reads=[vec], writes=[aneg])
        K.op(K.DVE, lambda: nc.vector.tensor_scalar(aneg[:, :], aneg[:, :], -1.0, None, ALU.mult), reads=[aneg], writes=[aneg])

        def issue_loads(sq_, ck):
            t0 = sq_ * S + ck * 128
            csl = slice(t0, t0 + 128)
            gt = [mid_t[t0 // TG]]
            st_l = st_l2[ck % 2]
            xsT, zsT, BT, CT, dtT = xsT2[ck % 2], zsT2[ck % 2], BT2[ck % 2], CT2[ck % 2], dtT2[ck % 2]
            xv = xbc_d[:, :].rearrange("(c p) t -> p c t", p=128)
            for q in range(4):
                K.dma(K.SP, st_l, xsT[:, 8 * q:8 * q + 8, :], xv[:, 8 * q:8 * q + 8, csl], reads=gt, writes=[xsT])
            K.dma(K.SP, st_l, BT[:, :, :], xv[:, 32:40, csl], reads=gt, writes=[BT])
            K.dma(K.SP, st_l, CT[:, :, :], xv[:, 40:48, csl], reads=gt, writes=[CT])
            zv = z_d[:, :].rearrange("(c p) t -> p c t", p=128)
            for q in range(4):
                K.dma(K.SP, st_l, zsT[:, 8 * q:8 * q + 8, :], zv[:, 8 * q:8 * q + 8, csl], reads=gt, writes=[zsT])
            K.dma(K.SP, st_l, dtT[:, :], dt_d[t0:t0 + 128, :], reads=gt, writes=[dtT])
            K.retoken(st_l, [xsT, BT, CT, zsT, dtT])

        tri = C.tri
        idf = C.ident_f
        NCK = S // 128
        for sq_ in range(NSEQ):
            K.op(K.DVE, lambda: nc.vector.memset(state_f[:, :, :], 0.0), writes=[state_f])
            K.op(K.DVE, lambda: nc.vector.memset(state_b[:, :, :], 0.0), writes=[state_b])
            for ck in range(NCK):
                t0 = sq_ * S + ck * 128
                csl = slice(t0, t0 + 128)
                xsT, zsT, BT, CT, dtT = xsT2[ck % 2], zsT2[ck % 2], BT2[ck % 2], CT2[ck % 2], dtT2[ck % 2]
                if ck == 0:
                    issue_loads(sq_, 0)
                if ck + 1 < NCK:
                    issue_loads(sq_, ck + 1)
                for q in range(4):
                    pt = P[q % 2]
                    ptb = pt[:, :].bitcast(BF16)
                    for j in range(8):
                        K.op(K.PE, lambda ptb=ptb, q=q, j=j: nc.tensor.transpose(ptb[:, j * 128:(j + 1) * 128], xsT[:, q * 8 + j, :], C.ident_b[:, :]),
                             reads=[xsT, C.ident_b], writes=[pt], mark=(j == 7))
                    K.op(K.ACT, lambda ptb=ptb, q=q: nc.scalar.copy(
                        xtok[:, q * 16:(q + 1) * 16, :], ptb.rearrange("p (e d) -> p e d", d=SP_)), reads=[pt], writes=[xtok])
                pt = P[2]
                ptb = pt[:, :].bitcast(BF16)
                for j in range(8):
                    K.op(K.PE, lambda ptb=ptb, j=j: nc.tensor.transpose(ptb[:, j * 128:(j + 1) * 128], BT[:, j, :], C.ident_b[:, :]),
                         reads=[BT, C.ident_b], writes=[pt], mark=(j == 7))
                K.op(K.ACT, lambda ptb=ptb: nc.scalar.copy(Btok[:, :, :], ptb.rearrange("p (g n) -> p g n", n=128)), reads=[pt], writes=[Btok])
                K.op(K.DVE, lambda: nc.vector.tensor_tensor(daT[:, :], dtT[:, :], aneg[:, :], ALU.mult), reads=[dtT, aneg], writes=[daT])
                pa = P[3]
                K.mm(pa, pa[:, 0:64], tri, tri[:, :], daT, daT[:, :], True, True)
                K.mm(pa, pa[0:64, 128:256], daT, daT[:, :], tri, tri[:, :], True, True)
                K.op(K.ACT, lambda pa=pa: nc.scalar.copy(acT[:, :], pa[:, 0:64]), reads=[pa], writes=[acT])
                K.op(K.ACT, lambda pa=pa: nc.scalar.copy(acF[:, :], pa[0:64, 128:256]), reads=[pa], writes=[acF])
                pl = P[3]
                K.mm(pl, pl[:, 256:320], idf, idf[:, 127:128].to_broadcast([128, 128]), acT, acT[:, :], True, True)
                K.op(K.DVE, lambda pl=pl: nc.vector.tensor_copy(lastb[:, :], pl[:, 256:320]), reads=[pl], writes=[lastb])
                K.op(K.DVE, lambda: nc.vector.tensor_tensor(decT[:, :], lastb[:, :], acT[:, :], ALU.subtract), reads=[lastb, acT], writes=[decT])
                K.op(K.ACT, lambda: nc.scalar.activation(decT[:, :], decT[:, :], AF.Exp), reads=[decT], writes=[decT])
                K.op(K.ACT, lambda: nc.scalar.activation(fac[:, :], lastb[:, :], AF.Exp), reads=[lastb], writes=[fac])
                K.op(K.DVE, lambda: nc.vector.tensor_tensor(ddT[:, :], dtT[:, :], decT[:, :], ALU.mult), reads=[dtT, decT], writes=[ddT])
                K.op(K.DVE, lambda: nc.vector.tensor_tensor(xdt[:, :, :], xtok[:, :, :], dtT[:, :].unsqueeze(2).to_broadcast([128, SH, SP_]), ALU.mult),
                     reads=[xtok, dtT], writes=[xdt])
                K.op(K.DVE, lambda: nc.vector.tensor_tensor(xdd[:, :, :], xtok[:, :, :], ddT[:, :].unsqueeze(2).to_broadcast([128, SH, SP_]), ALU.mult),
                     reads=[xtok, ddT], writes=[xdd])
                for hf in range(2):
                    pc = P[4 + hf]
                    for j in range(4):
                        gq = hf * 4 + j
                        K.mm(pc, pc[:, j * 128:(j + 1) * 128], BT, BT[:, gq, :], CT, CT[:, gq, :], True, True)
                    K.op(K.DVE, lambda pc=pc, hf=hf: nc.vector.tensor_tensor(
                        cbm[:, hf * 4:hf * 4 + 4, :], pc[:, :].rearrange("p (j t) -> p j t", j=4),
                        tri[:, :].unsqueeze(1).to_broadcast([128, 4, 128]), ALU.mult), reads=[pc, tri], writes=[cbm])
                def part_a(e4):
                    gq = e4 // 2
                    pb = P[6 + e4 % 2]
                    bl = bcl[e4 % 2]
                    mt = MT4[e4 % 2]
                    eb = ebc[e4 % 2]
                    cs = Cs4[e4 % 2]
                    for j in range(4):
                        e = e4 * 4 + j
                        K.mm(pb, pb[:, j * 128:(j + 1) * 128], idf, idf[0:64, e:e + 1].to_broadcast([64, 128]), acF, acF[:, :], True, True)
                    for j in range(4):
                        e = e4 * 4 + j
                        K.op(K.DVE, lambda bl=bl, pb=pb, j=j, e=e: nc.vector.tensor_scalar(
                            bl[:, j, :], pb[:, j * 128:(j + 1) * 128], acT[:, e:e + 1], 0.0, ALU.subtract, ALU.min),
                            reads=[pb, acT], writes=[bl])
                    K.op(K.ACT, lambda bl=bl: nc.scalar.activation(bl[:, :, :], bl[:, :, :], AF.Exp), reads=[bl], writes=[bl])
                    K.op(K.DVE, lambda mt=mt, bl=bl, gq=gq: nc.vector.tensor_tensor(
                        mt[:, :, :], bl[:, :, :], cbm[:, gq:gq + 1, :].to_broadcast([128, 4, 128]), ALU.mult), reads=[bl, cbm], writes=[mt])
                    K.op(K.ACT, lambda eb=eb, pb=pb: nc.scalar.activation(eb[:, :, :], pb[:, :].rearrange("p (j t) -> p j t", j=4), AF.Exp),
                         reads=[pb], writes=[eb])
                    K.op(K.DVE, lambda cs=cs, eb=eb, gq=gq: nc.vector.tensor_tensor(
                        cs[:, :, :], eb[:, :, :], CT[:, gq:gq + 1, :].to_broadcast([128, 4, 128]), ALU.mult), reads=[eb, CT], writes=[cs])
                    return mt, cs

                def part_b(e4, mt, cs):
                    for jp in range(2):
                        pr_i = e4 * 2 + jp
                        py = P[jp]
                        for jj in range(2):
                            j = jp * 2 + jj
                            e = e4 * 4 + j
                            osl = slice(jj * 64, jj * 64 + 64)
                            K.op(K.PE, lambda py=py, osl=osl, e=e, mt=mt, j=j, jj=jj: nc.tensor.matmul(
                                py[osl, 0:128], xdt[:, e, :], mt[:, j, :], start=True, stop=False, tile_position=(0, jj * 64)),
                                reads=[xdt, mt], writes=[py], mark=False)
                            K.op(K.PE, lambda py=py, osl=osl, e=e, cs=cs, j=j, jj=jj: nc.tensor.matmul(
                                py[osl, 0:128], state_b[:, e, :], cs[:, j, :], start=False, stop=True, tile_position=(0, jj * 64)),
                                reads=[state_b, cs], writes=[py], mark=True)
                        ta = t1[pr_i % 2]
                        K.op(K.DVE, lambda ta=ta, py=py, pr_i=pr_i: nc.vector.scalar_tensor_tensor(
                            ta[:, :], xsT[:, pr_i, :], pvcol(C, ("ssm_dsk",), pr_i), py[:, 0:128], ALU.mult, ALU.add),
                            reads=[xsT, py, C.pv], writes=[ta])
                        K.op(K.DVE, lambda ta=ta, pr_i=pr_i: nc.vector.tensor_tensor(t2[:, pr_i % 4, :], ta[:, :], zsT[:, pr_i, :], ALU.mult),
                             reads=[ta, zsT], writes=[t2])
                        sq = sqs[pr_i % 2]
                        K.op(K.ACT, lambda sq=sq, pr_i=pr_i: nc.scalar.activation(sq[:, :], t2[:, pr_i % 4, :], AF.Square), reads=[t2], writes=[sq])
                        pn = P[2]
                        K.mm(pn, pn[:, 0:128], C.ones_f, C.ones_f[:, :], sq, sq[:, :], pr_i % 4 == 0, pr_i % 4 == 3, mark=True)
                        if pr_i % 4 == 3:
                            K.op(K.DVE, lambda pn=pn: nc.vector.tensor_scalar(rstd[:, :], pn[:, 0:128], 1.0 / 512.0, RMS_EPS, ALU.mult, ALU.add),
                                 reads=[pn], writes=[rstd])
                            K.op(K.ACT, lambda: nc.scalar.activation(rstd[:, :], rstd[:, :], AF.Sqrt), reads=[rstd], writes=[rstd])
                            K.op(K.DVE, lambda: nc.vector.reciprocal(rstd[:, :], rstd[:, :]), reads=[rstd], writes=[rstd])
                            for k4 in range(4):
                                pi = pr_i - 3 + k4
                                tb = t3[k4 % 2]
                                K.op(K.DVE, lambda tb=tb, k4=k4: nc.vector.tensor_tensor(tb[:, :], t2[:, k4, :], rstd[:, :], ALU.mult),
                                     reads=[t2, rstd], writes=[tb])
                                K.op(K.ACT, lambda tb=tb, pi=pi: nc.scalar.activation(ynb[:, pi, :], tb[:, :], AF.Copy, scale=pvcol(C, ("ssm_nw",), pi)),
                                     reads=[tb, C.pv], writes=[ynb])

                nxt_mc = part_a(0)
                for e4 in range(SH // 4):
                    cur_mc = nxt_mc
                    if e4 + 1 < SH // 4:
                        nxt_mc = part_a(e4 + 1)
                    part_b(e4, *cur_mc)
                for gq in range(8):
                    ph = P[4 + gq % 2]
                    K.mm(ph, ph[:, :], Btok, Btok[:, gq, :], xdd, xdd[:, gq * 8:(gq + 1) * 8, :].rearrange("p e d -> p (e d)"), True, True)
                    K.op(K.DVE, lambda gq=gq: nc.vector.tensor_tensor(
                        stmp[:, :, :], state_f[:, gq * 8:(gq + 1) * 8, :], fac[:, gq * 8:(gq + 1) * 8].unsqueeze(2).to_broadcast([128, 8, SP_]), ALU.mult),
                        reads=[state_f, fac], writes=[stmp])
                    K.op(K.DVE, lambda gq=gq, ph=ph: nc.vector.tensor_tensor(
                        state_f[:, gq * 8:(gq + 1) * 8, :], stmp[:, :, :], ph[:, :].rearrange("p (e d) -> p e d", d=SP_), ALU.add),
                        reads=[stmp, ph], writes=[state_f])
                K.op(K.ACT, lambda: nc.scalar.copy(state_b[:, :, :], state_f[:, :, :]), reads=[state_f], writes=[state_b])
                yv = yn_d[:, :].rearrange("(c p) t -> p c t", p=128)
                for q in range(4):
                    K.dma(K.ACT, st_o, yv[:, 8 * q:8 * q + 8, csl], ynb[:, 8 * q:8 * q + 8, :], reads=[ynb], writes=[yn_t[t0 // TG]])


def ssd_stage_c(K, C, li, wo_b, yn_d, yn_t, xin, xin_t, xout, xout_t, prep_t):
    nc = K.nc
    T = C.T
    NG = T // TG
    P = C.psum
    with K.stage():
        wr = [K.sb("sc_w%d" % i, [128, 4096], BF16) for i in range(3)]
        yb = K.sb("sc_yb", [128, 32, TG], BF16)
        xacc = K.sb("sc_xacc", [128, DC, TG], F32)
        st_w = [K.stream("wring%d" % i) for i in range(3)]
        st_x = K.stream("xf")
        st_y = K.stream("xb")
        xin_v = xin[:, :].rearrange("(c p) t -> p c t", p=128)
        xout_v = xout[:, :].rearrange("(c p) t -> p c t", p=128)
        y_v = yn_d[:, :].rearrange("(c p) t -> p c t", p=128)
        wl = 0
        for g in range(NG):
            tsl = slice(g * TG, (g + 1) * TG)
            load_x_group(K, nc, st_x, xacc, None, xin_v, tsl, xin_t[g])
            for q in range(8):
                K.dma(K.SP, st_y, yb[:, 4 * q:4 * q + 4, :], y_v[:, 4 * q:4 * q + 4, tsl], reads=[yn_t[g]], writes=[yb])
            for n in range(DC):
                w = wr[wl % 3]
                K.dma(K.SP, st_w[wl % 3], w[:, :], wo_b[n * 128:(n + 1) * 128, :], reads=[prep_t], writes=[w])
                wl += 1
                po = P[n % 2]
                for kc in range(32):
                    K.mm(po, po[:, :], w, w[:, kc * 128:(kc + 1) * 128], yb, yb[:, kc, :], kc == 0, kc == 31)
                K.op(K.DVE, lambda po=po, n=n: nc.vector.scalar_tensor_tensor(
                    xacc[:, n, :], xacc[:, n, :], ALPHA, po[:, :], ALU.mult, ALU.add), reads=[xacc, po], writes=[xacc])
            layer_norm_inplace(K, C, xacc, li, 0, P[6], P[7])
            for q in range(4):
                K.dma(K.ACT, st_x, xout_v[:, 4 * q:4 * q + 4, tsl], xacc[:, 4 * q:4 * q + 4, :], reads=[xacc], writes=[xout_t[g]])


NCORES = 8
NSEQ_CORE = 2
T_CORE = NSEQ_CORE * S

def weight_specs():
    sp = []
    for j in range(2):
        sp.append(("sc_in%d" % j, [2048, 6144]))
        sp.append(("sc_out%d" % j, [128, 16 * 2048]))
    sp += [("ml_in", [128, 16 * 1088]), ("ml_uq", [128, 4 * 3072]), ("ml_ukv", [128, 4 * 4096]), ("ml_o", [128, 16 * 2048])]
    sp += [("ss_in", [NCH_IN * 128, 2048]), ("ss_out", [2048, 4096])]
    for l in range(DEPTH):
        sp += [("pe_wq%d" % l, [2048, 2048]), ("pe_sk%d" % l, [128, 2048]), ("pe_ut%d" % l, [16384, 2048]), ("pe_v%d" % l, [16384, 2048])]
    return sp


CAST_GROUP = {"sc_in0": 0, "sc_out0": 0, "pe_wq0": 4, "pe_sk0": 4, "pe_ut0": 4, "pe_v0": 4,
              "ml_in": 1, "ml_uq": 1, "ml_ukv": 1, "ml_o": 1, "pe_wq1": 1, "pe_sk1": 1, "pe_ut1": 1, "pe_v1": 1,
              "ss_in": 2, "ss_out": 2, "pe_wq2": 2, "pe_sk2": 2, "pe_ut2": 2, "pe_v2": 2,
              "sc_in1": 3, "sc_out1": 3, "pe_wq3": 3, "pe_sk3": 3, "pe_ut3": 3, "pe_v3": 3}


def host_weights(inp):
    w = {}
    for j in range(2):
        w["sc_in%d" % j] = lay_conv_in(inp["sc_w_in"][j]).reshape(2048, 6144)
        w["sc_out%d" % j] = lay_kmajor(inp["sc_w_out"][j])
    w["ml_in"] = lay_kmajor(inp["mla_w_in"][0])
    w["ml_uq"] = lay_kmajor(inp["mla_w_uq"][0])
    w["ml_ukv"] = lay_kmajor(inp["mla_w_ukv"][0])
    w["ml_o"] = lay_kmajor(inp["mla_w_o"][0])
    w["ss_in"] = lay_ssm_in(inp["ssm_w_in"][0])
    w["ss_out"] = lay_ssm_out(inp["ssm_w_out"][0])
    for l in range(DEPTH):
        w["pe_wq%d" % l] = lay_wq(inp["peer_w_q"][l])
        w["pe_sk%d" % l] = lay_sk(inp["peer_sub_keys"][l])
        w["pe_ut%d" % l] = lay_ut(inp["peer_u"][l])
        w["pe_v%d" % l] = np.ascontiguousarray(inp["peer_v"][l], dtype=np.float32)
    return w


def build_program(T=T_CORE):
    K = KB()
    C = setup_common(K, T)
    setup_peer_consts(K, C)
    setup_mla_consts(K, C)
    setup_ssm_consts(K, C)
    NG = T // TG
    xT = K.dram("xT", [D, T], F32, kind="ExternalInput")
    pos_d = K.dram("pos", [1, T], I32, kind="ExternalInput")
    vec_d = K.dram("ssm_vec", [1, 128], F32, kind="ExternalInput")
    outT = K.dram("outT", [D, T], F32, kind="ExternalOutput")
    XA = K.dram("act_a", [D, T], F32)
    XB = K.dram("act_b", [D, T], F32)
    wf, wb = {}, {}
    for name, shape in weight_specs():
        wf[name] = K.dram(name, shape, F32, kind="ExternalInput")
        wb[name] = K.dram(name + "_bf", shape, BF16)
    prep = [K.vt("prep%d" % i) for i in range(5)]

    def casts(gi):
        st = K.stream("cast%d" % gi)
        for name, shape in weight_specs():
            if CAST_GROUP[name] == gi:
                cast_dram(K, st, wb[name][:, :], wf[name][:, :], shape[0], shape[1], [prep[gi]])

    def vts():
        return [K.vt() for _ in range(NG)]

    x_t = vts()
    bufs = [XA, XB]
    cur, cur_t = xT, x_t

    def nxt(i):
        return bufs[i % 2]

    casts(0)
    casts(4)
    step = 0
    for li in range(DEPTH):
        kind = li % 3
        dst, dst_t = nxt(step), vts()
        step += 1
        if kind == 0:
            j = li // 3
            conv_stage(K, C, li, j, wb["sc_in%d" % j][:, :].rearrange("(j p) n -> j p n", p=128), wb["sc_out%d" % j],
                       cur, cur_t, dst, dst_t, prep[li])
        elif kind == 1:
            cq_d = K.dram("cq_d", [512, T], BF16)
            ckv_d = K.dram("ckv_d", [512, T], BF16)
            kr_d = K.dram("kr_d", [64, T], BF16)
            cs_d = K.dram("cs_d", [128, T], F32)
            o_d = K.dram("o_d", [2048, T], BF16)
            mid_t, o_t = vts(), vts()
            mla_stage_a(K, C, wb["ml_in"], pos_d, cur, cur_t, cq_d, ckv_d, kr_d, cs_d, mid_t, prep[li])
            mla_stage_b(K, C, wb["ml_uq"], wb["ml_ukv"], cq_d, ckv_d, kr_d, cs_d, mid_t, o_d, o_t, prep[li])
            mla_stage_c(K, C, li, wb["ml_o"], o_d, o_t, cur, cur_t, dst, dst_t, prep[li])
        else:
            z_d = K.dram("z_d", [4096, T], BF16)
            xbc_d = K.dram("xbc_d", [6144, T], BF16)
            dt_d = K.dram("dt_d", [T, 64], F32)
            yn_d = K.dram("yn_d", [4096, T], BF16)
            mid_t, yn_t = vts(), vts()
            ssd_stage_a(K, C, wb["ss_in"], vec_d, cur, cur_t, z_d, xbc_d, dt_d, mid_t, prep[li])
            ssd_stage_b(K, C, vec_d, z_d, xbc_d, dt_d, mid_t, yn_d, yn_t)
            ssd_stage_c(K, C, li, wb["ss_out"], yn_d, yn_t, cur, cur_t, dst, dst_t, prep[li])
        cur, cur_t = dst, dst_t
        if li + 1 < DEPTH:
            casts(li + 1)
        last = li == DEPTH - 1
        dst, dst_t = (outT, vts()) if last else (nxt(step), vts())
        step += 1
        peer_stage(K, C, li, wb["pe_wq%d" % li], wb["pe_sk%d" % li], wb["pe_ut%d" % li], wb["pe_v%d" % li],
                   cur, cur_t, dst, dst_t, prep[4] if li == 0 else prep[li])
        cur, cur_t = dst, dst_t
    K.wait_all(K.SP, cur_t)
    return K


_PROG = None


def core_inputs(inp, c, hw, consts):
    x = np.asarray(inp["x"], np.float32)[NSEQ_CORE * c:NSEQ_CORE * (c + 1)].reshape(T_CORE, D)
    m = {"xT": np.ascontiguousarray(x.T),
         "pos": np.ascontiguousarray(np.asarray(inp["positions"], np.int32)[NSEQ_CORE * c:NSEQ_CORE * (c + 1)].reshape(1, T_CORE))}
    m.update(consts)
    m.update(hw)
    return m


def kernel(**inp):
    global _PROG
    inp = {k: np.asarray(v) for k, v in inp.items()}
    if _PROG is None:
        _PROG = build_program()
    K = _PROG
    hw = host_weights(inp)
    consts = {"pv": build_pv(inp), "ssm_vec": ssm_host_vec(inp)}
    consts.update(host_consts())
    consts.update(mla_host_consts())
    consts.update(ssm_host_consts())
    in_maps = [core_inputs(inp, c, hw, consts) for c in range(NCORES)]
    res = run_bass_kernel_spmd(K.nc, in_maps, core_ids=list(range(NCORES)))
    out = np.empty((NCORES * NSEQ_CORE, S, D), np.float32)
    for c in range(NCORES):
        o = np.asarray(res.results[c]["outT"], np.float32)
        out[NSEQ_CORE * c:NSEQ_CORE * (c + 1)] = o.T.reshape(NSEQ_CORE, S, D)
    return out
```

```python
import numpy as np
from contextlib import ExitStack
import concourse.bass as bass
import concourse.mybir as mybir
from concourse.bass_utils import run_bass_kernel_spmd

F32 = mybir.dt.float32
BF16 = mybir.dt.bfloat16
I32 = mybir.dt.int32
AF = mybir.ActivationFunctionType
ALU = mybir.AluOpType
AX = mybir.AxisListType

D = 2048
DC = 16
S = 2048
DEPTH = 4
ALPHA = (2 * DEPTH) ** 0.25
LN_EPS = 1e-5
RMS_EPS = 1e-6
TG = 512


class Tile:
    __slots__ = ("h", "w", "r", "name")

    def __init__(self, h, name=""):
        self.h = h
        self.w = {}
        self.r = {}
        self.name = name

    def __getitem__(self, idx):
        return self.h[idx]


class Eng:
    def __init__(self, K, name, e, is_pe=False):
        self.name = name
        self.e = e
        self.is_pe = is_pe
        self.sem = K.newsem("c_" + name)
        self.count = 0
        self.seen = {}


class Stream:
    def __init__(self, K, name):
        self.sem = K.newsem("d_" + name)
        self.count = 0
        self.nobar = name.startswith("cast")


class KB:
    def __init__(self):
        self.nc = bass.Bass("TRN2", target_bir_lowering=False)
        self.es = ExitStack()
        self.ses = self.es
        self.nsem = 0
        self.streams = {}
        nc = self.nc
        self.PE = Eng(self, "pe", nc.tensor, is_pe=True)
        self.ACT = Eng(self, "act", nc.scalar)
        self.DVE = Eng(self, "dve", nc.vector)
        self.POOL = Eng(self, "pool", nc.gpsimd)
        self.SP = Eng(self, "sp", nc.sync)
        self.n_ins = 0

    def newsem(self, name):
        self.nsem += 1
        return self.es.enter_context(self.nc.semaphore(name))

    def sb(self, name, shape, dt):
        self.nsb = getattr(self, "nsb", 0) + 1
        name = "%s_u%d" % (name, self.nsb)
        return Tile(self.ses.enter_context(self.nc.sbuf_tensor(name, list(shape), dt)), name)

    def stream(self, name):
        if name not in self.streams:
            self.streams[name] = Stream(self, name)
        return self.streams[name]

    def engines(self):
        return [self.PE, self.ACT, self.DVE, self.POOL, self.SP]

    def barrier(self):
        for E in self.engines():
            for Fe in self.engines():
                if Fe is E or Fe.count == 0:
                    continue
                if E.seen.get(id(Fe.sem), 0) < Fe.count:
                    E.e.wait_ge(Fe.sem, Fe.count)
                    E.seen[id(Fe.sem)] = Fe.count
            for st in self.streams.values():
                if st.nobar:
                    continue
                if st.count and E.seen.get(id(st.sem), 0) < st.count:
                    E.e.wait_ge(st.sem, st.count)
                    E.seen[id(st.sem)] = st.count

    def stage(self):
        K = self

        class _S:
            def __enter__(s):
                s.es = ExitStack()
                s.es.__enter__()
                K.ses = s.es
                return s

            def __exit__(s, *a):
                K.barrier()
                K.ses = K.es
                return s.es.__exit__(*a)
        return _S()

    def ps(self, name, shape=(128, 512), dt=F32):
        return Tile(self.es.enter_context(self.nc.psum_tensor(name, list(shape), dt)), name)

    def dram(self, name, shape, dt, kind=None):
        if kind is None:
            return self.nc.dram_tensor(name, list(shape), dt)
        return self.nc.dram_tensor(name, list(shape), dt, kind=kind)

    def vt(self, name=""):
        return Tile(None, name)

    def _waits(self, E, reads, writes, skip_sem=None):
        need = {}
        for t in reads:
            for k, v in t.w.items():
                if k not in need or need[k][1] < v[1]:
                    need[k] = v
        for t in writes:
            for dct in (t.w, t.r):
                for k, v in dct.items():
                    if dct is t.w and k == skip_sem:
                        continue
                    if k not in need or need[k][1] < v[1]:
                        need[k] = v
        for k, (sem, val, src) in need.items():
            if src is E and E.is_pe:
                continue
            if E.seen.get(k, 0) >= val:
                continue
            E.e.wait_ge(sem, val)
            E.seen[k] = val
            self.n_ins += 1

    def _record(self, tok, reads, writes):
        k = id(tok[0])
        for t in reads:
            t.r[k] = tok
        for t in writes:
            if t.h is None:
                t.w[k] = tok
            else:
                t.w = {k: tok}
                t.r = {}

    def op(self, E, fn, reads=(), writes=(), mark=True):
        self._waits(E, reads, writes)
        ins = fn()
        self.n_ins += 1
        if mark:
            E.count += 1
            ins.then_inc(E.sem, 1)
            tok = (E.sem, E.count, E)
        else:
            tok = (E.sem, E.count + 1, E)
        self._record(tok, reads, writes)
        return ins

    def dma(self, Q, st, out, in_, reads=(), writes=()):
        self._waits(Q, reads, writes, skip_sem=id(st.sem))
        ins = Q.e.dma_start(out=out, in_=in_)
        self.n_ins += 1
        st.count += 16
        ins.then_inc(st.sem, 16)
        tok = (st.sem, st.count, None)
        self._record(tok, reads, writes)
        return ins

    def retoken(self, st, tiles):
        k = id(st.sem)
        for t in tiles:
            hit = False
            for dct in (t.w, t.r):
                if k in dct:
                    dct[k] = (st.sem, st.count, None)
                    hit = True
            if not hit:
                t.w[k] = (st.sem, st.count, None)

    def wait_all(self, E, tiles):
        self._waits(E, tiles, tiles)

    def mm(self, out_t, out_ap, l_t, l_ap, r_t, r_ap, start, stop, mark=None):
        nc = self.nc
        return self.op(self.PE, lambda: nc.tensor.matmul(out_ap, l_ap, r_ap, start=start, stop=stop),
                       reads=[l_t, r_t], writes=[out_t], mark=stop if mark is None else mark)


def pv_layout():
    off = {}
    n = 0

    def add(name, cols):
        nonlocal n
        off[name] = n
        n += cols

    for i in range(DEPTH):
        for w in range(2):
            add(("ln_g", i, w), DC)
            add(("ln_b", i, w), DC)
    for j in range(2):
        for k in range(3):
            add(("sc_cw", j, k), DC)
    add(("q_norm",), 4)
    add(("kv_norm",), 4)
    for k in range(4):
        add(("ssm_cw", k), 48)
    add(("ssm_cb",), 48)
    add(("ssm_nw",), 32)
    add(("ssm_dsk",), 32)
    return off, n


PV_OFF, NPV = pv_layout()


def chunked(v):
    return np.ascontiguousarray(np.asarray(v, dtype=np.float32).reshape(-1, 128).T)


def build_pv(inp):
    pv = np.zeros((128, NPV), np.float32)

    def put(key, v):
        c = chunked(v)
        pv[:, PV_OFF[key]:PV_OFF[key] + c.shape[1]] = c

    for i in range(DEPTH):
        for w in range(2):
            put(("ln_g", i, w), inp["ln_g"][i, w])
            put(("ln_b", i, w), inp["ln_b"][i, w])
    for j in range(2):
        for k in range(3):
            put(("sc_cw", j, k), inp["sc_conv_w"][j, k])
    put(("q_norm",), inp["mla_q_norm"][0])
    put(("kv_norm",), inp["mla_kv_norm"][0])
    for k in range(4):
        put(("ssm_cw", k), inp["ssm_conv_w"][0, k])
    put(("ssm_cb",), inp["ssm_conv_b"][0])
    put(("ssm_nw",), inp["ssm_norm_w"][0])
    put(("ssm_dsk",), np.repeat(np.asarray(inp["ssm_d"][0], np.float32), 64))
    return pv


def lay_kmajor(w):
    K, N = w.shape
    return np.ascontiguousarray(w.reshape(K // 128, 128, N).transpose(1, 0, 2).reshape(128, -1))


def lay_conv_in(w):
    a = w.reshape(16, 128, 3, 16, 128)
    return np.ascontiguousarray(a.transpose(3, 1, 0, 2, 4).reshape(16, 128, 16 * 3 * 128))


class Ctx:
    pass


def setup_common(K, T):
    nc = K.nc
    C = Ctx()
    C.T = T
    C.pv_d = K.dram("pv", [128, NPV], F32, kind="ExternalInput")
    C.pv = K.sb("pv_sb", [128, NPV], F32)
    C.st_misc = K.stream("misc")
    K.dma(K.SP, C.st_misc, C.pv[:, :], C.pv_d[:, :], writes=[C.pv])
    C.const_tiles = [C.pv]
    C.ones_f = K.sb("ones_f", [128, 128], F32)
    K.op(K.DVE, lambda: nc.vector.memset(C.ones_f[:, :], 1.0), writes=[C.ones_f])
    C.ones_b = K.sb("ones_b", [128, 128], BF16)
    K.op(K.DVE, lambda: nc.vector.memset(C.ones_b[:, :], 1.0), writes=[C.ones_b])
    C.psum = [K.ps("ps%d" % i) for i in range(8)]
    C.ln_sq = [K.sb("ln_sq%d" % i, [128, TG], F32) for i in range(1)] * 2
    C.ln_mean = K.sb("ln_mean", [128, TG], F32)
    C.ln_m2 = K.sb("ln_m2", [128, TG], F32)
    C.ln_rstd = K.sb("ln_rstd", [128, TG], F32)
    C.ln_t = [K.sb("ln_t%d" % i, [128, TG], F32) for i in range(2)]
    return C


def pvcol(C, key, c):
    o = PV_OFF[key] + c
    return C.pv[:, o:o + 1]


def cast_dram(K, st, dst, src, rows, cols, tiles_w, tiles_r=(), maxc=16384):
    maxc = min(maxc, 4096)
    for r0 in range(0, rows, 128):
        r1 = min(rows, r0 + 128)
        for c0 in range(0, cols, maxc):
            c1 = min(cols, c0 + maxc)
            K.dma(K.POOL, st, dst[r0:r1, c0:c1], src[r0:r1, c0:c1], reads=list(tiles_r), writes=list(tiles_w))


def layer_norm_inplace(K, C, xf, li, which, ps_a, ps_b):
    nc = K.nc
    for c in range(DC):
        K.mm(ps_a, ps_a[:, :], C.ones_f, C.ones_f[:, :], xf, xf[:, c, :], c == 0, c == DC - 1)
    for c in range(DC):
        sq = C.ln_sq[c % 2]
        K.op(K.ACT, lambda sq=sq, c=c: nc.scalar.activation(sq[:, :], xf[:, c, :], AF.Square),
             reads=[xf], writes=[sq])
        K.mm(ps_b, ps_b[:, :], C.ones_f, C.ones_f[:, :], sq, sq[:, :], c == 0, c == DC - 1, mark=True)
    K.op(K.DVE, lambda: nc.vector.tensor_scalar(C.ln_mean[:, :], ps_a[:, :], 1.0 / D, None, ALU.mult),
         reads=[ps_a], writes=[C.ln_mean])
    K.op(K.DVE, lambda: nc.vector.tensor_tensor(C.ln_m2[:, :], C.ln_mean[:, :], C.ln_mean[:, :], ALU.mult),
         reads=[C.ln_mean], writes=[C.ln_m2])
    K.op(K.DVE, lambda: nc.vector.scalar_tensor_tensor(C.ln_m2[:, :], ps_b[:, :], 1.0 / D, C.ln_m2[:, :],
                                                       ALU.mult, ALU.subtract),
         reads=[ps_b, C.ln_m2], writes=[C.ln_m2])
    K.op(K.DVE, lambda: nc.vector.tensor_scalar(C.ln_m2[:, :], C.ln_m2[:, :], LN_EPS, None, ALU.add),
         reads=[C.ln_m2], writes=[C.ln_m2])
    K.op(K.ACT, lambda: nc.scalar.activation(C.ln_m2[:, :], C.ln_m2[:, :], AF.Sqrt),
         reads=[C.ln_m2], writes=[C.ln_m2])
    K.op(K.DVE, lambda: nc.vector.reciprocal(C.ln_rstd[:, :], C.ln_m2[:, :]),
         reads=[C.ln_m2], writes=[C.ln_rstd])
    for c in range(DC):
        t = C.ln_t[c % 2]
        K.op(K.DVE, lambda t=t, c=c: nc.vector.tensor_tensor(t[:, :], xf[:, c, :], C.ln_mean[:, :], ALU.subtract),
             reads=[xf, C.ln_mean], writes=[t])
        K.op(K.DVE, lambda t=t: nc.vector.tensor_tensor(t[:, :], t[:, :], C.ln_rstd[:, :], ALU.mult),
             reads=[t, C.ln_rstd], writes=[t])
        K.op(K.ACT, lambda t=t, c=c: nc.scalar.activation(
            xf[:, c, :], t[:, :], AF.Identity,
            bias=pvcol(C, ("ln_b", li, which), c), scale=pvcol(C, ("ln_g", li, which), c)),
            reads=[t, C.pv], writes=[xf])


def conv_stage(K, C, li, j, w_in_bf, w_out_bf, xin, xin_t, xout, xout_t, prep_t):
    nc = K.nc
    T = C.T
    NG = T // TG
    with K.stage():
        wo = K.sb("cv_wo", [128, DC, D], BF16)
        xb = K.sb("cv_xb", [128, DC, TG], BF16)
        xf = K.sb("cv_xf", [128, DC, TG], F32)
        yb = K.sb("cv_yb", [128, DC, TG], BF16)
        w3 = [K.sb("cv_w3_%d" % i, [128, DC * 3 * 128], BF16) for i in range(3)]
        halo = K.sb("cv_halo", [128, DC, 2], F32)
        ub = [K.sb("cv_ub%d" % i, [128, TG + 2], F32) for i in range(2)]
        cS = [K.sb("cv_cS%d" % i, [128, TG], F32) for i in range(2)]
        vv = [K.sb("cv_v%d" % i, [128, TG], F32) for i in range(2)]
        st_wo = K.stream("wres")
        st_w3 = [K.stream("wring%d" % i) for i in range(3)]
        st_xb = K.stream("xb")
        st_xf = K.stream("xf")
        wov = w_out_bf[:, :].rearrange("p (k n) -> p k n", k=DC)
        for q in range(4):
            K.dma(K.SP, st_wo, wo[:, 4 * q:4 * q + 4, :], wov[:, 4 * q:4 * q + 4, :], reads=[prep_t], writes=[wo])
        xin_v = xin[:, :].rearrange("(c p) t -> p c t", p=128)
        xout_v = xout[:, :].rearrange("(c p) t -> p c t", p=128)
        P = C.psum
        wl = 0
        for g in range(NG):
            tsl = slice(g * TG, (g + 1) * TG)
            for q in range(4):
                K.dma(K.POOL, st_xb, xb[:, 4 * q:4 * q + 4, :], xin_v[:, 4 * q:4 * q + 4, tsl], reads=[xin_t[g]], writes=[xb])
                K.dma(K.SP, st_xf, xf[:, 4 * q:4 * q + 4, :], xin_v[:, 4 * q:4 * q + 4, tsl], reads=[xin_t[g]], writes=[xf])
            if (g * TG) % S == 0:
                K.op(K.DVE, lambda: nc.vector.memset(halo[:, :, :], 0.0), writes=[halo])
            for jc in range(DC):
                w = w3[wl % 3]
                K.dma(K.SP, st_w3[wl % 3], w[:, :], w_in_bf[jc, :, :], reads=[prep_t], writes=[w])
                wl += 1
                pB, pC, ph = P[(jc % 2) * 3 + 0], P[(jc % 2) * 3 + 1], P[(jc % 2) * 3 + 2]
                for s, pp in enumerate((pB, pC, ph)):
                    for kc in range(DC):
                        o = (kc * 3 + s) * 128
                        K.mm(pp, pp[:, :], w, w[:, o:o + 128], xb, xb[:, kc, :], kc == 0, kc == DC - 1)
                u = ub[jc % 2]
                c_s = cS[jc % 2]
                v = vv[jc % 2]
                K.op(K.ACT, lambda c_s=c_s, pC=pC: nc.scalar.copy(c_s[:, :], pC[:, :]), reads=[pC], writes=[c_s])
                K.op(K.DVE, lambda u=u, jc=jc: nc.vector.tensor_copy(u[:, 0:2], halo[:, jc, :]), reads=[halo], writes=[u])
                K.op(K.DVE, lambda u=u, c_s=c_s, ph=ph: nc.vector.tensor_tensor(u[:, 2:TG + 2], c_s[:, :], ph[:, :], ALU.mult),
                     reads=[c_s, ph], writes=[u])
                K.op(K.DVE, lambda u=u, jc=jc: nc.vector.tensor_copy(halo[:, jc, :], u[:, TG:TG + 2]), reads=[u], writes=[halo])
                K.op(K.DVE, lambda u=u, v=v, jc=jc: nc.vector.tensor_scalar(
                    v[:, :], u[:, 0:TG], pvcol(C, ("sc_cw", j, 0), jc), None, ALU.mult), reads=[u, C.pv], writes=[v])
                for k in (1, 2):
                    K.op(K.DVE, lambda u=u, v=v, jc=jc, k=k: nc.vector.scalar_tensor_tensor(
                        v[:, :], u[:, k:TG + k], pvcol(C, ("sc_cw", j, k), jc), v[:, :], ALU.mult, ALU.add),
                        reads=[u, v, C.pv], writes=[v])
                K.op(K.DVE, lambda v=v, pB=pB, jc=jc: nc.vector.tensor_tensor(yb[:, jc, :], pB[:, :], v[:, :], ALU.mult),
                     reads=[pB, v], writes=[yb])
            for n in range(DC):
                po = P[6 + n % 2]
                for kc in range(DC):
                    K.mm(po, po[:, :], wo, wo[:, kc, n * 128:(n + 1) * 128], yb, yb[:, kc, :], kc == 0, kc == DC - 1)
                K.op(K.DVE, lambda po=po, n=n: nc.vector.scalar_tensor_tensor(
                    xf[:, n, :], xf[:, n, :], ALPHA, po[:, :], ALU.mult, ALU.add), reads=[xf, po], writes=[xf])
            layer_norm_inplace(K, C, xf, li, 0, P[6], P[7])
            for q in range(4):
                K.dma(K.ACT, st_xf, xout_v[:, 4 * q:4 * q + 4, tsl], xf[:, 4 * q:4 * q + 4, :], reads=[xf], writes=[xout_t[g]])


NH = 8
SBK = 4
NEG = -1.0e30
THETA_MARGIN = 2.0e-4
ACC_LAG = 1


def host_consts():
    ident = np.eye(128, dtype=np.float32)
    selc = np.zeros((16, NH, 128), np.float32)
    selz = np.zeros((8, NH, 128), np.float32)
    for h in range(NH):
        selc[h, h, :] = -1.0
        selc[8 + h, h, :] = -1.0
        selz[h, h, :] = 1.0
    return {"c_ident": ident}


def lay_ut(u):
    a = u.reshape(128, 128, 16, 128)
    return np.ascontiguousarray(a.transpose(0, 3, 2, 1).reshape(128 * 128, 2048))


def lay_wq(w):
    a = w.reshape(16, 128, 16, 128)
    return np.ascontiguousarray(a.transpose(2, 1, 0, 3).reshape(16 * 128, 2048))


def lay_sk(sk):
    return np.ascontiguousarray(sk.transpose(3, 0, 1, 2).reshape(128, NH * 2 * 128))


def setup_peer_consts(K, C):
    nc = K.nc
    C.c_ident_d = K.dram("c_ident", [128, 128], F32, kind="ExternalInput")
    C.ident_f = K.sb("ident_f", [128, 128], F32)
    C.ident_b = K.sb("ident_b", [128, 128], BF16)
    stp = K.stream("misc_p")
    K.dma(K.SP, stp, C.ident_f[:, :], C.c_ident_d[:, :], writes=[C.ident_f])
    K.op(K.DVE, lambda: nc.vector.tensor_copy(C.ident_b[:, :], C.ident_f[:, :]), reads=[C.ident_f], writes=[C.ident_b])


def peer_stage(K, C, li, wq_b, sk_b, ut_b, v_b, xin, xin_t, xout, xout_t, prep_t):
    nc = K.nc
    T = C.T
    NG = T // TG
    P = C.psum
    NTT = TG // 128
    NBLK = 128 // SBK
    with K.stage():
        xb = K.sb("pr_xb", [128, DC, TG], BF16)
        xacc = K.sb("pr_xacc", [128, DC, TG], F32)
        skT = K.sb("pr_skT", [128, NH * 2 * 128], BF16)
        wq = [K.sb("pr_wq%d" % i, [128, 2048], BF16) for i in range(2)]
        qr = [K.sb("pr_qr%d" % i, [128, TG], BF16) for i in range(4)]
        sc = K.sb("pr_sc", [128, NTT, NH, 256], F32)
        m16 = K.sb("pr_m16", [128, NH * 2 * 16], F32)
        c16 = K.sb("pr_c16", [128, NH * 16], F32)
        m16s = [K.vt() for _ in range(NH)]
        c16s = [K.vt() for _ in range(NH)]
        d16 = K.sb("pr_d16", [128, NH * 16], F32)
        zs = K.sb("pr_zs", [128, NH], F32)
        thp = K.sb("pr_thp", [128, NTT, NH], F32)
        nlz = K.sb("pr_nlz", [128, NTT, NH], F32)
        Ut = [K.sb("pr_U%d" % i, [128, 2048], BF16) for i in range(SBK)]
        Vt = [K.sb("pr_V%d" % i, [128, 2048], BF16) for i in range(2 * SBK)]
        WT = [K.sb("pr_W%d" % i, [128, TG], BF16) for i in range(2 * SBK)]
        u_t = [K.sb("pr_u%d" % i, [128, TG], F32) for i in range(3)]
        e_t = [K.sb("pr_e%d" % i, [128, TG], BF16) for i in range(4)]
        g_t = [K.sb("pr_g%d" % i, [128, TG], BF16) for i in range(6)]
        at_t = [K.sb("pr_at%d" % i, [128, TG], BF16) for i in range(2)]
        Gsb = [K.sb("pr_G%d" % i, [128, SBK, TG], BF16) for i in range(2)]
        gel = [K.sb("pr_gel%d" % i, [128, TG], BF16) for i in range(SBK)]
        gel2 = [K.sb("pr_gl2%d" % i, [128, TG], BF16) for i in range(2)]
        st_x = K.stream("xf")
        st_sk = K.stream("wres")
        st_wq = [K.stream("wring%d" % i) for i in range(2)]
        st_u = [K.stream("uring%d" % i) for i in range(SBK)]
        st_v = [K.stream("vring%d" % i) for i in range(2 * SBK)]
        K.dma(K.SP, st_sk, skT[:, :], sk_b[:, :], reads=[prep_t], writes=[skT])
        xin_v = xin[:, :].rearrange("(c p) t -> p c t", p=128)
        xout_v = xout[:, :].rearrange("(c p) t -> p c t", p=128)
        skv = lambda h, s: skT[:, (h * 2 + s) * 128:(h * 2 + s + 1) * 128]
        accT = [P[0], P[1]]
        pGs = [P[2], P[3]]
        pHs = [P[4], P[5]]
        pVs = [P[6], P[7]]
        wl = 0
        ql = 0
        ui = 0
        vl = 0

        for g in range(NG):
            tsl = slice(g * TG, (g + 1) * TG)
            for q in range(4):
                K.dma(K.SP, st_x, xacc[:, 4 * q:4 * q + 4, :], xin_v[:, 4 * q:4 * q + 4, tsl], reads=[xin_t[g]], writes=[xacc])
            K.op(K.ACT, lambda: nc.scalar.copy(xb[:, :, :], xacc[:, :, :]), reads=[xacc], writes=[xb])
            K.op(K.DVE, lambda: nc.vector.tensor_scalar(xacc[:, :, :], xacc[:, :, :], ALPHA, None, ALU.mult),
                 reads=[xacc], writes=[xacc])
            for h in range(NH):
                qs = []
                for s_ in range(2):
                    n = 2 * h + s_
                    w = wq[wl % 2]
                    K.dma(K.SP, st_wq[wl % 2], w[:, :], wq_b[n * 128:(n + 1) * 128, :], reads=[prep_t], writes=[w])
                    wl += 1
                    pq = P[4 + n % 2]
                    for kc in range(DC):
                        K.mm(pq, pq[:, :], w, w[:, kc * 128:(kc + 1) * 128], xb, xb[:, kc, :], kc == 0, kc == DC - 1)
                    qq = qr[ql % 4]
                    ql += 1
                    K.op(K.ACT, lambda pq=pq, qq=qq: nc.scalar.copy(qq[:, :], pq[:, :]), reads=[pq], writes=[qq])
                    qs.append(qq)
                for half in range(2):
                    ps = P[6 + half]
                    for t2 in range(2):
                        tt = half * 2 + t2
                        for s_ in range(2):
                            o = (t2 * 2 + s_) * 128
                            K.mm(ps, ps[:, o:o + 128], qs[s_], qs[s_][:, tt * 128:(tt + 1) * 128], skT, skv(h, s_), True, True)
                    K.op(K.ACT, lambda ps=ps, half=half, h=h: nc.scalar.copy(
                        sc[:, 2 * half:2 * half + 2, h, :], ps[:, :].rearrange("p (t k) -> p t k", t=2)), reads=[ps], writes=[sc])
            tmps = [WT[4 + i][:, :].bitcast(F32)[:, 0:128] for i in range(4)]
            tmp_t = [WT[4 + i] for i in range(4)]
            cands = [u_t[i][:, 0:256] for i in range(3)] + [at_t[0][:, :].bitcast(F32)]
            cand_t = [u_t[0], u_t[1], u_t[2], at_t[0]]
            ctmps = [WT[i][:, :].bitcast(F32) for i in range(4)]
            ctmp_t = [WT[i] for i in range(4)]
            for tt in range(NTT):
                for hq4 in range(NH // 4):
                    hs = [hq4 * 4 + i for i in range(4)]
                    for s_ in range(2):
                        for i, h in enumerate(hs):
                            src = sc[:, tt, h, s_ * 128:(s_ + 1) * 128]
                            mo = (h * 2 + s_) * 16
                            K.op(K.DVE, lambda src=src, mo=mo: nc.vector.max(out=m16[:, mo:mo + 8], in_=src), reads=[sc], writes=[m16s[h]])
                        for i, h in enumerate(hs):
                            src = sc[:, tt, h, s_ * 128:(s_ + 1) * 128]
                            mo = (h * 2 + s_) * 16
                            K.op(K.DVE, lambda src=src, mo=mo, i=i: nc.vector.match_replace(
                                out=tmps[i], in_to_replace=m16[:, mo:mo + 8], in_values=src, imm_value=NEG),
                                reads=[sc, m16s[h]], writes=[tmp_t[i]])
                        for i, h in enumerate(hs):
                            mo = (h * 2 + s_) * 16
                            K.op(K.DVE, lambda mo=mo, i=i: nc.vector.max(out=m16[:, mo + 8:mo + 16], in_=tmps[i]), reads=[tmp_t[i]], writes=[m16s[h]])
                    for i, h in enumerate(hs):
                        v1 = m16[:, (h * 2) * 16:(h * 2) * 16 + 16]
                        v2 = m16[:, (h * 2 + 1) * 16:(h * 2 + 1) * 16 + 16]
                        K.op(K.DVE, lambda v1=v1, v2=v2, i=i: nc.vector.tensor_tensor(
                            cands[i].rearrange("p (i j) -> p i j", i=16),
                            v1.unsqueeze(2).to_broadcast([128, 16, 16]),
                            v2.unsqueeze(1).to_broadcast([128, 16, 16]), ALU.add), reads=[m16s[h]], writes=[cand_t[i]])
                    for i, h in enumerate(hs):
                        co = h * 16
                        K.op(K.DVE, lambda co=co, i=i: nc.vector.max(out=c16[:, co:co + 8], in_=cands[i]), reads=[cand_t[i]], writes=[c16s[h]])
                    for i, h in enumerate(hs):
                        co = h * 16
                        K.op(K.DVE, lambda co=co, i=i: nc.vector.match_replace(
                            out=ctmps[i], in_to_replace=c16[:, co:co + 8], in_values=cands[i], imm_value=NEG),
                            reads=[cand_t[i], c16s[h]], writes=[ctmp_t[i]])
                    for i, h in enumerate(hs):
                        co = h * 16
                        K.op(K.DVE, lambda co=co, i=i: nc.vector.max(out=c16[:, co + 8:co + 16], in_=ctmps[i]), reads=[ctmp_t[i]], writes=[c16s[h]])
                c16_all = c16s
                c16v = c16[:, :].rearrange("p (h k) -> p h k", h=NH)
                d16v = d16[:, :].rearrange("p (h k) -> p h k", h=NH)
                K.op(K.DVE, lambda tt=tt: nc.vector.tensor_scalar(thp[:, tt, :], c16v[:, :, 15], -THETA_MARGIN, None, ALU.add),
                     reads=c16s, writes=[thp])
                K.op(K.DVE, lambda tt=tt: nc.vector.tensor_tensor(d16v, c16v, thp[:, tt, :].unsqueeze(2).to_broadcast([128, NH, 16]), ALU.subtract),
                     reads=c16s + [thp], writes=[d16])
                K.op(K.ACT, lambda: nc.scalar.activation(d16[:, :], d16[:, :], AF.Exp), reads=[d16], writes=[d16])
                K.op(K.DVE, lambda: nc.vector.reduce_sum(zs[:, :], d16v, axis=AX.X), reads=[d16], writes=[zs])
                K.op(K.ACT, lambda: nc.scalar.activation(zs[:, :], zs[:, :], AF.Ln), reads=[zs], writes=[zs])
                K.op(K.DVE, lambda tt=tt: nc.vector.tensor_tensor(zs[:, :], zs[:, :], thp[:, tt, :], ALU.add), reads=[zs, thp], writes=[zs])
                K.op(K.DVE, lambda tt=tt: nc.vector.tensor_scalar(nlz[:, tt, :], zs[:, :], -1.0, None, ALU.mult), reads=[zs], writes=[nlz])

            dq = []
            clock = [0]

            import os as _os
            _DM = _os.environ.get("DEFER_MODE", "")

            def defer(n, fn, kind="hv"):
                if _DM == "none" or (_DM == "tr" and kind != "tr") or (_DM == "hv" and kind == "tr") or (_DM == "h" and kind != "h") or (_DM == "v" and kind != "v"):
                    fn()
                    return
                dq.append([clock[0] + n, fn])

            def run_due():
                k = 0
                while k < len(dq):
                    if dq[k][0] <= clock[0]:
                        dq.pop(k)[1]()
                    else:
                        k += 1

            def tick():
                clock[0] += 1

            def drain():
                while dq:
                    dq.sort(key=lambda x: x[0])
                    clock[0] = max(clock[0], dq[0][0])
                    dq.pop(0)[1]()

            def h_items(blk, Ublk):
                items = []
                for j in range(SBK):
                    U = Ublk[j]
                    pH = pHs[j % 2]
                    ge = gel[j]
                    for kc in range(DC):
                        def f(U=U, pH=pH, kc=kc, ge=ge):
                            K.mm(pH, pH[:, :], U, U[:, kc * 128:(kc + 1) * 128], xb, xb[:, kc, :], kc == 0, kc == DC - 1)
                            if kc == DC - 1:
                                defer(1, lambda: K.op(K.ACT, lambda: nc.scalar.copy(ge[:, :], pH[:, :]), reads=[pH], writes=[ge]), "h")
                        items.append(f)
                return items

            def v_items(blk, vbase):
                items = []
                for dc in range(DC):
                    pV = pVs[dc % 2]
                    for ei in range(SBK):
                        Ve = Vt[(vbase + ei) % (2 * SBK)]
                        We = WT[(blk * SBK + ei) % (2 * SBK)]

                        def f(pV=pV, Ve=Ve, We=We, dc=dc, ei=ei):
                            K.mm(pV, pV[:, :], Ve, Ve[:, dc * 128:(dc + 1) * 128], We, We[:, :], ei == 0, ei == SBK - 1)
                            if ei == SBK - 1:
                                defer(1, lambda: K.op(K.DVE, lambda: nc.vector.tensor_tensor(xacc[:, dc, :], xacc[:, dc, :], pV[:, :], ALU.add),
                                                      reads=[xacc, pV], writes=[xacc]), "v")
                        items.append(f)
                return items

            prev_v = []
            for blk in range(NBLK):
                a0 = blk * SBK
                Gs = Gsb[blk % 2]
                vbase = vl
                Ublk = []
                for j in range(SBK):
                    U = Ut[j]
                    K.dma(K.SP, st_u[j], U[:, :], ut_b[(a0 + j) * 128:(a0 + j + 1) * 128, :], reads=[prep_t], writes=[U])
                    Ublk.append(U)
                    V = Vt[vl % (2 * SBK)]
                    K.dma(K.SP, st_v[vl % (2 * SBK)], V[:, :], v_b[(a0 + j) * 128:(a0 + j + 1) * 128, :], reads=[prep_t], writes=[V])
                    vl += 1
                hq = h_items(blk, Ublk)
                mainq = []
                while prev_v or hq:
                    for _ in range(2):
                        if prev_v:
                            mainq.append(prev_v.pop(0))
                    for _ in range(2):
                        if hq:
                            mainq.append(hq.pop(0))
                per_unit = -(-len(mainq) // (NTT * NH))
                pend_acc = []

                def tr_steps(tt, Gs=Gs):
                    pG = pGs[tt % 2]
                    at = at_t[tt % 2]
                    aT = accT[tt % 2]

                    def s1():
                        K.op(K.ACT, lambda: nc.scalar.copy(at[:, :], aT[:, :]), reads=[aT], writes=[at])
                        defer(2, s2, 'tr')

                    def s2():
                        for j in range(SBK):
                            K.mm(pG, pG[:, j * 128:(j + 1) * 128], at, at[:, j * 128:(j + 1) * 128], C.ident_b, C.ident_b[:, :],
                                 j == 0, j == SBK - 1, mark=(j == SBK - 1))
                        defer(2, s3, 'tr')

                    def s3():
                        K.op(K.ACT, lambda: nc.scalar.copy(
                            Gs[:, :, tt * 128:(tt + 1) * 128], pG[:, :].rearrange("p (j t) -> p j t", j=SBK)), reads=[pG], writes=[Gs])
                    return s1

                for tt in range(NTT):
                    for h in range(NH):
                        uu = u_t[ui % 3]
                        e = e_t[ui % 4]
                        gg = g_t[ui % 6]
                        ui += 1
                        aT = accT[tt % 2]
                        K.op(K.POOL, lambda uu=uu, tt=tt, h=h: nc.gpsimd.tensor_tensor(
                            uu[:, :].rearrange("p (j b) -> p j b", j=SBK),
                            sc[:, tt, h, 128:256].unsqueeze(1).to_broadcast([128, SBK, 128]),
                            sc[:, tt, h, a0:a0 + SBK].unsqueeze(2).to_broadcast([128, SBK, 128]), ALU.add), reads=[sc], writes=[uu])
                        K.op(K.ACT, lambda e=e, uu=uu, tt=tt, h=h: nc.scalar.activation(e[:, :], uu[:, :], AF.Exp, bias=nlz[:, tt, h:h + 1]),
                             reads=[uu, nlz], writes=[e])
                        K.op(K.DVE, lambda gg=gg, e=e, uu=uu, tt=tt, h=h: nc.vector.scalar_tensor_tensor(
                            gg[:, :], uu[:, :], thp[:, tt, h:h + 1], e[:, :], ALU.is_ge, ALU.mult), reads=[uu, e, thp], writes=[gg])
                        run_due()

                        def acc_fn(aT=aT, gg=gg, h=h, tt=tt):
                            K.mm(aT, aT[:, :], C.ident_b, C.ident_b[:, :], gg, gg[:, :], h == 0, h == NH - 1, mark=True)
                            if h == NH - 1:
                                defer(2, tr_steps(tt), 'tr')
                        if ACC_LAG:
                            if pend_acc:
                                pend_acc.pop(0)()
                            pend_acc.append(acc_fn)
                        else:
                            acc_fn()
                        for _ in range(per_unit):
                            if mainq:
                                mainq.pop(0)()
                        tick()
                while pend_acc:
                    pend_acc.pop(0)()
                k_ = 0
                while mainq:
                    mainq.pop(0)()
                    k_ += 1
                    if k_ % SBK == 0:
                        tick()
                        run_due()
                drain()
                for j in range(SBK):
                    W = WT[(a0 + j) % (2 * SBK)]
                    g2 = gel2[j % 2]
                    K.op(K.ACT, lambda g2=g2, j=j: nc.scalar.activation(g2[:, :], gel[j][:, :], AF.Gelu), reads=[gel[j]], writes=[g2])
                    K.op(K.DVE, lambda W=W, g2=g2, j=j: nc.vector.tensor_tensor(W[:, :], g2[:, :], Gs[:, j, :], ALU.mult),
                         reads=[g2, Gs], writes=[W])
                prev_v = v_items(blk, vbase)
            for idx, f in enumerate(prev_v):
                f()
                if idx % SBK == SBK - 1:
                    tick()
                    run_due()
            drain()
            layer_norm_inplace(K, C, xacc, li, 1, P[0], P[1])
            for q in range(4):
                K.dma(K.ACT, st_x, xout_v[:, 4 * q:4 * q + 4, tsl], xacc[:, 4 * q:4 * q + 4, :], reads=[xacc], writes=[xout_t[g]])


MH = 16
SCALE = 1.0 / (192.0 ** 0.5)
TWO_PI = 2.0 * np.pi


def mla_host_consts():
    pm = np.zeros((64, 64), np.float32)
    for j in range(32):
        pm[j + 32, j] = -1.0
        pm[j, j + 32] = 1.0
    inv = (1.0 / (10000.0 ** (np.arange(0, 64, 2, dtype=np.float32) / 64.0))).astype(np.float32)
    invf = np.concatenate([inv, inv]).reshape(64, 1).astype(np.float32)
    cm = np.zeros((128, 4, 512), np.float32)
    k = np.arange(128)[:, None]
    q = np.arange(512)[None, :]
    for j in range(4):
        cm[:, j, :] = (q >= 128 * j + k).astype(np.float32)
    return {"c_pm": pm, "c_invf": invf, "c_cmask": cm.reshape(128, 2048)}


def setup_mla_consts(K, C):
    nc = K.nc
    C.c_pm_d = K.dram("c_pm", [64, 64], F32, kind="ExternalInput")
    C.c_invf_d = K.dram("c_invf", [64, 1], F32, kind="ExternalInput")
    C.c_cmask_d = K.dram("c_cmask", [128, 2048], F32, kind="ExternalInput")
    C.pm = K.sb("pm_f", [64, 64], F32)
    C.invf = K.sb("invf", [64, 1], F32)
    stp = K.stream("misc_m")
    K.dma(K.SP, stp, C.pm[:, :], C.c_pm_d[:, :], writes=[C.pm])
    K.dma(K.SP, stp, C.invf[:, :], C.c_invf_d[:, :], writes=[C.invf])
    K.retoken(stp, [C.pm, C.invf])


def rope_tables(K, nc, C, pos_i, cs, scr):
    posf, t, ti, tf = scr
    K.op(K.DVE, lambda: nc.vector.tensor_copy(posf[:, :], pos_i[:, :]), reads=[pos_i], writes=[posf])
    for which, shift in ((1, 0.0), (0, 0.25)):
        out = cs[which]
        K.op(K.DVE, lambda shift=shift: nc.vector.tensor_scalar(t[:, :], posf[:, :], C.invf[:, 0:1], 1.0 / TWO_PI, ALU.mult, ALU.mult),
             reads=[posf, C.invf], writes=[t])
        if shift:
            K.op(K.DVE, lambda shift=shift: nc.vector.tensor_scalar(t[:, :], t[:, :], shift, None, ALU.add), reads=[t], writes=[t])
        K.op(K.DVE, lambda: nc.vector.tensor_copy(ti[:, :], t[:, :]), reads=[t], writes=[ti])
        K.op(K.DVE, lambda: nc.vector.tensor_copy(tf[:, :], ti[:, :]), reads=[ti], writes=[tf])
        K.op(K.DVE, lambda: nc.vector.tensor_tensor(t[:, :], t[:, :], tf[:, :], ALU.subtract), reads=[t, tf], writes=[t])
        K.op(K.DVE, lambda: nc.vector.scalar_tensor_tensor(tf[:, :], t[:, :], 0.5, t[:, :], ALU.is_gt, ALU.subtract),
             reads=[t], writes=[tf])
        K.op(K.DVE, lambda: nc.vector.scalar_tensor_tensor(t[:, :], t[:, :], -0.5, tf[:, :], ALU.is_lt, ALU.subtract),
             reads=[t, tf], writes=[t])
        K.op(K.ACT, lambda out=out: nc.scalar.activation(out[:, :], t[:, :], AF.Sin, scale=TWO_PI), reads=[t], writes=[out])


def load_x_group(K, nc, st, xacc, xb, xin_v, tsl, xin_tile):
    for q in range(4):
        K.dma(K.SP, st, xacc[:, 4 * q:4 * q + 4, :], xin_v[:, 4 * q:4 * q + 4, tsl], reads=[xin_tile], writes=[xacc])
    if xb is not None:
        K.op(K.ACT, lambda: nc.scalar.copy(xb[:, :, :], xacc[:, :, :]), reads=[xacc], writes=[xb])


def mla_stage_a(K, C, win_b, pos_d, xin, xin_t, cq_d, ckv_d, kr_d, cs_d, mid_t, prep_t):
    nc = K.nc
    T = C.T
    NG = T // TG
    P = C.psum
    with K.stage():
        win = K.sb("ma_win", [128, DC, 1088], BF16)
        xb = K.sb("ma_xb", [128, DC, TG], BF16)
        xacc = K.sb("ma_xacc", [128, DC, TG], F32)
        pj = K.sb("ma_pj", [128, 9, TG], F32)
        sq = [K.sb("ma_sq%d" % i, [128, TG], F32) for i in range(2)]
        rstd = K.sb("ma_rstd", [128, TG], F32)
        tmpn = [K.sb("ma_tn%d" % i, [128, TG], F32) for i in range(2)]
        stg = K.sb("ma_stg", [128, 4, TG], BF16)
        pos_i = K.sb("ma_pos", [64, TG], I32)
        scr = (K.sb("ma_posf", [64, TG], F32), K.sb("ma_t", [64, TG], F32), K.sb("ma_ti", [64, TG], I32), K.sb("ma_tf", [64, TG], F32))
        cs = [K.sb("ma_cos", [64, TG], F32), K.sb("ma_sin", [64, TG], F32)]
        ra = K.sb("ma_ra", [64, TG], F32)
        rb = K.sb("ma_rb", [64, TG], F32)
        krb = K.sb("ma_krb", [64, TG], BF16)
        st_w = K.stream("wres")
        st_x = K.stream("xf")
        st_p = K.stream("pos")
        st_o = K.stream("stg")
        st_o2 = K.stream("stg2")
        winv = win_b[:, :].rearrange("p (k n) -> p k n", k=DC)
        for q in range(4):
            K.dma(K.SP, st_w, win[:, 4 * q:4 * q + 4, :], winv[:, 4 * q:4 * q + 4, :], reads=[prep_t], writes=[win])
        xin_v = xin[:, :].rearrange("(c p) t -> p c t", p=128)
        for g in range(NG):
            tsl = slice(g * TG, (g + 1) * TG)
            load_x_group(K, nc, st_x, xacc, xb, xin_v, tsl, xin_t[g])
            K.dma(K.SP, st_p, pos_i[:, :], pos_d[0:1, tsl].to_broadcast([64, TG]), writes=[pos_i])
            for c in range(9):
                M = 128 if c < 8 else 64
                pp = P[c % 2]
                for kc in range(DC):
                    K.mm(pp, pp[0:M, :], win, win[:, kc, c * 128:c * 128 + M], xb, xb[:, kc, :], kc == 0, kc == DC - 1)
                K.op(K.ACT, lambda pp=pp, c=c, M=M: nc.scalar.copy(pj[0:M, c, :], pp[0:M, :]), reads=[pp], writes=[pj])
            for which, (key, dst) in enumerate(((("q_norm",), cq_d), (("kv_norm",), ckv_d))):
                ps = P[2 + which]
                for c in range(4):
                    s_ = sq[c % 2]
                    K.op(K.ACT, lambda s_=s_, c=c, which=which: nc.scalar.activation(s_[:, :], pj[:, which * 4 + c, :], AF.Square),
                         reads=[pj], writes=[s_])
                    K.mm(ps, ps[:, :], C.ones_f, C.ones_f[:, :], s_, s_[:, :], c == 0, c == 3, mark=True)
                K.op(K.DVE, lambda ps=ps: nc.vector.tensor_scalar(rstd[:, :], ps[:, :], 1.0 / 512.0, RMS_EPS, ALU.mult, ALU.add),
                     reads=[ps], writes=[rstd])
                K.op(K.ACT, lambda: nc.scalar.activation(rstd[:, :], rstd[:, :], AF.Sqrt), reads=[rstd], writes=[rstd])
                K.op(K.DVE, lambda: nc.vector.reciprocal(rstd[:, :], rstd[:, :]), reads=[rstd], writes=[rstd])
                for c in range(4):
                    tn = tmpn[c % 2]
                    K.op(K.DVE, lambda tn=tn, c=c, which=which: nc.vector.tensor_tensor(tn[:, :], pj[:, which * 4 + c, :], rstd[:, :], ALU.mult),
                         reads=[pj, rstd], writes=[tn])
                    K.op(K.ACT, lambda tn=tn, c=c, key=key: nc.scalar.activation(stg[:, c, :], tn[:, :], AF.Copy, scale=pvcol(C, key, c)),
                         reads=[tn, C.pv], writes=[stg])
                K.dma(K.ACT, st_o, dst[:, :].rearrange("(c p) t -> p c t", p=128)[:, :, tsl], stg[:, :, :], reads=[stg], writes=[mid_t[g]])
            rope_tables(K, nc, C, pos_i, cs, scr)
            K.dma(K.ACT, st_o2, cs_d[0:64, tsl], cs[0][:, :], reads=[cs[0]], writes=[mid_t[g]])
            K.dma(K.ACT, st_o2, cs_d[64:128, tsl], cs[1][:, :], reads=[cs[1]], writes=[mid_t[g]])
            pr = P[4]
            K.mm(pr, pr[0:64, :], C.pm, C.pm[:, :], pj, pj[0:64, 8, :], True, True)
            K.op(K.DVE, lambda: nc.vector.tensor_tensor(ra[:, :], pj[0:64, 8, :], cs[0][:, :], ALU.mult), reads=[pj, cs[0]], writes=[ra])
            K.op(K.DVE, lambda: nc.vector.tensor_tensor(rb[:, :], pr[0:64, :], cs[1][:, :], ALU.mult), reads=[pr, cs[1]], writes=[rb])
            K.op(K.DVE, lambda: nc.vector.tensor_tensor(krb[:, :], ra[:, :], rb[:, :], ALU.add), reads=[ra, rb], writes=[krb])
            K.dma(K.ACT, st_o2, kr_d[:, tsl], krb[:, :], reads=[krb], writes=[mid_t[g]])
            K.retoken(st_o2, [cs[0], cs[1], krb, mid_t[g]])


def mla_stage_b(K, C, wuq_b, wukv_b, cq_d, ckv_d, kr_d, cs_d, mid_t, o_d, o_t, prep_t):
    nc = K.nc
    T = C.T
    NSEQ = T // S
    P = C.psum
    with K.stage():
        wuq = K.sb("mb_wuq", [128, 4, 3072], BF16)
        wukv = K.sb("mb_wukv", [128, 4, 4096], BF16)
        cmf = K.sb("mb_cmf", [128, 2048], F32)
        cm = K.sb("mb_cm", [128, 4, TG], BF16)
        cqT = K.sb("mb_cq", [128, 4, S], BF16)
        ckvT = K.sb("mb_ckv", [128, 4, S], BF16)
        krT = K.sb("mb_kr", [64, S], BF16)
        cos = K.sb("mb_cos", [64, S], F32)
        sin = K.sb("mb_sin", [64, S], F32)
        qn = [K.sb("mb_qn%d" % i, [128, S], BF16) for i in range(2)]
        qr = [K.sb("mb_qr%d" % i, [64, S], BF16) for i in range(2)]
        kn = [K.sb("mb_kn%d" % i, [128, S], BF16) for i in range(2)]
        Vh = [K.sb("mb_v%d" % i, [128, 16, 128], BF16) for i in range(2)]
        qraw = [K.sb("mb_qraw%d" % i, [64, TG], F32) for i in range(2)]
        ra = [K.sb("mb_ra%d" % i, [64, TG], F32) for i in range(2)]
        rb = [K.sb("mb_rb%d" % i, [64, TG], F32) for i in range(2)]
        E = [K.sb("mb_E%d" % i, [128, TG], BF16) for i in range(3)]
        rec = K.sb("mb_rec", [128, TG], F32)
        ost = [K.sb("mb_o%d" % i, [128, TG], BF16) for i in range(2)]
        st_w = K.stream("wres")
        st_a = K.stream("xf")
        st_o = [K.stream("stg"), K.stream("stg2")]
        for q in range(4):
            K.dma(K.SP, st_w, wuq[:, q, :], wuq_b[:, q * 3072:(q + 1) * 3072], reads=[prep_t], writes=[wuq])
            K.dma(K.SP, st_w, wukv[:, q, :], wukv_b[:, q * 4096:(q + 1) * 4096], reads=[prep_t], writes=[wukv])
        K.dma(K.SP, st_w, cmf[:, :], C.c_cmask_d[:, :], writes=[cmf])
        K.retoken(st_w, [wuq, wukv, cmf])
        K.op(K.DVE, lambda: nc.vector.tensor_copy(cm[:, :, :], cmf[:, :].rearrange("p (j t) -> p j t", j=4)), reads=[cmf], writes=[cm])
        oi = 0
        ei = 0
        for sq_ in range(NSEQ):
            ssl = slice(sq_ * S, (sq_ + 1) * S)
            gts = mid_t[sq_ * 4:(sq_ + 1) * 4]
            K.dma(K.SP, st_a, cqT[:, :, :], cq_d[:, :].rearrange("(c p) t -> p c t", p=128)[:, :, ssl], reads=gts, writes=[cqT])
            K.dma(K.SP, st_a, ckvT[:, :, :], ckv_d[:, :].rearrange("(c p) t -> p c t", p=128)[:, :, ssl], reads=gts, writes=[ckvT])
            K.dma(K.SP, st_a, krT[:, :], kr_d[:, ssl], reads=gts, writes=[krT])
            K.dma(K.SP, st_a, cos[:, :], cs_d[0:64, ssl], reads=gts, writes=[cos])
            K.dma(K.SP, st_a, sin[:, :], cs_d[64:128, ssl], reads=gts, writes=[sin])
            K.retoken(st_a, [cqT, ckvT, krT, cos, sin])
            for h in range(MH):
                b = h % 2
                for g in range(4):
                    gsl = slice(g * TG, (g + 1) * TG)
                    pq, pr, pk, prot = P[0], P[1], P[2], P[3]
                    for kc in range(4):
                        K.mm(pq, pq[:, :], wuq, wuq[:, kc, h * 192:h * 192 + 128], cqT, cqT[:, kc, gsl], kc == 0, kc == 3)
                    K.op(K.ACT, lambda pq=pq, b=b, gsl=gsl: nc.scalar.activation(qn[b][:, gsl], pq[:, :], AF.Copy, scale=SCALE),
                         reads=[pq], writes=[qn[b]])
                    for kc in range(4):
                        K.mm(pr, pr[0:64, :], wuq, wuq[:, kc, h * 192 + 128:h * 192 + 192], cqT, cqT[:, kc, gsl], kc == 0, kc == 3)
                    qw = qraw[g % 2]
                    K.op(K.ACT, lambda pr=pr, qw=qw: nc.scalar.activation(qw[:, :], pr[0:64, :], AF.Copy, scale=SCALE),
                         reads=[pr], writes=[qw])
                    for kc in range(4):
                        K.mm(pk, pk[:, :], wukv, wukv[:, kc, h * 256:h * 256 + 128], ckvT, ckvT[:, kc, gsl], kc == 0, kc == 3)
                    K.op(K.ACT, lambda pk=pk, b=b, gsl=gsl: nc.scalar.copy(kn[b][:, gsl], pk[:, :]), reads=[pk], writes=[kn[b]])
                    K.mm(prot, prot[0:64, :], C.pm, C.pm[:, :], qw, qw[:, :], True, True)
                    a_, b_ = ra[g % 2], rb[g % 2]
                    K.op(K.DVE, lambda a_=a_, qw=qw, gsl=gsl: nc.vector.tensor_tensor(a_[:, :], qw[:, :], cos[:, gsl], ALU.mult),
                         reads=[qw, cos], writes=[a_])
                    K.op(K.DVE, lambda b_=b_, prot=prot, gsl=gsl: nc.vector.tensor_tensor(b_[:, :], prot[0:64, :], sin[:, gsl], ALU.mult),
                         reads=[prot, sin], writes=[b_])
                    K.op(K.DVE, lambda a_=a_, b_=b_, b=b, gsl=gsl: nc.vector.tensor_tensor(qr[b][:, gsl], a_[:, :], b_[:, :], ALU.add),
                         reads=[a_, b_], writes=[qr[b]])
                for q4 in range(4):
                    pv = P[4 + q4 % 2]
                    for j in range(4):
                        tt = q4 * 4 + j
                        for kc in range(4):
                            K.mm(pv, pv[:, j * 128:(j + 1) * 128], ckvT, ckvT[:, kc, tt * 128:(tt + 1) * 128],
                                 wukv, wukv[:, kc, h * 256 + 128:h * 256 + 256], kc == 0, kc == 3)
                    K.op(K.DVE, lambda pv=pv, b=b, q4=q4: nc.vector.tensor_copy(
                        Vh[b][:, 4 * q4:4 * q4 + 4, :], pv[:, :].rearrange("p (j d) -> p j d", j=4)), reads=[pv], writes=[Vh[b]])
                for qg in range(4):
                    qsl = slice(qg * TG, (qg + 1) * TG)
                    pO, pD = P[6], P[7]
                    nk = 4 * (qg + 1)
                    pend_pv = []
                    for kt in range(nk):
                        ksl = slice(kt * 128, (kt + 1) * 128)
                        pS = P[4 + ei % 2]
                        e = E[ei % 3]
                        ei += 1
                        K.mm(pS, pS[:, :], kn[b], kn[b][:, ksl], qn[b], qn[b][:, qsl], True, False)
                        K.mm(pS, pS[:, :], krT, krT[:, ksl], qr[b], qr[b][:, qsl], False, True)
                        K.op(K.ACT, lambda e=e, pS=pS: nc.scalar.activation(e[:, :], pS[:, :], AF.Exp), reads=[pS], writes=[e])
                        j = kt - 4 * qg
                        if j >= 0:
                            K.op(K.DVE, lambda e=e, j=j: nc.vector.tensor_tensor(e[:, :], e[:, :], cm[:, j, :], ALU.mult),
                                 reads=[e, cm], writes=[e])

                        def pv_fn(e=e, kt=kt):
                            K.mm(pO, pO[:, :], Vh[b], Vh[b][:, kt, :], e, e[:, :], kt == 0, kt == nk - 1, mark=False)
                            K.mm(pD, pD[:, :], C.ones_b, C.ones_b[:, :], e, e[:, :], kt == 0, kt == nk - 1, mark=True)
                        if pend_pv:
                            pend_pv.pop(0)()
                        pend_pv.append(pv_fn)
                    while pend_pv:
                        pend_pv.pop(0)()
                    o_ = ost[oi % 2]
                    K.op(K.DVE, lambda pD=pD: nc.vector.reciprocal(rec[:, :], pD[:, :]), reads=[pD], writes=[rec])
                    K.op(K.DVE, lambda o_=o_, pO=pO: nc.vector.tensor_tensor(o_[:, :], pO[:, :], rec[:, :], ALU.mult),
                         reads=[pO, rec], writes=[o_])
                    K.dma(K.ACT, st_o[oi % 2], o_d[h * 128:(h + 1) * 128, sq_ * S + qg * TG:sq_ * S + (qg + 1) * TG], o_[:, :],
                          reads=[o_], writes=[o_t[sq_ * 4 + qg]])
                    oi += 1
        for tl in o_t:
            tl.w = {id(s_.sem): (s_.sem, s_.count, None) for s_ in st_o}


def mla_stage_c(K, C, li, wo_b, o_d, o_t, xin, xin_t, xout, xout_t, prep_t):
    nc = K.nc
    T = C.T
    NG = T // TG
    P = C.psum
    with K.stage():
        wo = K.sb("mc_wo", [128, DC, D], BF16)
        ob = K.sb("mc_ob", [128, DC, TG], BF16)
        xacc = K.sb("mc_xacc", [128, DC, TG], F32)
        st_w = K.stream("wres")
        st_x = K.stream("xf")
        st_ob = K.stream("xb")
        wov = wo_b[:, :].rearrange("p (k n) -> p k n", k=DC)
        for q in range(4):
            K.dma(K.SP, st_w, wo[:, 4 * q:4 * q + 4, :], wov[:, 4 * q:4 * q + 4, :], reads=[prep_t], writes=[wo])
        xin_v = xin[:, :].rearrange("(c p) t -> p c t", p=128)
        xout_v = xout[:, :].rearrange("(c p) t -> p c t", p=128)
        o_v = o_d[:, :].rearrange("(c p) t -> p c t", p=128)
        for g in range(NG):
            tsl = slice(g * TG, (g + 1) * TG)
            load_x_group(K, nc, st_x, xacc, None, xin_v, tsl, xin_t[g])
            for q in range(4):
                K.dma(K.SP, st_ob, ob[:, 4 * q:4 * q + 4, :], o_v[:, 4 * q:4 * q + 4, tsl], reads=[o_t[g]], writes=[ob])
            for n in range(DC):
                po = P[n % 2]
                for kc in range(DC):
                    K.mm(po, po[:, :], wo, wo[:, kc, n * 128:(n + 1) * 128], ob, ob[:, kc, :], kc == 0, kc == DC - 1)
                K.op(K.DVE, lambda po=po, n=n: nc.vector.scalar_tensor_tensor(
                    xacc[:, n, :], xacc[:, n, :], ALPHA, po[:, :], ALU.mult, ALU.add), reads=[xacc, po], writes=[xacc])
            layer_norm_inplace(K, C, xacc, li, 0, P[6], P[7])
            for q in range(4):
                K.dma(K.ACT, st_x, xout_v[:, 4 * q:4 * q + 4, tsl], xacc[:, 4 * q:4 * q + 4, :], reads=[xacc], writes=[xout_t[g]])


SH = 64
SP_ = 64
SN = 128
NCH_IN = 81


def lay_ssm_in(w):
    wp = np.zeros((2048, NCH_IN * 128), np.float32)
    wp[:, :10304] = w
    a = wp.reshape(16, 128, NCH_IN, 128)
    return np.ascontiguousarray(a.transpose(2, 1, 0, 3).reshape(NCH_IN * 128, 2048))


def lay_ssm_out(w):
    a = w.reshape(32, 128, 16, 128)
    return np.ascontiguousarray(a.transpose(2, 1, 0, 3).reshape(16 * 128, 4096))


def ssm_host_vec(inp):
    return np.ascontiguousarray(np.stack([inp["ssm_dt_bias"][0], inp["ssm_a_log"][0]]).astype(np.float32).reshape(1, 128))


def ssm_host_consts():
    tri = (np.arange(128)[:, None] <= np.arange(128)[None, :]).astype(np.float32)
    return {"c_tri": tri}


def ssd_stage_a(K, C, win_b, vec_d, xin, xin_t, z_d, xbc_d, dt_d, mid_t, prep_t):
    nc = K.nc
    T = C.T
    NG = T // TG
    P = C.psum
    with K.stage():
        xb = K.sb("sa_xb", [128, DC, TG], BF16)
        xacc = K.sb("sa_xacc", [128, DC, TG], F32)
        wr = [K.sb("sa_w%d" % i, [128, 2048], BF16) for i in range(3)]
        ub = [K.sb("sa_ub%d" % i, [128, TG + 3], F32) for i in range(2)]
        vv = [K.sb("sa_v%d" % i, [128, TG], F32) for i in range(2)]
        halo = K.sb("sa_halo", [128, 48, 3], F32)
        stg = [K.sb("sa_stg%d" % i, [128, TG], BF16) for i in range(4)]
        vec = K.sb("sa_vec", [128, 128], F32)
        dtt = [K.sb("sa_dt%d" % i, [128, 64], F32) for i in range(2)]
        st_x = K.stream("xf")
        st_w = [K.stream("wring%d" % i) for i in range(3)]
        st_s = [K.stream("s_stg%d" % i) for i in range(4)]
        st_v = K.stream("wres")
        st_dt = [K.stream("stg"), K.stream("stg2")]
        K.dma(K.SP, st_v, vec[:, :], vec_d[0:1, :].to_broadcast([128, 128]), writes=[vec])
        xin_v = xin[:, :].rearrange("(c p) t -> p c t", p=128)
        wl = 0
        si = 0
        for g in range(NG):
            tsl = slice(g * TG, (g + 1) * TG)
            load_x_group(K, nc, st_x, xacc, xb, xin_v, tsl, xin_t[g])
            if (g * TG) % S == 0:
                K.op(K.DVE, lambda: nc.vector.memset(halo[:, :, :], 0.0), writes=[halo])
            for n in range(NCH_IN):
                w = wr[wl % 3]
                K.dma(K.SP, st_w[wl % 3], w[:, :], win_b[n * 128:(n + 1) * 128, :], reads=[prep_t], writes=[w])
                wl += 1
                if n < 80:
                    pp = P[n % 4]
                    for kc in range(DC):
                        K.mm(pp, pp[:, :], w, w[:, kc * 128:(kc + 1) * 128], xb, xb[:, kc, :], kc == 0, kc == DC - 1)
                    so = stg[si % 4]
                    sst = st_s[si % 4]
                    si += 1
                    if n < 32:
                        K.op(K.ACT, lambda so=so, pp=pp: nc.scalar.activation(so[:, :], pp[:, :], AF.Silu), reads=[pp], writes=[so])
                        K.dma(K.ACT, sst, z_d[n * 128:(n + 1) * 128, tsl], so[:, :], reads=[so], writes=[mid_t[g]])
                    else:
                        cc = n - 32
                        u = ub[cc % 2]
                        v = vv[cc % 2]
                        K.op(K.DVE, lambda u=u, cc=cc: nc.vector.tensor_copy(u[:, 0:3], halo[:, cc, :]), reads=[halo], writes=[u])
                        K.op(K.ACT, lambda u=u, pp=pp: nc.scalar.copy(u[:, 3:TG + 3], pp[:, :]), reads=[pp], writes=[u])
                        K.op(K.DVE, lambda u=u, cc=cc: nc.vector.tensor_copy(halo[:, cc, :], u[:, TG:TG + 3]), reads=[u], writes=[halo])
                        K.op(K.DVE, lambda u=u, v=v, cc=cc: nc.vector.tensor_scalar(
                            v[:, :], u[:, 0:TG], pvcol(C, ("ssm_cw", 0), cc), None, ALU.mult), reads=[u, C.pv], writes=[v])
                        for k in (1, 2, 3):
                            K.op(K.DVE, lambda u=u, v=v, cc=cc, k=k: nc.vector.scalar_tensor_tensor(
                                v[:, :], u[:, k:TG + k], pvcol(C, ("ssm_cw", k), cc), v[:, :], ALU.mult, ALU.add),
                                reads=[u, v, C.pv], writes=[v])
                        K.op(K.ACT, lambda so=so, v=v, cc=cc: nc.scalar.activation(
                            so[:, :], v[:, :], AF.Silu, bias=pvcol(C, ("ssm_cb",), cc)), reads=[v, C.pv], writes=[so])
                        K.dma(K.ACT, sst, xbc_d[cc * 128:(cc + 1) * 128, tsl], so[:, :], reads=[so], writes=[mid_t[g]])
                else:
                    for tt in range(4):
                        pd = P[4 + tt % 2]
                        for kc in range(DC):
                            K.mm(pd, pd[:, 0:64], xb, xb[:, kc, tt * 128:(tt + 1) * 128], w, w[:, kc * 128:kc * 128 + 64],
                                 kc == 0, kc == DC - 1)
                        d_ = dtt[tt % 2]
                        K.op(K.DVE, lambda d_=d_, pd=pd: nc.vector.tensor_tensor(d_[:, :], pd[:, 0:64], vec[:, 0:64], ALU.add),
                             reads=[pd, vec], writes=[d_])
                        K.op(K.ACT, lambda d_=d_: nc.scalar.activation(d_[:, :], d_[:, :], AF.Exp), reads=[d_], writes=[d_])
                        K.op(K.ACT, lambda d_=d_: nc.scalar.activation(d_[:, :], d_[:, :], AF.Ln, bias=1.0), reads=[d_], writes=[d_])
                        K.dma(K.ACT, st_dt[tt % 2], dt_d[g * TG + tt * 128:g * TG + (tt + 1) * 128, :], d_[:, :],
                              reads=[d_], writes=[mid_t[g]])
        for tl in mid_t:
            for s_ in st_s + st_dt:
                tl.w[id(s_.sem)] = (s_.sem, s_.count, None)


def setup_ssm_consts(K, C):
    nc = K.nc
    C.c_tri_d = K.dram("c_tri", [128, 128], F32, kind="ExternalInput")
    C.tri = K.sb("tri_f", [128, 128], F32)
    stp = K.stream("misc_s")
    K.dma(K.SP, stp, C.tri[:, :], C.c_tri_d[:, :], writes=[C.tri])


def ssd_stage_b(K, C, vec_d, z_d, xbc_d, dt_d, mid_t, yn_d, yn_t):
    nc = K.nc
    T = C.T
    NSEQ = T // S
    P = C.psum
    with K.stage():
        vec = K.sb("sb_vec", [128, 128], F32)
        aneg = K.sb("sb_aneg", [128, 64], F32)
        state_f = K.sb("sb_stf", [128, SH, SP_], F32)
        state_b = K.sb("sb_stb", [128, SH, SP_], BF16)
        xsT2 = [K.sb("sb_xsT%d" % i, [128, 32, 128], BF16) for i in range(2)]
        zsT2 = [K.sb("sb_zsT%d" % i, [128, 32, 128], BF16) for i in range(2)]
        BT2 = [K.sb("sb_BT%d" % i, [128, 8, 128], BF16) for i in range(2)]
        CT2 = [K.sb("sb_CT%d" % i, [128, 8, 128], BF16) for i in range(2)]
        dtT2 = [K.sb("sb_dtT%d" % i, [128, 64], F32) for i in range(2)]
        xtok = K.sb("sb_xtok", [128, SH, SP_], BF16)
        Btok = K.sb("sb_Btok", [128, 8, 128], BF16)
        xdt = K.sb("sb_xdt", [128, SH, SP_], BF16)
        xdd = K.sb("sb_xdd", [128, SH, SP_], BF16)
        daT = K.sb("sb_daT", [128, 64], F32)
        acT = K.sb("sb_acT", [128, 64], F32)
        acF = K.sb("sb_acF", [64, 128], F32)
        lastb = K.sb("sb_lastb", [128, 64], F32)
        decT = K.sb("sb_decT", [128, 64], F32)
        ddT = K.sb("sb_ddT", [128, 64], F32)
        fac = K.sb("sb_fac", [128, 64], F32)
        cbm = K.sb("sb_cbm", [128, 8, 128], BF16)
        bcl = [K.sb("sb_bcl%d" % i, [128, 4, 128], F32) for i in range(2)]
        MT4 = [K.sb("sb_MT%d" % i, [128, 4, 128], BF16) for i in range(2)]
        ebc = [K.sb("sb_ebc%d" % i, [128, 4, 128], F32) for i in range(2)]
        Cs4 = [K.sb("sb_Cs%d" % i, [128, 4, 128], BF16) for i in range(2)]
        stmp = K.sb("sb_stmp", [128, 8, SP_], F32)
        t1 = [K.sb("sb_t1_%d" % i, [128, 128], F32) for i in range(2)]
        t2s = [K.sb("sb_t2_%d" % i, [128, 4, 128], F32) for i in range(2)]
        sqs = [K.sb("sb_sq%d" % i, [128, 128], F32) for i in range(2)]
        rstd = K.sb("sb_rstd", [128, 128], F32)
        t3 = [K.sb("sb_t3_%d" % i, [128, 128], F32) for i in range(2)]
        ynb = K.sb("sb_ynb", [128, 32, 128], BF16)
        st_v = K.stream("wres")
        st_l2 = [K.stream("xf"), K.stream("xb")]
        st_o = K.stream("stg")
        K.dma(K.SP, st_v, vec[:, :], vec_d[0:1, :].to_broadcast([128, 128]), writes=[vec])
        K.op(K.ACT, lambda: nc.scalar.activation(aneg[:, :], vec[:, 64:128], AF.Exp), reads=[vec], writes=[aneg])
        K.op(K.DVE, lambda: nc.vector.tensor_scalar(aneg[:, :], aneg[:, :], -1.0, None, ALU.mult), reads=[aneg], writes=[aneg])

        def issue_loads(sq_, ck):
            t0 = sq_ * S + ck * 128
            csl = slice(t0, t0 + 128)
            gt = [mid_t[t0 // TG]]
            st_l = st_l2[ck % 2]
            xsT, zsT, BT, CT, dtT = xsT2[ck % 2], zsT2[ck % 2], BT2[ck % 2], CT2[ck % 2], dtT2[ck % 2]
            xv = xbc_d[:, :].rearrange("(c p) t -> p c t", p=128)
            for q in range(4):
                K.dma(K.SP, st_l, xsT[:, 8 * q:8 * q + 8, :], xv[:, 8 * q:8 * q + 8, csl], reads=gt, writes=[xsT])
            K.dma(K.SP, st_l, BT[:, :, :], xv[:, 32:40, csl], reads=gt, writes=[BT])
            K.dma(K.SP, st_l, CT[:, :, :], xv[:, 40:48, csl], reads=gt, writes=[CT])
            zv = z_d[:, :].rearrange("(c p) t -> p c t", p=128)
            for q in range(4):
                K.dma(K.SP, st_l, zsT[:, 8 * q:8 * q + 8, :], zv[:, 8 * q:8 * q + 8, csl], reads=gt, writes=[zsT])
            K.dma(K.SP, st_l, dtT[:, :], dt_d[t0:t0 + 128, :], reads=gt, writes=[dtT])
            K.retoken(st_l, [xsT, BT, CT, zsT, dtT])

        tri = C.tri
        idf = C.ident_f
        NCK = S // 128
        for sq_ in range(NSEQ):
            K.op(K.DVE, lambda: nc.vector.memset(state_f[:, :, :], 0.0), writes=[state_f])
            K.op(K.DVE, lambda: nc.vector.memset(state_b[:, :, :], 0.0), writes=[state_b])
            for ck in range(NCK):
                t0 = sq_ * S + ck * 128
                csl = slice(t0, t0 + 128)
                xsT, zsT, BT, CT, dtT = xsT2[ck % 2], zsT2[ck % 2], BT2[ck % 2], CT2[ck % 2], dtT2[ck % 2]
                if ck == 0:
                    issue_loads(sq_, 0)
                if ck + 1 < NCK:
                    issue_loads(sq_, ck + 1)
                for q in range(4):
                    pt = P[q % 2]
                    ptb = pt[:, :].bitcast(BF16)
                    for j in range(8):
                        K.op(K.PE, lambda ptb=ptb, q=q, j=j: nc.tensor.transpose(ptb[:, j * 128:(j + 1) * 128], xsT[:, q * 8 + j, :], C.ident_b[:, :]),
                             reads=[xsT, C.ident_b], writes=[pt], mark=(j == 7))
                    K.op(K.ACT, lambda ptb=ptb, q=q: nc.scalar.copy(
                        xtok[:, q * 16:(q + 1) * 16, :], ptb.rearrange("p (e d) -> p e d", d=SP_)), reads=[pt], writes=[xtok])
                pt = P[2]
                ptb = pt[:, :].bitcast(BF16)
                for j in range(8):
                    K.op(K.PE, lambda ptb=ptb, j=j: nc.tensor.transpose(ptb[:, j * 128:(j + 1) * 128], BT[:, j, :], C.ident_b[:, :]),
                         reads=[BT, C.ident_b], writes=[pt], mark=(j == 7))
                K.op(K.ACT, lambda ptb=ptb: nc.scalar.copy(Btok[:, :, :], ptb.rearrange("p (g n) -> p g n", n=128)), reads=[pt], writes=[Btok])
                K.op(K.DVE, lambda: nc.vector.tensor_tensor(daT[:, :], dtT[:, :], aneg[:, :], ALU.mult), reads=[dtT, aneg], writes=[daT])
                pa = P[3]
                K.mm(pa, pa[:, 0:64], tri, tri[:, :], daT, daT[:, :], True, True)
                K.mm(pa, pa[0:64, 128:256], daT, daT[:, :], tri, tri[:, :], True, True)
                K.op(K.ACT, lambda pa=pa: nc.scalar.copy(acT[:, :], pa[:, 0:64]), reads=[pa], writes=[acT])
                K.op(K.ACT, lambda pa=pa: nc.scalar.copy(acF[:, :], pa[0:64, 128:256]), reads=[pa], writes=[acF])
                pl = P[3]
                K.mm(pl, pl[:, 256:320], idf, idf[:, 127:128].to_broadcast([128, 128]), acT, acT[:, :], True, True)
                K.op(K.DVE, lambda pl=pl: nc.vector.tensor_copy(lastb[:, :], pl[:, 256:320]), reads=[pl], writes=[lastb])
                K.op(K.DVE, lambda: nc.vector.tensor_tensor(decT[:, :], lastb[:, :], acT[:, :], ALU.subtract), reads=[lastb, acT], writes=[decT])
                K.op(K.ACT, lambda: nc.scalar.activation(decT[:, :], decT[:, :], AF.Exp), reads=[decT], writes=[decT])
                K.op(K.ACT, lambda: nc.scalar.activation(fac[:, :], lastb[:, :], AF.Exp), reads=[lastb], writes=[fac])
                K.op(K.DVE, lambda: nc.vector.tensor_tensor(ddT[:, :], dtT[:, :], decT[:, :], ALU.mult), reads=[dtT, decT], writes=[ddT])
                K.op(K.DVE, lambda: nc.vector.tensor_tensor(xdt[:, :, :], xtok[:, :, :], dtT[:, :].unsqueeze(2).to_broadcast([128, SH, SP_]), ALU.mult),
                     reads=[xtok, dtT], writes=[xdt])
                K.op(K.DVE, lambda: nc.vector.tensor_tensor(xdd[:, :, :], xtok[:, :, :], ddT[:, :].unsqueeze(2).to_broadcast([128, SH, SP_]), ALU.mult),
                     reads=[xtok, ddT], writes=[xdd])
                for hf in range(2):
                    pc = P[4 + hf]
                    for j in range(4):
                        gq = hf * 4 + j
                        K.mm(pc, pc[:, j * 128:(j + 1) * 128], BT, BT[:, gq, :], CT, CT[:, gq, :], True, True)
                    K.op(K.DVE, lambda pc=pc, hf=hf: nc.vector.tensor_tensor(
                        cbm[:, hf * 4:hf * 4 + 4, :], pc[:, :].rearrange("p (j t) -> p j t", j=4),
                        tri[:, :].unsqueeze(1).to_broadcast([128, 4, 128]), ALU.mult), reads=[pc, tri], writes=[cbm])
                def part_a(e4):
                    gq = e4 // 2
                    pb = P[6 + e4 % 2]
                    bl = bcl[e4 % 2]
                    mt = MT4[e4 % 2]
                    eb = ebc[e4 % 2]
                    cs = Cs4[e4 % 2]
                    for j in range(4):
                        e = e4 * 4 + j
                        K.mm(pb, pb[:, j * 128:(j + 1) * 128], idf, idf[0:64, e:e + 1].to_broadcast([64, 128]), acF, acF[:, :], True, True)
                    for j in range(4):
                        e = e4 * 4 + j
                        K.op(K.DVE, lambda bl=bl, pb=pb, j=j, e=e: nc.vector.tensor_scalar(
                            bl[:, j, :], pb[:, j * 128:(j + 1) * 128], acT[:, e:e + 1], 0.0, ALU.subtract, ALU.min),
                            reads=[pb, acT], writes=[bl])
                    K.op(K.ACT, lambda bl=bl: nc.scalar.activation(bl[:, :, :], bl[:, :, :], AF.Exp), reads=[bl], writes=[bl])
                    K.op(K.DVE, lambda mt=mt, bl=bl, gq=gq: nc.vector.tensor_tensor(
                        mt[:, :, :], bl[:, :, :], cbm[:, gq:gq + 1, :].to_broadcast([128, 4, 128]), ALU.mult), reads=[bl, cbm], writes=[mt])
                    K.op(K.ACT, lambda eb=eb, pb=pb: nc.scalar.activation(eb[:, :, :], pb[:, :].rearrange("p (j t) -> p j t", j=4), AF.Exp),
                         reads=[pb], writes=[eb])
                    K.op(K.DVE, lambda cs=cs, eb=eb, gq=gq: nc.vector.tensor_tensor(
                        cs[:, :, :], eb[:, :, :], CT[:, gq:gq + 1, :].to_broadcast([128, 4, 128]), ALU.mult), reads=[eb, CT], writes=[cs])
                    return mt, cs

                def part_b(e4, mt, cs):
                    for jp in range(2):
                        pr_i = e4 * 2 + jp
                        py = P[jp]
                        for jj in range(2):
                            j = jp * 2 + jj
                            e = e4 * 4 + j
                            osl = slice(jj * 64, jj * 64 + 64)
                            K.op(K.PE, lambda py=py, osl=osl, e=e, mt=mt, j=j, jj=jj: nc.tensor.matmul(
                                py[osl, 0:128], xdt[:, e, :], mt[:, j, :], start=True, stop=False, tile_position=(0, jj * 64)),
                                reads=[xdt, mt], writes=[py], mark=False)
                            K.op(K.PE, lambda py=py, osl=osl, e=e, cs=cs, j=j, jj=jj: nc.tensor.matmul(
                                py[osl, 0:128], state_b[:, e, :], cs[:, j, :], start=False, stop=True, tile_position=(0, jj * 64)),
                                reads=[state_b, cs], writes=[py], mark=True)
                        ta = t1[pr_i % 2]
                        t2 = t2s[(pr_i // 4) % 2]
                        K.op(K.DVE, lambda ta=ta, py=py, pr_i=pr_i: nc.vector.scalar_tensor_tensor(
                            ta[:, :], xsT[:, pr_i, :], pvcol(C, ("ssm_dsk",), pr_i), py[:, 0:128], ALU.mult, ALU.add),
                            reads=[xsT, py, C.pv], writes=[ta])
                        K.op(K.DVE, lambda ta=ta, pr_i=pr_i, t2=t2: nc.vector.tensor_tensor(t2[:, pr_i % 4, :], ta[:, :], zsT[:, pr_i, :], ALU.mult),
                             reads=[ta, zsT], writes=[t2])
                        sq = sqs[pr_i % 2]
                        K.op(K.ACT, lambda sq=sq, pr_i=pr_i, t2=t2: nc.scalar.activation(sq[:, :], t2[:, pr_i % 4, :], AF.Square), reads=[t2], writes=[sq])
                        def pn_fn(sq=sq, pr_i=pr_i, t2=t2):
                            pn = P[2]
                            K.mm(pn, pn[:, 0:128], C.ones_f, C.ones_f[:, :], sq, sq[:, :], pr_i % 4 == 0, pr_i % 4 == 3, mark=True)
                            if pr_i % 4 == 3:
                                K.op(K.DVE, lambda pn=pn: nc.vector.tensor_scalar(rstd[:, :], pn[:, 0:128], 1.0 / 512.0, RMS_EPS, ALU.mult, ALU.add),
                                     reads=[pn], writes=[rstd])
                                K.op(K.ACT, lambda: nc.scalar.activation(rstd[:, :], rstd[:, :], AF.Sqrt), reads=[rstd], writes=[rstd])
                                K.op(K.DVE, lambda: nc.vector.reciprocal(rstd[:, :], rstd[:, :]), reads=[rstd], writes=[rstd])
                                for k4 in range(4):
                                    pi = pr_i - 3 + k4
                                    tb = t3[k4 % 2]
                                    K.op(K.DVE, lambda tb=tb, k4=k4, t2=t2: nc.vector.tensor_tensor(tb[:, :], t2[:, k4, :], rstd[:, :], ALU.mult),
                                         reads=[t2, rstd], writes=[tb])
                                    K.op(K.ACT, lambda tb=tb, pi=pi: nc.scalar.activation(ynb[:, pi, :], tb[:, :], AF.Copy, scale=pvcol(C, ("ssm_nw",), pi)),
                                         reads=[tb, C.pv], writes=[ynb])
                        if pend_pn:
                            pend_pn.pop(0)()
                        pend_pn.append(pn_fn)

                pend_pn = []
                nxt_mc = part_a(0)
                for e4 in range(SH // 4):
                    cur_mc = nxt_mc
                    if e4 + 1 < SH // 4:
                        nxt_mc = part_a(e4 + 1)
                    part_b(e4, *cur_mc)
                while pend_pn:
                    pend_pn.pop(0)()
                for gq in range(8):
                    ph = P[4 + gq % 2]
                    K.mm(ph, ph[:, :], Btok, Btok[:, gq, :], xdd, xdd[:, gq * 8:(gq + 1) * 8, :].rearrange("p e d -> p (e d)"), True, True)
                    K.op(K.DVE, lambda gq=gq: nc.vector.tensor_tensor(
                        stmp[:, :, :], state_f[:, gq * 8:(gq + 1) * 8, :], fac[:, gq * 8:(gq + 1) * 8].unsqueeze(2).to_broadcast([128, 8, SP_]), ALU.mult),
                        reads=[state_f, fac], writes=[stmp])
                    K.op(K.DVE, lambda gq=gq, ph=ph: nc.vector.tensor_tensor(
                        state_f[:, gq * 8:(gq + 1) * 8, :], stmp[:, :, :], ph[:, :].rearrange("p (e d) -> p e d", d=SP_), ALU.add),
                        reads=[stmp, ph], writes=[state_f])
                K.op(K.ACT, lambda: nc.scalar.copy(state_b[:, :, :], state_f[:, :, :]), reads=[state_f], writes=[state_b])
                yv = yn_d[:, :].rearrange("(c p) t -> p c t", p=128)
                for q in range(4):
                    K.dma(K.ACT, st_o, yv[:, 8 * q:8 * q + 8, csl], ynb[:, 8 * q:8 * q + 8, :], reads=[ynb], writes=[yn_t[t0 // TG]])


def ssd_stage_c(K, C, li, wo_b, yn_d, yn_t, xin, xin_t, xout, xout_t, prep_t):
    nc = K.nc
    T = C.T
    NG = T // TG
    P = C.psum
    with K.stage():
        wr = [K.sb("sc_w%d" % i, [128, 4096], BF16) for i in range(3)]
        yb = K.sb("sc_yb", [128, 32, TG], BF16)
        xacc = K.sb("sc_xacc", [128, DC, TG], F32)
        st_w = [K.stream("wring%d" % i) for i in range(3)]
        st_x = K.stream("xf")
        st_y = K.stream("xb")
        xin_v = xin[:, :].rearrange("(c p) t -> p c t", p=128)
        xout_v = xout[:, :].rearrange("(c p) t -> p c t", p=128)
        y_v = yn_d[:, :].rearrange("(c p) t -> p c t", p=128)
        wl = 0
        for g in range(NG):
            tsl = slice(g * TG, (g + 1) * TG)
            load_x_group(K, nc, st_x, xacc, None, xin_v, tsl, xin_t[g])
            for q in range(8):
                K.dma(K.SP, st_y, yb[:, 4 * q:4 * q + 4, :], y_v[:, 4 * q:4 * q + 4, tsl], reads=[yn_t[g]], writes=[yb])
            for n in range(DC):
                w = wr[wl % 3]
                K.dma(K.SP, st_w[wl % 3], w[:, :], wo_b[n * 128:(n + 1) * 128, :], reads=[prep_t], writes=[w])
                wl += 1
                po = P[n % 2]
                for kc in range(32):
                    K.mm(po, po[:, :], w, w[:, kc * 128:(kc + 1) * 128], yb, yb[:, kc, :], kc == 0, kc == 31)
                K.op(K.DVE, lambda po=po, n=n: nc.vector.scalar_tensor_tensor(
                    xacc[:, n, :], xacc[:, n, :], ALPHA, po[:, :], ALU.mult, ALU.add), reads=[xacc, po], writes=[xacc])
            layer_norm_inplace(K, C, xacc, li, 0, P[6], P[7])
            for q in range(4):
                K.dma(K.ACT, st_x, xout_v[:, 4 * q:4 * q + 4, tsl], xacc[:, 4 * q:4 * q + 4, :], reads=[xacc], writes=[xout_t[g]])


NCORES = 8
NSEQ_CORE = 2
T_CORE = NSEQ_CORE * S

def weight_specs():
    sp = []
    for j in range(2):
        sp.append(("sc_in%d" % j, [2048, 6144]))
        sp.append(("sc_out%d" % j, [128, 16 * 2048]))
    sp += [("ml_in", [128, 16 * 1088]), ("ml_uq", [128, 4 * 3072]), ("ml_ukv", [128, 4 * 4096]), ("ml_o", [128, 16 * 2048])]
    sp += [("ss_in", [NCH_IN * 128, 2048]), ("ss_out", [2048, 4096])]
    for l in range(DEPTH):
        sp += [("pe_wq%d" % l, [2048, 2048]), ("pe_sk%d" % l, [128, 2048]), ("pe_ut%d" % l, [16384, 2048]), ("pe_v%d" % l, [16384, 2048])]
    return sp


CAST_GROUP = {"sc_in0": 0, "sc_out0": 0, "pe_wq0": 4, "pe_sk0": 4, "pe_ut0": 4, "pe_v0": 4,
              "ml_in": 1, "ml_uq": 1, "ml_ukv": 1, "ml_o": 1, "pe_wq1": 1, "pe_sk1": 1, "pe_ut1": 1, "pe_v1": 1,
              "ss_in": 2, "ss_out": 2, "pe_wq2": 2, "pe_sk2": 2, "pe_ut2": 2, "pe_v2": 2,
              "sc_in1": 3, "sc_out1": 3, "pe_wq3": 3, "pe_sk3": 3, "pe_ut3": 3, "pe_v3": 3}


def host_weights(inp):
    w = {}
    for j in range(2):
        w["sc_in%d" % j] = lay_conv_in(inp["sc_w_in"][j]).reshape(2048, 6144)
        w["sc_out%d" % j] = lay_kmajor(inp["sc_w_out"][j])
    w["ml_in"] = lay_kmajor(inp["mla_w_in"][0])
    w["ml_uq"] = lay_kmajor(inp["mla_w_uq"][0])
    w["ml_ukv"] = lay_kmajor(inp["mla_w_ukv"][0])
    w["ml_o"] = lay_kmajor(inp["mla_w_o"][0])
    w["ss_in"] = lay_ssm_in(inp["ssm_w_in"][0])
    w["ss_out"] = lay_ssm_out(inp["ssm_w_out"][0])
    for l in range(DEPTH):
        w["pe_wq%d" % l] = lay_wq(inp["peer_w_q"][l])
        w["pe_sk%d" % l] = lay_sk(inp["peer_sub_keys"][l])
        w["pe_ut%d" % l] = lay_ut(inp["peer_u"][l])
        w["pe_v%d" % l] = np.ascontiguousarray(inp["peer_v"][l], dtype=np.float32)
    return w


def build_program(T=T_CORE):
    K = KB()
    C = setup_common(K, T)
    setup_peer_consts(K, C)
    setup_mla_consts(K, C)
    setup_ssm_consts(K, C)
    NG = T // TG
    xT = K.dram("xT", [D, T], F32, kind="ExternalInput")
    pos_d = K.dram("pos", [1, T], I32, kind="ExternalInput")
    vec_d = K.dram("ssm_vec", [1, 128], F32, kind="ExternalInput")
    outT = K.dram("outT", [D, T], F32, kind="ExternalOutput")
    XA = K.dram("act_a", [D, T], F32)
    XB = K.dram("act_b", [D, T], F32)
    wf, wb = {}, {}
    for name, shape in weight_specs():
        wf[name] = K.dram(name, shape, F32, kind="ExternalInput")
        wb[name] = K.dram(name + "_bf", shape, BF16)
    prep = [K.vt("prep%d" % i) for i in range(5)]

    def casts(gi):
        st = K.stream("cast%d" % gi)
        for name, shape in weight_specs():
            if CAST_GROUP[name] == gi:
                cast_dram(K, st, wb[name][:, :], wf[name][:, :], shape[0], shape[1], [prep[gi]])

    def vts():
        return [K.vt() for _ in range(NG)]

    x_t = vts()
    bufs = [XA, XB]
    cur, cur_t = xT, x_t

    def nxt(i):
        return bufs[i % 2]

    casts(0)
    casts(4)
    step = 0
    for li in range(DEPTH):
        kind = li % 3
        dst, dst_t = nxt(step), vts()
        step += 1
        if kind == 0:
            j = li // 3
            conv_stage(K, C, li, j, wb["sc_in%d" % j][:, :].rearrange("(j p) n -> j p n", p=128), wb["sc_out%d" % j],
                       cur, cur_t, dst, dst_t, prep[li])
        elif kind == 1:
            cq_d = K.dram("cq_d", [512, T], BF16)
            ckv_d = K.dram("ckv_d", [512, T], BF16)
            kr_d = K.dram("kr_d", [64, T], BF16)
            cs_d = K.dram("cs_d", [128, T], F32)
            o_d = K.dram("o_d", [2048, T], BF16)
            mid_t, o_t = vts(), vts()
            mla_stage_a(K, C, wb["ml_in"], pos_d, cur, cur_t, cq_d, ckv_d, kr_d, cs_d, mid_t, prep[li])
            mla_stage_b(K, C, wb["ml_uq"], wb["ml_ukv"], cq_d, ckv_d, kr_d, cs_d, mid_t, o_d, o_t, prep[li])
            mla_stage_c(K, C, li, wb["ml_o"], o_d, o_t, cur, cur_t, dst, dst_t, prep[li])
        else:
            z_d = K.dram("z_d", [4096, T], BF16)
            xbc_d = K.dram("xbc_d", [6144, T], BF16)
            dt_d = K.dram("dt_d", [T, 64], F32)
            yn_d = K.dram("yn_d", [4096, T], BF16)
            mid_t, yn_t = vts(), vts()
            ssd_stage_a(K, C, wb["ss_in"], vec_d, cur, cur_t, z_d, xbc_d, dt_d, mid_t, prep[li])
            ssd_stage_b(K, C, vec_d, z_d, xbc_d, dt_d, mid_t, yn_d, yn_t)
            ssd_stage_c(K, C, li, wb["ss_out"], yn_d, yn_t, cur, cur_t, dst, dst_t, prep[li])
        cur, cur_t = dst, dst_t
        if li + 1 < DEPTH:
            casts(li + 1)
        last = li == DEPTH - 1
        dst, dst_t = (outT, vts()) if last else (nxt(step), vts())
        step += 1
        peer_stage(K, C, li, wb["pe_wq%d" % li], wb["pe_sk%d" % li], wb["pe_ut%d" % li], wb["pe_v%d" % li],
                   cur, cur_t, dst, dst_t, prep[4] if li == 0 else prep[li])
        cur, cur_t = dst, dst_t
    K.wait_all(K.SP, cur_t)
    return K


_PROG = None


def core_inputs(inp, c, hw, consts):
    x = np.asarray(inp["x"], np.float32)[NSEQ_CORE * c:NSEQ_CORE * (c + 1)].reshape(T_CORE, D)
    m = {"xT": np.ascontiguousarray(x.T),
         "pos": np.ascontiguousarray(np.asarray(inp["positions"], np.int32)[NSEQ_CORE * c:NSEQ_CORE * (c + 1)].reshape(1, T_CORE))}
    m.update(consts)
    m.update(hw)
    return m


def kernel(**inp):
    global _PROG
    inp = {k: np.asarray(v) for k, v in inp.items()}
    if _PROG is None:
        _PROG = build_program()
    K = _PROG
    hw = host_weights(inp)
    consts = {"pv": build_pv(inp), "ssm_vec": ssm_host_vec(inp)}
    consts.update(host_consts())
    consts.update(mla_host_consts())
    consts.update(ssm_host_consts())
    in_maps = [core_inputs(inp, c, hw, consts) for c in range(NCORES)]
    res = run_bass_kernel_spmd(K.nc, in_maps, core_ids=list(range(NCORES)))
    out = np.empty((NCORES * NSEQ_CORE, S, D), np.float32)
    for c in range(NCORES):
        o = np.asarray(res.results[c]["outT"], np.float32)
        out[NSEQ_CORE * c:NSEQ_CORE * (c + 1)] = o.T.reshape(NSEQ_CORE, S, D)
    return out
```
